# Optimizing a Trainium2 kernel written in Bass

```python
import jax, jax.numpy as jnp
from jax import lax
import numpy as np

D_MODEL = 2048
BATCH = 4
SEQ = 4096
DEPTH = 4

D_A = D_MODEL // 2
HEAD_SIZE = 64
N_HEADS_A = D_A // HEAD_SIZE
LORA_W = 64
LORA_A = 64
D_B = D_MODEL // 2
CONV_GROUPS = 16
CONV_WIDTH = 3
N_BRANCH = 2
RMS_EPS = 1e-6
GN_EPS = 64e-5

COLS_A = 3 * D_A + LORA_W + LORA_A
OFF_ZA = COLS_A
OFF_B = OFF_ZA + D_A
OFF_GATE = OFF_B + 4 * D_B
N_IN = OFF_GATE + N_BRANCH * D_MODEL

kernel_name = "hybrid_rwkv7_shortconv_gated_merge"


def rms_norm(x, g):
    xf = x.astype(jnp.float32)
    y = xf * lax.rsqrt(jnp.mean(xf * xf, axis=-1, keepdims=True) + RMS_EPS)
    return (y * g.astype(jnp.float32)).astype(x.dtype)


def shift_time(u, n):
    return jnp.pad(u, ((0, 0), (n, 0), (0, 0)))[:, : u.shape[1]]


def wkv7_scan(r, decay, k, v, kk, b):
    def step(S, inp):
        r_t, w_t, k_t, v_t, kk_t, b_t = inp
        sa = jnp.einsum('bhvk,bhk->bhv', S, -kk_t)
        S = (S * w_t[:, :, None, :]
             + sa[..., None] * b_t[:, :, None, :]
             + v_t[..., None] * k_t[:, :, None, :])
        y_t = jnp.einsum('bhvk,bhk->bhv', S, r_t)
        return S, y_t
    xs = (jnp.swapaxes(r, 0, 1), jnp.swapaxes(decay, 0, 1), jnp.swapaxes(k, 0, 1),
          jnp.swapaxes(v, 0, 1), jnp.swapaxes(kk, 0, 1), jnp.swapaxes(b, 0, 1))
    bsz, _, nh, n = r.shape
    S0 = jnp.zeros((bsz, nh, n, n), jnp.float32)
    _, y = lax.scan(step, S0, xs)
    return jnp.swapaxes(y, 0, 1)


def rwkv7_mix(pa, w0, w2, a0, a2, k_k, k_a, r_k, lnx_g, lnx_b):
    bsz, t, _ = pa.shape
    pa = pa.astype(jnp.float32)
    r = pa[..., :D_A]
    k = pa[..., D_A:2 * D_A]
    v = pa[..., 2 * D_A:3 * D_A]
    xw = pa[..., 3 * D_A:3 * D_A + LORA_W]
    xa = pa[..., 3 * D_A + LORA_W:]
    f = lambda p: p.astype(jnp.float32)
    w_log = -jax.nn.softplus(-(f(w0) + jnp.tanh(xw) @ f(w2))) - 0.5
    decay = jnp.exp(-jnp.exp(w_log))
    a = jax.nn.sigmoid(f(a0) + xa @ f(a2))
    hs = (bsz, t, N_HEADS_A, HEAD_SIZE)
    kk = (k * f(k_k)).reshape(hs)
    kk = kk / jnp.maximum(jnp.sqrt(jnp.sum(kk * kk, axis=-1, keepdims=True)), 1e-12)
    k = k * (1.0 + (a - 1.0) * f(k_a))
    r_h, k_h, v_h = r.reshape(hs), k.reshape(hs), v.reshape(hs)
    b_h = kk * a.reshape(hs)
    y = wkv7_scan(r_h, decay.reshape(hs), k_h, v_h, kk, b_h)
    mu = jnp.mean(y, axis=-1, keepdims=True)
    var = jnp.mean(jnp.square(y - mu), axis=-1, keepdims=True)
    y = ((y - mu) * lax.rsqrt(var + GN_EPS)).reshape(bsz, t, D_A) * f(lnx_g) + f(lnx_b)
    bonus = jnp.sum(r_h * k_h * f(r_k), axis=-1, keepdims=True) * v_h
    return y + bonus.reshape(bsz, t, D_A)


def short_conv_mix(pb, conv_w):
    bg = pb[..., :D_B]
    cg = pb[..., D_B:2 * D_B]
    hb = pb[..., 2 * D_B:]
    u = cg * hb
    y = conv_w[0] * shift_time(u, 2) + conv_w[1] * shift_time(u, 1) + conv_w[2] * u
    return bg * y


def setup_inputs(seed: int = 0) -> dict:
    key = jax.random.key(seed)
    ks = jax.random.split(key, 24)
    L, D = DEPTH, D_MODEL
    nrm = lambda k, s, sc: jax.random.normal(k, s, jnp.float32) * sc
    return {
        "x": nrm(ks[0], (BATCH, SEQ, D), 1.0),
        "c": nrm(ks[1], (BATCH, D), 1.0),
        "ada_w": nrm(ks[2], (L, D, 3 * D), 0.5 * D ** -0.5),
        "ada_b": nrm(ks[3], (L, 3 * D), 0.01),
        "pre_gain": 1.0 + nrm(ks[4], (L, D), 0.05),
        "post_gain": 1.0 + nrm(ks[5], (L, D), 0.05),
        "w_in": nrm(ks[6], (L, D, N_IN), D ** -0.5),
        "mu_shift": jax.random.uniform(ks[7], (L, COLS_A), jnp.float32),
        "w0": nrm(ks[8], (L, D_A), 1.0),
        "w2": nrm(ks[9], (L, LORA_W, D_A), 0.5 * LORA_W ** -0.5),
        "a0": nrm(ks[10], (L, D_A), 0.5),
        "a2": nrm(ks[11], (L, LORA_A, D_A), 0.5 * LORA_A ** -0.5),
        "k_k": 0.85 + nrm(ks[12], (L, D_A), 0.05),
        "k_a": 1.0 + nrm(ks[13], (L, D_A), 0.05),
        "r_k": nrm(ks[14], (L, N_HEADS_A, HEAD_SIZE), 0.1),
        "lnx_gain": 1.0 + nrm(ks[15], (L, D_A), 0.05),
        "lnx_bias": nrm(ks[16], (L, D_A), 0.01),
        "conv_w": nrm(ks[17], (L, CONV_WIDTH, D_B), CONV_WIDTH ** -0.5),
        "p_a": nrm(ks[18], (L, D_A, D), D_A ** -0.5),
        "p_b": nrm(ks[19], (L, D_B, D), D_B ** -0.5),
        "w_out": nrm(ks[20], (L, D, D), D ** -0.5),
    }


def reference(x, c, ada_w, ada_b, pre_gain, post_gain, w_in, mu_shift, w0, w2, a0, a2,
              k_k, k_a, r_k, lnx_gain, lnx_bias, conv_w, p_a, p_b, w_out):
    D = D_MODEL
    c_act = jax.nn.silu(c)
    for l in range(DEPTH):
        mod = c_act @ ada_w[l] + ada_b[l]
        shift = mod[:, None, :D]
        scale = mod[:, None, D:2 * D]
        gate = mod[:, None, 2 * D:]
        h = rms_norm(x, pre_gain[l]) * (1.0 + scale) + shift
        proj = h @ w_in[l]
        pa = proj[..., :COLS_A]
        pa = pa + (shift_time(pa, 1) - pa) * mu_shift[l]
        z_a = proj[..., OFF_ZA:OFF_B]
        pb = proj[..., OFF_B:OFF_B + 3 * D_B]
        z_b = proj[..., OFF_B + 3 * D_B:OFF_GATE]
        g_a = proj[..., OFF_GATE:OFF_GATE + D]
        g_b = proj[..., OFF_GATE + D:]
        y_a = rwkv7_mix(pa, w0[l], w2[l], a0[l], a2[l], k_k[l], k_a[l], r_k[l],
                        lnx_gain[l], lnx_bias[l]).astype(x.dtype)
        y_a = (y_a * jax.nn.silu(z_a)) @ p_a[l]
        y_b = (short_conv_mix(pb, conv_w[l]) * jax.nn.silu(z_b)) @ p_b[l]
        m = jax.nn.sigmoid(g_a) * y_a + jax.nn.sigmoid(g_b) * y_b
        o = m @ w_out[l]
        x = x + gate * rms_norm(o, post_gain[l])
    return x
```

```python
import contextlib
import os
import numpy as np
import concourse.bass as bass
import concourse.mybir as mybir
from concourse.bass_utils import run_bass_kernel_spmd

F32, BF16 = mybir.dt.float32, mybir.dt.bfloat16
F32R = mybir.dt.float32r
AF = mybir.ActivationFunctionType
ALU = mybir.AluOpType

D = 2048
DC = 16
DA = 1024
NCC = 97
TT = 512
NSUB = 4
CH = 64
NCH = TT // CH
NPRM = 185
NCST = 1536
C0 = float(np.exp(-0.5))
RMS_EPS = 1e-6
GN_EPS = 64e-5
SEM_LIMIT = 30000


class _Stop(Exception):
    pass


STAGE = float(os.environ.get("KSTAGE", "99"))


def _stage(n):
    if STAGE <= n:
        raise _Stop()


class Buf:
    __slots__ = ("name", "w", "r", "t", "ps")

    def __init__(self, name, t=None, ps=False):
        self.name = name
        self.ps = ps
        self.w = None
        self.r = []
        self.t = t


class KB:
    def __init__(self, nc, es):
        self.nc = nc
        self.es = es
        self.eng = {"pe": nc.tensor, "act": nc.scalar, "dve": nc.vector, "pool": nc.gpsimd, "sp": nc.sync}
        self.cnt = {e: 0 for e in self.eng}
        self.gen = {e: 0 for e in self.eng}
        self.sem = {e: es.enter_context(nc.semaphore("c_" + e + "0")) for e in self.eng}
        self.waited = {e: {} for e in self.eng}
        self.dsem = {}
        self.nwait = 0
        self.nops = 0

    def _wait(self, e, ev):
        key, sem, val = ev
        if self.waited[e].get(key, 0) >= val:
            return
        self.eng[e].wait_ge(sem, val)
        self.waited[e][key] = val
        self.nwait += 1

    def _deps(self, e, reads, writes, skip_key=None):
        best = {}
        for b in reads:
            if b.w is not None:
                ev = b.w
                if ev[0] not in best or best[ev[0]][2] < ev[2]:
                    best[ev[0]] = ev
        for b in writes:
            evs = list(b.r)
            if b.w is not None:
                evs.append(b.w)
            for ev in evs:
                if ev[0] not in best or best[ev[0]][2] < ev[2]:
                    best[ev[0]] = ev
        for ev in best.values():
            if e == "pe" and ev[0][0] == "eng" and ev[0][1] == "pe":
                continue
            if skip_key is not None and ev[0] == skip_key:
                continue
            self._wait(e, ev)

    def _record(self, myev, reads, writes):
        for b in reads:
            b.r.append(myev)
            if len(b.r) > 64:
                best = {}
                for ev in b.r:
                    if ev[0] not in best or best[ev[0]][2] < ev[2]:
                        best[ev[0]] = ev
                b.r = list(best.values())
        for b in writes:
            b.w = myev
            b.r = []

    def op(self, e, fn, reads=(), writes=(), inc=True):
        if self.cnt[e] >= SEM_LIMIT:
            self.gen[e] += 1
            self.cnt[e] = 0
            self.sem[e] = self.es.enter_context(self.nc.semaphore("c_%s%d" % (e, self.gen[e])))
        psr = [b for b in reads if b.ps]
        if psr:
            reads = [b for b in reads if not b.ps]
            writes = list(writes) + psr
        self._deps(e, reads, writes)
        inst = fn()
        self.nops += 1
        key = ("eng", e, self.gen[e])
        if inc:
            self.cnt[e] += 1
            inst.then_inc(self.sem[e], 1)
            myev = (key, self.sem[e], self.cnt[e])
        else:
            myev = (key, self.sem[e], self.cnt[e] + 1)
        self._record(myev, reads, writes)
        return inst

    def dma(self, q, slot, out, in_, reads=(), writes=(), multi=None, **kw):
        if slot not in self.dsem:
            self.dsem[slot] = [self.es.enter_context(self.nc.semaphore("d%d" % len(self.dsem))), 0]
        sem, n = self.dsem[slot]
        key = ("dma", slot)
        self._deps(q, reads, writes, skip_key=(key if multi is not None else None))
        if multi is None and n > 0:
            self._wait(q, (key, sem, 16 * n))
        inst = self.eng[q].dma_start(out=out, in_=in_, **kw)
        inst.then_inc(sem, 16)
        self.dsem[slot][1] = n + 1
        val = 16 * (n + 1) if multi is None else 16 * multi
        myev = (key, sem, val)
        self._record(myev, reads, writes)
        self.nops += 1

    def barrier(self):
        for e in self.eng:
            for e2 in self.eng:
                if e2 == e or self.cnt[e2] == 0:
                    continue
                self._wait(e, (("eng", e2, self.gen[e2]), self.sem[e2], self.cnt[e2]))
            for slot, (sem, n) in self.dsem.items():
                if n > 0:
                    self._wait(e, (("dma", slot), sem, 16 * n))


def build_program(T, L, taps=False):
    NT = T // TT
    nc = bass.Bass("TRN2", target_bir_lowering=False)
    dt_in = lambda name, shape: nc.dram_tensor(name, shape, F32, kind="ExternalInput").ap()
    x_d = dt_in("x", [T, D])
    cfm_d = dt_in("cfm", [128, DC])
    ada_d = dt_in("ada_r", [L, 48, 128, DC * 128])
    win_d = dt_in("win_r", [L, NCC, 128, DC * 128])
    pa_d = dt_in("pa_r", [L, 16, 128, 8 * 128])
    pb_d = dt_in("pb_r", [L, 16, 128, 8 * 128])
    wo_d = dt_in("wo_r", [L, 8, 128, DC * 256])
    prm_d = dt_in("prm", [128, L, NPRM])
    lora_d = dt_in("lora", [L, 128, 8 * 256])
    cst_d = dt_in("cst", [128, NCST])
    out_d = nc.dram_tensor("out", [T, D], F32, kind="ExternalOutput").ap()
    win_b = nc.dram_tensor("win_b", [L, NCC, 128, DC * 128], BF16, kind="Internal").ap()
    pa_b = nc.dram_tensor("pa_b", [L, 16, 128, 8 * 128], BF16, kind="Internal").ap()
    pb_b = nc.dram_tensor("pb_b", [L, 16, 128, 8 * 128], BF16, kind="Internal").ap()
    wo_b = nc.dram_tensor("wo_b", [L, 8, 128, DC * 256], BF16, kind="Internal").ap()

    es = contextlib.ExitStack()
    with es:
        kb = KB(nc, es)
        _n = [0]

        def sb(shape, dt=F32, name=None):
            _n[0] += 1
            nm = "%s_%d" % (name or "t", _n[0])
            t = es.enter_context(nc.sbuf_tensor(nm, list(shape), dt))
            return Buf(nm, t)

        PS = []
        for i in range(8):
            t = es.enter_context(nc.psum_tensor("ps%d" % i, [128, 512], F32))
            PS.append(Buf("ps%d" % i, t, ps=True))
        _pi = [0]

        def ps():
            b = PS[_pi[0] % 8]
            _pi[0] += 1
            return b

        V = lambda fn, r, w: kb.op("dve", fn, r, w)
        A = lambda fn, r, w: kb.op("act", fn, r, w)
        _alt = [0]

        def AV(fa, fv, r, w):
            _alt[0] += 1
            kav = os.environ.get("KAV", "")
            if kav == "act" or (kav != "dve" and _alt[0] % 2):
                return kb.op("act", fa, r, w)
            return kb.op("dve", fv, r, w)

        def MM(out_ap, lhsT, rhs, r, w, start=True, stop=True, inc=True):
            return kb.op("pe", lambda: nc.tensor.matmul(out_ap, lhsT, rhs, start=start, stop=stop), r, w, inc=inc)

        conv_ev = {}

        def issue_conv(l):
            bufs = {k: Buf("cv_%s_%d" % (k, l)) for k in ("win", "pa", "pb", "wo")}
            conv_ev[l] = bufs
            tot = NCC + 16 + 16 + 8
            slot = ("conv", l)
            for cc in range(NCC):
                kb.dma("pool", slot, win_b[l, cc], win_d[l, cc], writes=[bufs["win"]], multi=tot, max_dma_last_dim=4096)
            for m in range(16):
                kb.dma("pool", slot, pa_b[l, m], pa_d[l, m], writes=[bufs["pa"]], multi=tot, max_dma_last_dim=4096)
                kb.dma("pool", slot, pb_b[l, m], pb_d[l, m], writes=[bufs["pb"]], multi=tot, max_dma_last_dim=4096)
            for g in range(8):
                kb.dma("pool", slot, wo_b[l, g], wo_d[l, g], writes=[bufs["wo"]], multi=tot, max_dma_last_dim=4096)

        issue_conv(0)

        cst = sb([128, NCST], F32, "cst")
        kb.dma("sp", "cst", cst.t[:, :], cst_d, writes=[cst])
        ident = cst.t[:, 0:128]
        onesblk = cst.t[:, 128:256]
        meanblk = cst.t[:, 256:384]
        maskAll = cst.t[:, 384:896]
        rmask = cst.t[:, 896:1408]
        ones = cst.t[:, 1408:1536]
        zcol = cst.t[:, 384:385]
        prm = sb([128, L, NPRM], F32, "prm")
        kb.dma("sp", "prm", prm.t[:, :, :], prm_d, writes=[prm])
        der = sb([128, L, 33], F32, "der")
        for l in range(L):
            V(lambda l=l: nc.vector.tensor_scalar(der.t[:, l, 0:25], prm.t[:, l, 0:25], -1.0, 1.0, ALU.mult, ALU.add), [prm], [der])
            V(lambda l=l: nc.vector.tensor_scalar(der.t[:, l, 25:33], prm.t[:, l, 49:57], -1.0, 1.0, ALU.mult, ALU.add), [prm], [der])
        cf = sb([128, DC], F32, "cf")
        kb.dma("sp", "cf", cf.t[:, :], cfm_d, writes=[cf])
        sgc = sb([128, DC], F32, "sgc")
        sc = sb([128, DC], F32, "sc")
        A(lambda: nc.scalar.activation(out=sgc.t[:, :], in_=cf.t[:, :], func=AF.Sigmoid), [cf], [sgc])
        V(lambda: nc.vector.tensor_tensor(sc.t[:, :], cf.t[:, :], sgc.t[:, :], ALU.mult), [cf, sgc], [sc])

        modfm = sb([128, L, 48], F32, "modfm")
        g1 = sb([128, L, DC], F32, "g1")
        gpfm = sb([128, L, DC], F32, "gpfm")
        with contextlib.ExitStack() as es2:
            adab = []
            for i in range(3):
                t = es2.enter_context(nc.sbuf_tensor("adab%d" % i, [128, DC, 128], F32))
                adab.append(Buf("adab%d" % i, t))
            k = 0
            for l in range(L):
                pm = ps()
                for cc in range(48):
                    ab = adab[k % 3]
                    kb.dma("sp", ("ada", k % 3), ab.t[:, :, :], ada_d[l, cc].rearrange("p (a b) -> p a b", b=128), writes=[ab])
                    k += 1
                    for dc in range(DC):
                        MM(pm.t[:, cc:cc + 1], ab.t[:, dc, :], sc.t[:, dc:dc + 1], [ab, sc], [pm],
                           start=(dc == 0), stop=(dc == DC - 1), inc=(dc == DC - 1))
                V(lambda l=l, pm=pm: nc.vector.tensor_tensor(modfm.t[:, l, :], pm.t[:, 0:48], prm.t[:, l, 137:185], ALU.add), [pm, prm], [modfm])
                V(lambda l=l: nc.vector.scalar_tensor_tensor(g1.t[:, l, :], modfm.t[:, l, 16:32], 1.0, prm.t[:, l, 105:121], ALU.add, ALU.mult), [modfm, prm], [g1])
                V(lambda l=l: nc.vector.tensor_tensor(gpfm.t[:, l, :], modfm.t[:, l, 32:48], prm.t[:, l, 121:137], ALU.mult), [modfm, prm], [gpfm])
            kb.barrier()

        R = lambda ap: ap.bitcast(F32R)
        cstr = sb([128, 384], F32, "cstr")
        V(lambda: nc.vector.tensor_copy(R(cstr.t[:, :]), cst.t[:, 0:384]), [cst], [cstr])
        identr, onesblkr, meanblkr = R(cstr.t[:, 0:128]), R(cstr.t[:, 128:256]), R(cstr.t[:, 256:384])
        hT = sb([128, DC, TT], BF16, "hT")
        hTs = [Buf("hT%d" % i) for i in range(DC)]
        NW = 3
        wbufs = [sb([128, DC, 128], BF16, "wb") for _ in range(NW)]
        pbufs = [sb([128, 8, 128], BF16, "pbuf") for _ in range(4)]
        wobufs = [sb([128, DC, 256], BF16, "wob") for _ in range(2)]
        ya_in = sb([128, 8, TT], BF16, "ya_in")
        yb_in = sb([128, 8, TT], BF16, "yb_in")
        mix = sb([128, DC, TT], BF16, "mix")
        yas = [Buf("ya%d" % i) for i in range(8)]
        ybs = [Buf("yb%d" % i) for i in range(8)]
        mixs = [Buf("mix%d" % i) for i in range(DC)]
        Hbufs = [Buf("H%d" % i) for i in range(8)]
        GP = sb([128, D], F32, "GP")
        lorab = [sb([128, 256], F32, "lora") for _ in range(2)]
        xo = [sb([128, D], F32, "xo") for _ in range(2)]
        ss4 = sb([128, 8], F32, "ss4")
        rs4 = sb([128, 8], F32, "rs4")
        bnd = sb([128, 25], F32, "bnd")
        cbnd = sb([128, 8, 2], F32, "cbnd")
        exts = [sb([128, TT + 2], F32, "ext") for _ in range(2)]
        uext = exts[0]
        Hst = sb([128, 8, 128], F32, "Hst")
        names = ["tmpm", "r", "k", "v", "sz", "tl", "sig", "csum", "t1", "EG", "EGi", "EGp", "a", "kk",
                 "kkn", "kp", "bb", "bonus", "sqrk", "Y"]
        W = {n: sb([128, TT], F32, n) for n in names}
        for al, tgt in (("sq", "sqrk"), ("rk", "sqrk"), ("lnv", "sig"), ("rn", "csum"), ("cex", "t1"), ("fac", "t1"),
                        ("yc", "kk"), ("t2", "kkn"), ("t3", "bb")):
            W[al] = W[tgt]
        KR = sb([128, NCH, 192], F32, "KR")
        Ktb = sb([128, NCH, 128], F32, "Ktb")
        Btb = sb([128, NCH, 128], F32, "Btb")
        Vb = sb([128, NCH, 128], F32, "Vb")
        NSLOT = 3
        TM = [sb([128, 384], F32, "TM") for _ in range(NSLOT)]
        AKB = [sb([128, 512], F32, "AKB") for _ in range(NSLOT)]
        MT = [sb([128, 128], F32, "MT") for _ in range(NSLOT)]
        SXW = [[sb([128, 384], F32, "SXW") for _ in range(2)] for _ in range(2)]
        RH = sb([128, 128], F32, "RH")
        nU = sb([128, 128], F32, "nU")
        print("sbuf bytes remaining:", nc.sbuf_bytes_remaining)

        for b in (KR, Ktb, Btb, Vb):
            n_ = b.t.shape[1] * b.t.shape[2]
            V(lambda b=b, n_=n_: nc.vector.tensor_copy(R(b.t[:, :, :].rearrange("p a b -> p (a b)")), zcol.to_broadcast([128, n_])), [cst], [b])

        def v3(ap):
            return ap.rearrange("p (c t) -> p c t", t=CH)

        G = lambda fn, r, w: kb.op("pool", fn, r, w)
        _si = [0]

        def pss():
            b = PS[4 + _si[0] % 4]
            _si[0] += 1
            return b

        for l in range(L):
          try:
              _stage(0)
              P = lambda a, b2, l=l: prm.t[:, l, a:b2]
              x_src = x_d if l == 0 else out_d
              xsrc_buf = Buf("xsrc")
              V(lambda: nc.vector.memset(bnd.t[:, :], 0.0), [], [bnd])
              V(lambda: nc.vector.memset(cbnd.t[:, :, :], 0.0), [], [cbnd])
              V(lambda: nc.vector.tensor_copy(R(Hst.t[:, :, :].rearrange("p a b -> p (a b)")), zcol.to_broadcast([128, 1024])), [cst], Hbufs)
              if l + 1 < L:
                  issue_conv(l + 1)
              _stage(0.3)
              Rt = xo[0]
              for dc in range(DC):
                  V(lambda dc=dc, l=l: nc.vector.tensor_scalar(Rt.t[:, dc * 128:(dc + 1) * 128], ident, gpfm.t[:, l, dc:dc + 1], None, ALU.mult), [cst, gpfm], [Rt])
              for g in range(4):
                  pg = ps()
                  MM(pg.t[:, :], ones, Rt.t[:, g * 512:(g + 1) * 512], [cst, Rt], [pg])
                  A(lambda g=g, pg=pg: nc.scalar.copy(GP.t[:, g * 512:(g + 1) * 512], pg.t[:, :]), [pg], [GP])

              _stage(0.5)
              order = []
              for tt in range(NT):
                  seq_ = [24]
                  for j in range(8):
                      seq_ += [j, 8 + j, 16 + j, 25 + j]
                  for j in range(8):
                      seq_ += [33 + j, 41 + j, 49 + j, 57 + j]
                  for m in range(16):
                      seq_ += [65 + m, 81 + m]
                  order += seq_
              wstate = {"issued": 0, "used": 0}

              def wget(cc, l=l, order=order, wstate=wstate):
                  while wstate["issued"] < len(order) and wstate["issued"] < wstate["used"] + NW:
                      i = wstate["issued"]
                      wb = wbufs[i % NW]
                      kb.dma("sp", ("w", i % NW), wb.t[:, :, :], win_b[l, order[i]].rearrange("p (a b) -> p a b", b=128),
                             reads=[conv_ev[l]["win"]], writes=[wb])
                      wstate["issued"] += 1
                  i = wstate["used"]
                  assert order[i] == cc, (order[i], cc)
                  wstate["used"] += 1
                  return wbufs[i % NW]

              def g_inproj(ccs, banks):
                  for cc, p in zip(ccs, banks):
                      wb = wget(cc)
                      for dc in range(DC):
                          MM(p.t[:, :], wb.t[:, dc, :], hT.t[:, dc, :], [wb, hTs[dc]], [p], start=(dc == 0), stop=(dc == DC - 1), inc=(dc == DC - 1))
                          if dc % 4 == 3:
                              yield

              def inproj_now(ccs, banks):
                  for _ in g_inproj(ccs, banks):
                      pass

              _e = [0]

              def mix_shift(p, cc, outb, l=l, rr=False):
                  ext = exts[_e[0] % 2]
                  _e[0] += 1
                  A(lambda: nc.scalar.copy(ext.t[:, 1:TT + 1], p.t[:, :]), [p], [ext])
                  V(lambda: nc.vector.tensor_copy(ext.t[:, 0:1], bnd.t[:, cc:cc + 1]), [bnd], [ext])
                  tm = W["tmpm"]
                  V(lambda: nc.vector.tensor_scalar(tm.t[:, :], ext.t[:, 0:TT], prm.t[:, l, cc:cc + 1], None, ALU.mult), [ext, prm], [tm])
                  oap = R(outb.t[:, :]) if rr else outb.t[:, :]
                  V(lambda: nc.vector.scalar_tensor_tensor(oap, p.t[:, :], der.t[:, l, cc:cc + 1], tm.t[:, :], ALU.mult, ALU.add), [p, der, tm], [outb])
                  V(lambda: nc.vector.tensor_copy(bnd.t[:, cc:cc + 1], ext.t[:, TT:TT + 1]), [ext], [bnd])

              for tt in range(NT):
                  t0 = tt * TT
                  junk4 = ya_in.t[:, 0:4, :]
                  for s in range(NSUB):
                      xb = xo[s % 2]
                      kb.dma("sp", ("x", s % 2), xb.t[:, :], x_src[t0 + s * 128:t0 + (s + 1) * 128, :], reads=[xsrc_buf], writes=[xb])
                      A(lambda: nc.scalar.activation(out=junk4, in_=xb.t[:, :].rearrange("p (a b) -> p a b", b=TT), func=AF.Square, accum_out=ss4.t[:, s:s + 1]), [xb], yas[0:4] + [ss4])
                      V(lambda: nc.vector.tensor_scalar(rs4.t[:, s:s + 1], ss4.t[:, s:s + 1], 1.0 / D, RMS_EPS, ALU.mult, ALU.add), [ss4], [rs4])
                      A(lambda: nc.scalar.activation(out=rs4.t[:, s:s + 1], in_=rs4.t[:, s:s + 1], func=AF.Ln), [rs4], [rs4])
                      A(lambda: nc.scalar.activation(out=rs4.t[:, s:s + 1], in_=rs4.t[:, s:s + 1], func=AF.Exp, scale=-0.5), [rs4], [rs4])
                      V(lambda: nc.vector.tensor_scalar(xb.t[:, :], xb.t[:, :], rs4.t[:, s:s + 1], None, ALU.mult), [xb, rs4], [xb])
                      for q in range(4):
                          p = ps()
                          for i in range(4):
                              dc = q * 4 + i
                              kb.op("pe", lambda: nc.tensor.transpose(p.t[:, i * 128:(i + 1) * 128], xb.t[:, dc * 128:(dc + 1) * 128], ident),
                                    [xb, cst], [p], inc=(i == 3))
                          for i in range(4):
                              dc = q * 4 + i
                              if q % 2 == 0:
                                  A(lambda: nc.scalar.activation(out=hT.t[:, dc, s * 128:(s + 1) * 128], in_=p.t[:, i * 128:(i + 1) * 128], func=AF.Identity, scale=g1.t[:, l, dc:dc + 1], bias=modfm.t[:, l, dc:dc + 1]),
                                    [p, g1, modfm], [hTs[dc]])
                              else:
                                  V(lambda: nc.vector.tensor_scalar(hT.t[:, dc, s * 128:(s + 1) * 128], p.t[:, i * 128:(i + 1) * 128], g1.t[:, l, dc:dc + 1], modfm.t[:, l, dc:dc + 1], ALU.mult, ALU.add),
                                    [p, g1, modfm], [hTs[dc]])

                  _stage(1)
                  BIG = PS[0:4]
                  inproj_now([24], [BIG[0]])
                  tl = W["tl"]
                  mix_shift(BIG[0], 24, tl)
                  A(lambda: nc.scalar.activation(out=tl.t[0:64, :], in_=tl.t[0:64, :], func=AF.Tanh), [tl], [tl])
                  inproj_now([0, 8, 16, 25], BIG)
                  _stage(2)
                  for j in range(8):
                      r, k, v = W["r"], W["k"], W["v"]
                      lora = lorab[j % 2]
                      kb.dma("sp", ("lora", j % 2), lora.t[:, :], lora_d[l][:, j * 256:(j + 1) * 256], writes=[lora])
                      mix_shift(BIG[0], j, r)
                      mix_shift(BIG[1], 8 + j, k)
                      mix_shift(BIG[2], 16 + j, v)
                      p = BIG[3]
                      t1, sz = W["t1"], W["sz"]
                      A(lambda: nc.scalar.activation(out=t1.t[:, :], in_=p.t[:, :], func=AF.Sigmoid), [p], [t1])
                      V(lambda: nc.vector.tensor_tensor(sz.t[:, :], p.t[:, :], t1.t[:, :], ALU.mult), [p, t1], [sz])
                      pw = pss()
                      MM(pw.t[:, :], lora.t[:, 0:128], tl.t[:, :], [lora, tl], [pw])
                      sig = W["sig"]
                      A(lambda: nc.scalar.activation(out=sig.t[:, :], in_=pw.t[:, :], func=AF.Sigmoid, bias=P(25 + j, 26 + j)), [pw, prm], [sig])
                      pa_ = pss()
                      MM(pa_.t[:, :], lora.t[:, 128:256], tl.t[:, :], [lora, tl], [pa_])
                      a = W["a"]
                      A(lambda: nc.scalar.activation(out=a.t[:, :], in_=pa_.t[:, :], func=AF.Sigmoid, bias=P(33 + j, 34 + j)), [pa_, prm], [a])
                      csum, cex = W["csum"], W["cex"]
                      V(lambda: nc.vector.tensor_tensor_scan(csum.t[:, :], rmask, sig.t[:, :], 0.0, ALU.mult, ALU.add), [cst, sig], [csum])
                      G(lambda: nc.gpsimd.tensor_tensor(cex.t[:, :], csum.t[:, :], sig.t[:, :], ALU.subtract), [csum, sig], [cex])
                      EG, EGi, EGp = W["EG"], W["EGi"], W["EGp"]
                      A(lambda: nc.scalar.activation(out=EG.t[:, :], in_=csum.t[:, :], func=AF.Exp, scale=-C0), [csum], [EG])
                      A(lambda: nc.scalar.activation(out=EGi.t[:, :], in_=csum.t[:, :], func=AF.Exp, scale=C0), [csum], [EGi])
                      A(lambda: nc.scalar.activation(out=EGp.t[:, :], in_=cex.t[:, :], func=AF.Exp, scale=-C0), [cex], [EGp])
                      kk, sq = W["kk"], W["sq"]
                      V(lambda: nc.vector.tensor_scalar(kk.t[:, :], k.t[:, :], P(41 + j, 42 + j), None, ALU.mult), [k, prm], [kk])
                      A(lambda: nc.scalar.activation(out=R(sq.t[:, :]), in_=kk.t[:, :], func=AF.Square), [kk], [sq])
                      pss_ = pss()
                      MM(pss_.t[:, :], onesblkr, R(sq.t[:, :]), [cstr, sq], [pss_])
                      lnv, rn, kkn = W["lnv"], W["rn"], W["kkn"]
                      V(lambda: nc.vector.tensor_scalar(lnv.t[:, :], pss_.t[:, :], 1e-24, None, ALU.max), [pss_], [lnv])
                      A(lambda: nc.scalar.activation(out=lnv.t[:, :], in_=lnv.t[:, :], func=AF.Ln), [lnv], [lnv])
                      A(lambda: nc.scalar.activation(out=rn.t[:, :], in_=lnv.t[:, :], func=AF.Exp, scale=-0.5), [lnv], [rn])
                      G(lambda: nc.gpsimd.tensor_tensor(kkn.t[:, :], kk.t[:, :], rn.t[:, :], ALU.mult), [kk, rn], [kkn])
                      fac, kp, bb = W["fac"], W["kp"], W["bb"]
                      V(lambda: nc.vector.tensor_scalar(fac.t[:, :], a.t[:, :], P(49 + j, 50 + j), der.t[:, l, 25 + j:26 + j], ALU.mult, ALU.add), [a, prm, der], [fac])
                      G(lambda: nc.gpsimd.tensor_tensor(kp.t[:, :], k.t[:, :], fac.t[:, :], ALU.mult), [k, fac], [kp])
                      G(lambda: nc.gpsimd.tensor_tensor(bb.t[:, :], kkn.t[:, :], a.t[:, :], ALU.mult), [kkn, a], [bb])
                      rk, bonus = W["rk"], W["bonus"]
                      V(lambda: nc.vector.scalar_tensor_tensor(R(rk.t[:, :]), r.t[:, :], P(57 + j, 58 + j), kp.t[:, :], ALU.mult, ALU.mult), [r, prm, kp], [rk])
                      pbn = pss()
                      MM(pbn.t[:, :], onesblkr, R(rk.t[:, :]), [cstr, rk], [pbn])
                      V(lambda: nc.vector.tensor_tensor(bonus.t[:, :], pbn.t[:, :], v.t[:, :], ALU.mult), [pbn, v], [bonus])
                      for hd in range(2):
                          lo, hi = hd * 64, hd * 64 + 64
                          V(lambda: nc.vector.tensor_tensor(R(KR.t[lo:hi, :, lo:hi]), v3(kkn.t[lo:hi, :]), v3(EGp.t[lo:hi, :]), ALU.mult), [kkn, EGp], [KR])
                          V(lambda: nc.vector.tensor_tensor(R(Ktb.t[lo:hi, :, lo:hi]), v3(kp.t[lo:hi, :]), v3(EGi.t[lo:hi, :]), ALU.mult), [kp, EGi], [Ktb])
                          V(lambda: nc.vector.tensor_tensor(R(Btb.t[lo:hi, :, lo:hi]), v3(bb.t[lo:hi, :]), v3(EGi.t[lo:hi, :]), ALU.mult), [bb, EGi], [Btb])
                          A(lambda: nc.scalar.copy(R(Vb.t[lo:hi, :, lo:hi]), v3(v.t[lo:hi, :])), [v], [Vb])
                      V(lambda: nc.vector.tensor_tensor(R(KR.t[:, :, 128:192]), v3(r.t[:, :]), v3(EG.t[:, :]), ALU.mult), [r, EG], [KR])

                      _stage(3)
                      Hb = Hbufs[j]
                      Hj = Hst.t[:, j, :]
                      Y = W["Y"]

                      def phaseA(c):
                          sl = c % NSLOT
                          tm, akb, mt = TM[sl], AKB[sl], MT[sl]
                          S = SXW[c % 2]
                          on_act = (c % 2 == 0)

                          def evac(dst, src_ps, rds, wrs):
                              if on_act:
                                  A(lambda: nc.scalar.copy(R(dst), src_ps), rds, wrs)
                              else:
                                  V(lambda: nc.vector.tensor_copy(R(dst), src_ps), rds, wrs)

                          pT = pss()
                          for i, src in enumerate((Ktb, Btb, Vb)):
                              kb.op("pe", lambda: nc.tensor.transpose(pT.t[:, i * 128:(i + 1) * 128], src.t[:, c, :], ident),
                                    [src, cst], [pT], inc=(i == 2))
                          A(lambda: nc.scalar.copy(R(tm.t[:, :]), pT.t[:, 0:384]), [pT], [tm])
                          yield
                          pA = pss()
                          MM(pA.t[:, 0:192], R(Ktb.t[:, c, :]), R(KR.t[:, c, :]), [Ktb, KR], [pA], inc=False)
                          MM(pA.t[:, 192:384], R(Btb.t[:, c, :]), R(KR.t[:, c, :]), [Btb, KR], [pA], inc=False)
                          MM(pA.t[:, 384:512], R(KR.t[:, c, 0:128]), R(Btb.t[:, c, :]), [KR, Btb], [pA])
                          V(lambda: nc.vector.tensor_tensor(R(akb.t[:, :]), pA.t[:, :], maskAll, ALU.mult), [pA, cst], [akb])
                          yield
                          X, Wm, Yt = akb.t[:, 192:320], None, akb.t[:, 384:512]
                          srcb = akb
                          for lev in range(5):
                              dst = S[lev % 2]
                              p = pss()
                              Wr = identr if Wm is None else R(Wm)
                              rd = [srcb, cstr]
                              if lev < 4:
                                  MM(p.t[:, 0:128], R(Yt), R(X), rd, [p], inc=False)
                              MM(p.t[:, 128:256], identr, Wr, rd, [p], start=True, stop=False, inc=False)
                              MM(p.t[:, 128:256], R(Yt), Wr, rd, [p], start=False, stop=True, inc=False)
                              MM(p.t[:, 256:384], R(X), R(Yt), rd, [p])
                              if lev < 4:
                                  evac(dst.t[:, :], p.t[:, 0:384], [p], [dst])
                              else:
                                  evac(dst.t[:, 128:384], p.t[:, 128:384], [p], [dst])
                              yield
                              X, Wm, Yt = dst.t[:, 0:128], dst.t[:, 128:256], dst.t[:, 256:384]
                              srcb = dst
                          p = pss()
                          MM(p.t[:, 0:128], identr, R(Wm), [srcb, cstr], [p], start=True, stop=False, inc=False)
                          MM(p.t[:, 0:128], R(Yt), R(Wm), [srcb], [p], start=False, stop=True)
                          evac(mt.t[:, :], p.t[:, 0:128], [p], [mt])
                          yield

                      def seqc(c):
                          sl = c % NSLOT
                          tm, akb, mt = TM[sl], AKB[sl], MT[sl]
                          pR = pss()
                          MM(pR.t[:, 0:128], R(KR.t[:, c, 0:128]), R(Hj), [KR, Hb], [pR], start=True, stop=False, inc=False)
                          MM(pR.t[:, 0:128], R(akb.t[:, 0:128]), R(tm.t[:, 256:384]), [akb, tm], [pR], start=False, stop=True)
                          A(lambda: nc.scalar.copy(R(RH.t[:, :]), pR.t[:, 0:128]), [pR], [RH])
                          yield
                          pU = pss()
                          MM(pU.t[:, 0:128], R(mt.t[:, :]), R(RH.t[:, :]), [mt, RH], [pU])
                          V(lambda: nc.vector.tensor_scalar(R(nU.t[:, :]), pU.t[:, 0:128], -1.0, None, ALU.mult), [pU], [nU])
                          yield
                          pH = pss()
                          MM(pH.t[:, 0:128], identr, R(Hj), [cstr, Hb], [pH], start=True, stop=False, inc=False)
                          MM(pH.t[:, 0:128], R(tm.t[:, 0:128]), R(tm.t[:, 256:384]), [tm], [pH], start=False, stop=False, inc=False)
                          MM(pH.t[:, 0:128], R(tm.t[:, 128:256]), R(nU.t[:, :]), [tm, nU], [pH], start=False, stop=True, inc=False)
                          MM(pH.t[:, 128:192], R(Hj), R(KR.t[:, c, 128:192]), [Hb, KR], [pH], start=True, stop=False, inc=False)
                          MM(pH.t[:, 128:192], R(tm.t[:, 256:384]), R(akb.t[:, 128:192]), [tm, akb], [pH], start=False, stop=False, inc=False)
                          MM(pH.t[:, 128:192], R(nU.t[:, :]), R(akb.t[:, 320:384]), [nU, akb], [pH], start=False, stop=True)
                          A(lambda: nc.scalar.activation(out=R(Hj), in_=pH.t[:, 0:128], func=AF.Identity, scale=EG.t[:, c * 64 + 63:c * 64 + 64]), [pH, EG], [Hb])
                          A(lambda: nc.scalar.copy(R(Y.t[:, c * 64:(c + 1) * 64]), pH.t[:, 128:192]), [pH], [Y])
                          yield

                      if j < 7:
                          filler = g_inproj([j + 1, 9 + j, 17 + j, 26 + j], BIG)
                      else:
                          filler = g_inproj([33, 41, 49, 57], BIG)
                      active = []
                      nextA, doneA, doneSeq, seq_run = 0, set(), -1, False
                      it = 0
                      while doneSeq < NCH - 1:
                          while nextA < NCH and sum(1 for a_ in active if a_[0] == "A") < 2 and nextA <= doneSeq + NSLOT:
                              active.append(("A", nextA, phaseA(nextA)))
                              nextA += 1
                          if not seq_run and (doneSeq + 1) in doneA:
                              active.append(("S", doneSeq + 1, seqc(doneSeq + 1)))
                              seq_run = True
                          for item in list(active):
                              try:
                                  next(item[2])
                              except StopIteration:
                                  active.remove(item)
                                  if item[0] == "A":
                                      doneA.add(item[1])
                                  else:
                                      doneSeq = item[1]
                                      seq_run = False
                          it += 1
                          if filler is not None and it % 3 == 0:
                              try:
                                  next(filler)
                              except StopIteration:
                                  filler = None
                      if filler is not None:
                          for _ in filler:
                              pass
                      _stage(4)
                      pm_ = pss()
                      MM(pm_.t[:, :], meanblkr, R(Y.t[:, :]), [cstr, Y], [pm_])
                      yc = W["yc"]
                      V(lambda: nc.vector.tensor_tensor(yc.t[:, :], Y.t[:, :], pm_.t[:, :], ALU.subtract), [Y, pm_], [yc])
                      A(lambda: nc.scalar.activation(out=R(sq.t[:, :]), in_=yc.t[:, :], func=AF.Square), [yc], [sq])
                      pv_ = pss()
                      MM(pv_.t[:, :], meanblkr, R(sq.t[:, :]), [cstr, sq], [pv_])
                      V(lambda: nc.vector.tensor_scalar(lnv.t[:, :], pv_.t[:, :], GN_EPS, None, ALU.add), [pv_], [lnv])
                      A(lambda: nc.scalar.activation(out=lnv.t[:, :], in_=lnv.t[:, :], func=AF.Ln), [lnv], [lnv])
                      A(lambda: nc.scalar.activation(out=rn.t[:, :], in_=lnv.t[:, :], func=AF.Exp, scale=-0.5), [lnv], [rn])
                      t2, t3 = W["t2"], W["t3"]
                      G(lambda: nc.gpsimd.tensor_tensor(t2.t[:, :], yc.t[:, :], rn.t[:, :], ALU.mult), [yc, rn], [t2])
                      G(lambda: nc.gpsimd.tensor_scalar(t3.t[:, :], t2.t[:, :], P(65 + j, 66 + j), P(73 + j, 74 + j), ALU.mult, ALU.add), [t2, prm], [t3])
                      G(lambda: nc.gpsimd.tensor_tensor(t2.t[:, :], t3.t[:, :], bonus.t[:, :], ALU.add), [t3, bonus], [t2])
                      V(lambda: nc.vector.tensor_tensor(ya_in.t[:, j, :], t2.t[:, :], sz.t[:, :], ALU.mult), [t2, sz], [yas[j]])

                  _stage(5)
                  unit = [0]

                  def bankset():
                      s_ = PS[0:4] if unit[0] % 2 == 0 else PS[4:8]
                      unit[0] += 1
                      return s_

                  for j in range(8):
                      bs = bankset()
                      if j > 0:
                          inproj_now([33 + j, 41 + j, 49 + j, 57 + j], bs)
                      pbg, pcg, phb, pzb = bs
                      t1, t2, t3 = W["t1"], W["t2"], W["t3"]
                      A(lambda: nc.scalar.copy(t1.t[:, :], pcg.t[:, :]), [pcg], [t1])
                      V(lambda: nc.vector.tensor_tensor(uext.t[:, 2:TT + 2], phb.t[:, :], t1.t[:, :], ALU.mult), [phb, t1], [uext])
                      V(lambda: nc.vector.tensor_copy(uext.t[:, 0:2], cbnd.t[:, j, :]), [cbnd], [uext])
                      G(lambda: nc.gpsimd.tensor_scalar(t2.t[:, :], uext.t[:, 0:TT], P(81 + j, 82 + j), 0.0, ALU.mult, ALU.add), [uext, prm], [t2])
                      V(lambda: nc.vector.scalar_tensor_tensor(t3.t[:, :], uext.t[:, 1:TT + 1], P(89 + j, 90 + j), t2.t[:, :], ALU.mult, ALU.add), [uext, prm, t2], [t3])
                      V(lambda: nc.vector.scalar_tensor_tensor(t2.t[:, :], uext.t[:, 2:TT + 2], P(97 + j, 98 + j), t3.t[:, :], ALU.mult, ALU.add), [uext, prm, t3], [t2])
                      V(lambda: nc.vector.tensor_copy(cbnd.t[:, j, :], uext.t[:, TT:TT + 2]), [uext], [cbnd])
                      V(lambda: nc.vector.tensor_tensor(t3.t[:, :], pbg.t[:, :], t2.t[:, :], ALU.mult), [pbg, t2], [t3])
                      A(lambda: nc.scalar.activation(out=t1.t[:, :], in_=pzb.t[:, :], func=AF.Sigmoid), [pzb], [t1])
                      V(lambda: nc.vector.tensor_tensor(t2.t[:, :], pzb.t[:, :], t1.t[:, :], ALU.mult), [pzb, t1], [t2])
                      G(lambda: nc.gpsimd.tensor_tensor(yb_in.t[:, j, :], t3.t[:, :], t2.t[:, :], ALU.mult), [t3, t2], [ybs[j]])

                  _stage(6)
                  for m in range(16):
                      pab, pbb = pbufs[(2 * m) % 4], pbufs[(2 * m + 1) % 4]
                      kb.dma("sp", ("pp", (2 * m) % 4), pab.t[:, :, :], pa_b[l, m].rearrange("p (a b) -> p a b", b=128), reads=[conv_ev[l]["pa"]], writes=[pab])
                      kb.dma("sp", ("pp", (2 * m + 1) % 4), pbb.t[:, :, :], pb_b[l, m].rearrange("p (a b) -> p a b", b=128), reads=[conv_ev[l]["pb"]], writes=[pbb])
                      pya, pyb, pga, pgb = bankset()
                      for kc in range(8):
                          MM(pya.t[:, :], pab.t[:, kc, :], ya_in.t[:, kc, :], [pab, yas[kc]], [pya], start=(kc == 0), stop=(kc == 7), inc=(kc == 7))
                      for kc in range(8):
                          MM(pyb.t[:, :], pbb.t[:, kc, :], yb_in.t[:, kc, :], [pbb, ybs[kc]], [pyb], start=(kc == 0), stop=(kc == 7), inc=(kc == 7))
                      inproj_now([65 + m, 81 + m], [pga, pgb])
                      ta, tb = exts[0], exts[1]
                      A(lambda: nc.scalar.activation(out=ta.t[:, 0:TT], in_=pga.t[:, :], func=AF.Sigmoid), [pga], [ta])
                      V(lambda: nc.vector.tensor_tensor(ta.t[:, 0:TT], pya.t[:, :], ta.t[:, 0:TT], ALU.mult), [pya, ta], [ta])
                      A(lambda: nc.scalar.activation(out=tb.t[:, 0:TT], in_=pgb.t[:, :], func=AF.Sigmoid), [pgb], [tb])
                      V(lambda: nc.vector.tensor_tensor(tb.t[:, 0:TT], pyb.t[:, :], tb.t[:, 0:TT], ALU.mult), [pyb, tb], [tb])
                      G(lambda: nc.gpsimd.tensor_tensor(mix.t[:, m, :], ta.t[:, 0:TT], tb.t[:, 0:TT], ALU.add), [ta, tb], [mixs[m]])

                  _stage(7)
                  for half in range(2):
                      posb = [PS[0:4], PS[4:8]]
                      for g in range(8):
                          wo = wobufs[g % 2]
                          kb.dma("sp", ("wo", g % 2), wo.t[:, :, :], wo_b[l, g].rearrange("p (a b) -> p a b", b=256), reads=[conv_ev[l]["wo"]], writes=[wo])
                          c0_ = (g % 2) * 256
                          for sl_ in range(2):
                              s = half * 2 + sl_
                              po = posb[sl_][g // 2]
                              for dc in range(DC):
                                  MM(po.t[:, c0_:c0_ + 256], mix.t[:, dc, s * 128:(s + 1) * 128], wo.t[:, dc, :], [mixs[dc], wo], [po], start=(dc == 0), stop=(dc == DC - 1), inc=(dc == DC - 1))
                      for sl_ in range(2):
                          s = half * 2 + sl_
                          pos = posb[sl_]
                          xr = xo[1]
                          kb.dma("sp", ("x", 1), xr.t[:, :], x_src[t0 + s * 128:t0 + (s + 1) * 128, :], reads=[xsrc_buf], writes=[xr])
                          for q in range(4):
                              A(lambda: nc.scalar.activation(out=ya_in.t[:, 0, :], in_=pos[q].t[:, :], func=AF.Square, accum_out=ss4.t[:, 4 + q:5 + q]), [pos[q]], [yas[0], ss4])
                          V(lambda: nc.vector.tensor_reduce(rs4.t[:, 4:5], ss4.t[:, 4:8], mybir.AxisListType.X, ALU.add), [ss4], [rs4])
                          V(lambda: nc.vector.tensor_scalar(rs4.t[:, 4:5], rs4.t[:, 4:5], 1.0 / D, RMS_EPS, ALU.mult, ALU.add), [rs4], [rs4])
                          A(lambda: nc.scalar.activation(out=rs4.t[:, 4:5], in_=rs4.t[:, 4:5], func=AF.Ln), [rs4], [rs4])
                          A(lambda: nc.scalar.activation(out=rs4.t[:, 4:5], in_=rs4.t[:, 4:5], func=AF.Exp, scale=-0.5), [rs4], [rs4])
                          xn = xo[0]
                          for q in range(4):
                              V(lambda: nc.vector.scalar_tensor_tensor(xn.t[:, q * 512:(q + 1) * 512], pos[q].t[:, :], rs4.t[:, 4:5], GP.t[:, q * 512:(q + 1) * 512], ALU.mult, ALU.mult),
                                [pos[q], rs4, GP], [xn])
                          G(lambda: nc.gpsimd.tensor_tensor(xn.t[:, :], xn.t[:, :], xr.t[:, :], ALU.add), [xn, xr], [xn])
                          ob = Buf("orow")
                          kb.dma("pool", "st", out_d[t0 + s * 128:t0 + (s + 1) * 128, :], xn.t[:, :], reads=[xn], writes=[ob])
              kb.barrier()
          except _Stop:
            break
        kb.barrier()
        print("kernel built: ops=%d waits=%d" % (kb.nops, kb.nwait))
    return nc


def _consts():
    cst = np.zeros((128, NCST), np.float32)
    cst[:, 0:128] = np.eye(128, dtype=np.float32)
    blk = np.zeros((128, 128), np.float32)
    blk[0:64, 0:64] = 1.0
    blk[64:128, 64:128] = 1.0
    cst[:, 128:256] = blk
    cst[:, 256:384] = blk / 64.0
    s = np.arange(64)[:, None]
    t = np.arange(64)[None, :]
    su = (s < t).astype(np.float32)
    iu = (s <= t).astype(np.float32)
    mA = np.zeros((128, 192), np.float32)
    mA[0:64, 0:64] = su
    mA[64:128, 64:128] = su
    mA[0:64, 128:192] = iu
    mA[64:128, 128:192] = iu
    mQ = np.zeros((128, 128), np.float32)
    mQ[0:64, 0:64] = su.T
    mQ[64:128, 64:128] = su.T
    mB = mA.copy()
    mB[:, 0:128] *= -1.0
    cst[:, 384:576] = mA
    cst[:, 576:768] = mB
    cst[:, 768:896] = -mQ
    rm = np.ones((128, TT), np.float32)
    rm[:, ::CH] = 0.0
    cst[:, 896:1408] = rm
    cst[:, 1408:1536] = 1.0
    return cst


def prep_shared(inp, L):
    f = lambda a: np.ascontiguousarray(a, dtype=np.float32)
    sh = {}
    def relay(w, colblk):
        Lh, K, N = w.shape
        return f(w.reshape(Lh, K // 128, 128, N // colblk, colblk).transpose(0, 3, 2, 1, 4).reshape(Lh, N // colblk, 128, (K // 128) * colblk))
    sh["ada_r"] = relay(inp["ada_w"][:L], 128)
    sh["win_r"] = relay(inp["w_in"][:L], 128)
    sh["pa_r"] = relay(inp["p_a"][:L], 128)
    sh["pb_r"] = relay(inp["p_b"][:L], 128)
    sh["wo_r"] = relay(inp["w_out"][:L], 256)
    fm = lambda v, n: v.reshape(L, n, 128).transpose(0, 2, 1)
    prm = np.zeros((L, 128, NPRM), np.float32)
    prm[:, :, 0:25] = fm(inp["mu_shift"][:L], 25)
    prm[:, :, 25:33] = fm(inp["w0"][:L], 8)
    prm[:, :, 33:41] = fm(inp["a0"][:L], 8)
    prm[:, :, 41:49] = fm(inp["k_k"][:L], 8)
    prm[:, :, 49:57] = fm(inp["k_a"][:L], 8)
    prm[:, :, 57:65] = fm(inp["r_k"][:L].reshape(L, 1024), 8)
    prm[:, :, 65:73] = fm(inp["lnx_gain"][:L], 8)
    prm[:, :, 73:81] = fm(inp["lnx_bias"][:L], 8)
    cw = inp["conv_w"][:L]
    prm[:, :, 81:89] = fm(cw[:, 0], 8)
    prm[:, :, 89:97] = fm(cw[:, 1], 8)
    prm[:, :, 97:105] = fm(cw[:, 2], 8)
    prm[:, :, 105:121] = fm(inp["pre_gain"][:L], 16)
    prm[:, :, 121:137] = fm(inp["post_gain"][:L], 16)
    prm[:, :, 137:185] = fm(inp["ada_b"][:L], 48)
    sh["prm"] = f(prm.transpose(1, 0, 2))
    lora = np.zeros((L, 128, 8, 256), np.float32)
    w2 = inp["w2"][:L].reshape(L, 64, 8, 128)
    a2 = inp["a2"][:L].reshape(L, 64, 8, 128)
    lora[:, 0:64, :, 0:128] = w2
    lora[:, 64:128, :, 128:256] = a2
    sh["lora"] = f(lora.reshape(L, 128, 8 * 256))
    sh["cst"] = _consts()
    return sh


_CACHE = {}


def run(inp, T, L, nb):
    key = (T, L)
    if key not in _CACHE:
        _CACHE[key] = build_program(T, L)
    nc = _CACHE[key]
    sh = prep_shared(inp, L)
    in_maps = []
    zero = None
    slots = [0, 2, 4, 6, 1, 3, 5, 7]
    owner = {}
    for i in range(min(nb, 8)):
        owner[slots[i]] = i
    for core in range(8):
        if core in owner:
            b = owner[core]
            m = dict(sh)
            m["x"] = np.ascontiguousarray(inp["x"][b, :T], dtype=np.float32)
            m["cfm"] = np.ascontiguousarray(inp["c"][b].reshape(DC, 128).T, dtype=np.float32)
        else:
            if zero is None:
                zero = {k: np.zeros_like(v) for k, v in sh.items()}
                zero["x"] = np.zeros((T, D), np.float32)
                zero["cfm"] = np.zeros((128, DC), np.float32)
            m = zero
        in_maps.append(m)
    res = run_bass_kernel_spmd(nc, in_maps, core_ids=list(range(8)))
    inv = {b: c for c, b in owner.items()}
    return np.stack([res.results[inv[b]]["out"] for b in range(nb)], axis=0)


def kernel(**inputs):
    inp = {k: np.asarray(v) for k, v in inputs.items()}
    B, T, _ = inp["x"].shape
    L = inp["w_in"].shape[0]
    out = run(inp, T, L, B)
    return out.astype(np.float32)
```

```python
import contextlib
import os
import numpy as np
import concourse.bass as bass
import concourse.mybir as mybir
from concourse.bass_utils import run_bass_kernel_spmd

F32, BF16 = mybir.dt.float32, mybir.dt.bfloat16
F32R = mybir.dt.float32r
AF = mybir.ActivationFunctionType
ALU = mybir.AluOpType

D = 2048
DC = 16
DA = 1024
NCC = 97
TT = 512
NSUB = 4
CH = 64
NCH = TT // CH
NPRM = 185
NCST = 1536
C0 = float(np.exp(-0.5))
RMS_EPS = 1e-6
GN_EPS = 64e-5
SEM_LIMIT = 30000


class _Stop(Exception):
    pass


STAGE = float(os.environ.get("KSTAGE", "99"))


def _stage(n):
    if STAGE <= n:
        raise _Stop()


class Buf:
    __slots__ = ("name", "w", "r", "t", "ps")

    def __init__(self, name, t=None, ps=False):
        self.name = name
        self.ps = ps
        self.w = None
        self.r = []
        self.t = t


class KB:
    def __init__(self, nc, es):
        self.nc = nc
        self.es = es
        self.eng = {"pe": nc.tensor, "act": nc.scalar, "dve": nc.vector, "pool": nc.gpsimd, "sp": nc.sync}
        self.cnt = {e: 0 for e in self.eng}
        self.gen = {e: 0 for e in self.eng}
        self.sem = {e: es.enter_context(nc.semaphore("c_" + e + "0")) for e in self.eng}
        self.waited = {e: {} for e in self.eng}
        self.dsem = {}
        self.nwait = 0
        self.nops = 0

    def _wait(self, e, ev):
        key, sem, val = ev
        if self.waited[e].get(key, 0) >= val:
            return
        self.eng[e].wait_ge(sem, val)
        self.waited[e][key] = val
        self.nwait += 1

    def _deps(self, e, reads, writes, skip_key=None):
        best = {}
        for b in reads:
            if b.w is not None:
                ev = b.w
                if ev[0] not in best or best[ev[0]][2] < ev[2]:
                    best[ev[0]] = ev
        for b in writes:
            evs = list(b.r)
            if b.w is not None:
                evs.append(b.w)
            for ev in evs:
                if ev[0] not in best or best[ev[0]][2] < ev[2]:
                    best[ev[0]] = ev
        for ev in best.values():
            if e == "pe" and ev[0][0] == "eng" and ev[0][1] == "pe":
                continue
            if skip_key is not None and ev[0] == skip_key:
                continue
            self._wait(e, ev)

    def _record(self, myev, reads, writes):
        for b in reads:
            b.r.append(myev)
            if len(b.r) > 64:
                best = {}
                for ev in b.r:
                    if ev[0] not in best or best[ev[0]][2] < ev[2]:
                        best[ev[0]] = ev
                b.r = list(best.values())
        for b in writes:
            b.w = myev
            b.r = []

    def op(self, e, fn, reads=(), writes=(), inc=True):
        if self.cnt[e] >= SEM_LIMIT:
            self.gen[e] += 1
            self.cnt[e] = 0
            self.sem[e] = self.es.enter_context(self.nc.semaphore("c_%s%d" % (e, self.gen[e])))
        psr = [b for b in reads if b.ps]
        if psr:
            reads = [b for b in reads if not b.ps]
            writes = list(writes) + psr
        self._deps(e, reads, writes)
        inst = fn()
        self.nops += 1
        key = ("eng", e, self.gen[e])
        if inc:
            self.cnt[e] += 1
            inst.then_inc(self.sem[e], 1)
            myev = (key, self.sem[e], self.cnt[e])
        else:
            myev = (key, self.sem[e], self.cnt[e] + 1)
        self._record(myev, reads, writes)
        return inst

    def dma(self, q, slot, out, in_, reads=(), writes=(), multi=None, **kw):
        if slot not in self.dsem:
            self.dsem[slot] = [self.es.enter_context(self.nc.semaphore("d%d" % len(self.dsem))), 0]
        sem, n = self.dsem[slot]
        key = ("dma", slot)
        self._deps(q, reads, writes, skip_key=(key if multi is not None else None))
        if multi is None and n > 0:
            self._wait(q, (key, sem, 16 * n))
        inst = self.eng[q].dma_start(out=out, in_=in_, **kw)
        inst.then_inc(sem, 16)
        self.dsem[slot][1] = n + 1
        val = 16 * (n + 1) if multi is None else 16 * multi
        myev = (key, sem, val)
        self._record(myev, reads, writes)
        self.nops += 1

    def barrier(self):
        for e in self.eng:
            for e2 in self.eng:
                if e2 == e or self.cnt[e2] == 0:
                    continue
                self._wait(e, (("eng", e2, self.gen[e2]), self.sem[e2], self.cnt[e2]))
            for slot, (sem, n) in self.dsem.items():
                if n > 0:
                    self._wait(e, (("dma", slot), sem, 16 * n))


def build_program(T, L, taps=False):
    NT = T // TT
    nc = bass.Bass("TRN2", target_bir_lowering=False)
    dt_in = lambda name, shape: nc.dram_tensor(name, shape, F32, kind="ExternalInput").ap()
    x_d = dt_in("x", [T, D])
    cfm_d = dt_in("cfm", [128, DC])
    ada_d = dt_in("ada_r", [L, 48, 128, DC * 128])
    win_d = dt_in("win_r", [L, NCC, 128, DC * 128])
    pa_d = dt_in("pa_r", [L, 16, 128, 8 * 128])
    pb_d = dt_in("pb_r", [L, 16, 128, 8 * 128])
    wo_d = dt_in("wo_r", [L, 8, 128, DC * 256])
    prm_d = dt_in("prm", [128, L, NPRM])
    lora_d = dt_in("lora", [L, 128, 8 * 256])
    cst_d = dt_in("cst", [128, NCST])
    out_d = nc.dram_tensor("out", [T, D], F32, kind="ExternalOutput").ap()
    win_b = nc.dram_tensor("win_b", [L, NCC, 128, DC * 128], BF16, kind="Internal").ap()
    pa_b = nc.dram_tensor("pa_b", [L, 16, 128, 8 * 128], BF16, kind="Internal").ap()
    pb_b = nc.dram_tensor("pb_b", [L, 16, 128, 8 * 128], BF16, kind="Internal").ap()
    wo_b = nc.dram_tensor("wo_b", [L, 8, 128, DC * 256], BF16, kind="Internal").ap()

    es = contextlib.ExitStack()
    with es:
        kb = KB(nc, es)
        _n = [0]

        def sb(shape, dt=F32, name=None):
            _n[0] += 1
            nm = "%s_%d" % (name or "t", _n[0])
            t = es.enter_context(nc.sbuf_tensor(nm, list(shape), dt))
            return Buf(nm, t)

        PS = []
        for i in range(8):
            t = es.enter_context(nc.psum_tensor("ps%d" % i, [128, 512], F32))
            PS.append(Buf("ps%d" % i, t, ps=True))
        _pi = [0]

        def ps():
            b = PS[_pi[0] % 8]
            _pi[0] += 1
            return b

        V = lambda fn, r, w: kb.op("dve", fn, r, w)
        A = lambda fn, r, w: kb.op("act", fn, r, w)
        _alt = [0]

        def AV(fa, fv, r, w):
            _alt[0] += 1
            kav = os.environ.get("KAV", "")
            if kav == "act" or (kav != "dve" and _alt[0] % 2):
                return kb.op("act", fa, r, w)
            return kb.op("dve", fv, r, w)

        def MM(out_ap, lhsT, rhs, r, w, start=True, stop=True, inc=True):
            return kb.op("pe", lambda: nc.tensor.matmul(out_ap, lhsT, rhs, start=start, stop=stop), r, w, inc=inc)

        conv_ev = {}

        def issue_conv(l):
            bufs = {k: Buf("cv_%s_%d" % (k, l)) for k in ("win", "pa", "pb", "wo")}
            conv_ev[l] = bufs
            tot = NCC + 16 + 16 + 8
            slot = ("conv", l)
            for cc in range(NCC):
                kb.dma("pool", slot, win_b[l, cc], win_d[l, cc], writes=[bufs["win"]], multi=tot, max_dma_last_dim=4096)
            for m in range(16):
                kb.dma("pool", slot, pa_b[l, m], pa_d[l, m], writes=[bufs["pa"]], multi=tot, max_dma_last_dim=4096)
                kb.dma("pool", slot, pb_b[l, m], pb_d[l, m], writes=[bufs["pb"]], multi=tot, max_dma_last_dim=4096)
            for g in range(8):
                kb.dma("pool", slot, wo_b[l, g], wo_d[l, g], writes=[bufs["wo"]], multi=tot, max_dma_last_dim=4096)

        issue_conv(0)

        cst = sb([128, NCST], F32, "cst")
        kb.dma("sp", "cst", cst.t[:, :], cst_d, writes=[cst])
        ident = cst.t[:, 0:128]
        onesblk = cst.t[:, 128:256]
        meanblk = cst.t[:, 256:384]
        maskAll = cst.t[:, 384:896]
        rmask = cst.t[:, 896:1408]
        ones = cst.t[:, 1408:1536]
        zcol = cst.t[:, 384:385]
        prm = sb([128, L, NPRM], F32, "prm")
        kb.dma("sp", "prm", prm.t[:, :, :], prm_d, writes=[prm])
        der = sb([128, L, 33], F32, "der")
        for l in range(L):
            V(lambda l=l: nc.vector.tensor_scalar(der.t[:, l, 0:25], prm.t[:, l, 0:25], -1.0, 1.0, ALU.mult, ALU.add), [prm], [der])
            V(lambda l=l: nc.vector.tensor_scalar(der.t[:, l, 25:33], prm.t[:, l, 49:57], -1.0, 1.0, ALU.mult, ALU.add), [prm], [der])
        cf = sb([128, DC], F32, "cf")
        kb.dma("sp", "cf", cf.t[:, :], cfm_d, writes=[cf])
        sgc = sb([128, DC], F32, "sgc")
        sc = sb([128, DC], F32, "sc")
        A(lambda: nc.scalar.activation(out=sgc.t[:, :], in_=cf.t[:, :], func=AF.Sigmoid), [cf], [sgc])
        V(lambda: nc.vector.tensor_tensor(sc.t[:, :], cf.t[:, :], sgc.t[:, :], ALU.mult), [cf, sgc], [sc])

        modfm = sb([128, L, 48], F32, "modfm")
        g1 = sb([128, L, DC], F32, "g1")
        gpfm = sb([128, L, DC], F32, "gpfm")
        with contextlib.ExitStack() as es2:
            adab = []
            for i in range(3):
                t = es2.enter_context(nc.sbuf_tensor("adab%d" % i, [128, DC, 128], F32))
                adab.append(Buf("adab%d" % i, t))
            k = 0
            for l in range(L):
                pm = ps()
                for cc in range(48):
                    ab = adab[k % 3]
                    kb.dma("sp", ("ada", k % 3), ab.t[:, :, :], ada_d[l, cc].rearrange("p (a b) -> p a b", b=128), writes=[ab])
                    k += 1
                    for dc in range(DC):
                        MM(pm.t[:, cc:cc + 1], ab.t[:, dc, :], sc.t[:, dc:dc + 1], [ab, sc], [pm],
                           start=(dc == 0), stop=(dc == DC - 1), inc=(dc == DC - 1))
                V(lambda l=l, pm=pm: nc.vector.tensor_tensor(modfm.t[:, l, :], pm.t[:, 0:48], prm.t[:, l, 137:185], ALU.add), [pm, prm], [modfm])
                V(lambda l=l: nc.vector.scalar_tensor_tensor(g1.t[:, l, :], modfm.t[:, l, 16:32], 1.0, prm.t[:, l, 105:121], ALU.add, ALU.mult), [modfm, prm], [g1])
                V(lambda l=l: nc.vector.tensor_tensor(gpfm.t[:, l, :], modfm.t[:, l, 32:48], prm.t[:, l, 121:137], ALU.mult), [modfm, prm], [gpfm])
            kb.barrier()

        R = lambda ap: ap.bitcast(F32R)
        cstr = sb([128, 384], F32, "cstr")
        V(lambda: nc.vector.tensor_copy(R(cstr.t[:, :]), cst.t[:, 0:384]), [cst], [cstr])
        identr, onesblkr, meanblkr = R(cstr.t[:, 0:128]), R(cstr.t[:, 128:256]), R(cstr.t[:, 256:384])
        hT = sb([128, DC, TT], BF16, "hT")
        hTs = [Buf("hT%d" % i) for i in range(DC)]
        NW = 3
        wbufs = [sb([128, DC, 128], BF16, "wb") for _ in range(NW)]
        pbufs = [sb([128, 8, 128], BF16, "pbuf") for _ in range(4)]
        wobufs = [sb([128, DC, 256], BF16, "wob") for _ in range(2)]
        ya_in = sb([128, 8, TT], BF16, "ya_in")
        yb_in = sb([128, 8, TT], BF16, "yb_in")
        mix = sb([128, DC, TT], BF16, "mix")
        yas = [Buf("ya%d" % i) for i in range(8)]
        ybs = [Buf("yb%d" % i) for i in range(8)]
        mixs = [Buf("mix%d" % i) for i in range(DC)]
        Hbufs = [Buf("H%d" % i) for i in range(8)]
        GP = sb([128, D], F32, "GP")
        lorab = [sb([128, 256], F32, "lora") for _ in range(2)]
        xo = [sb([128, D], F32, "xo") for _ in range(2)]
        ss4 = sb([128, 8], F32, "ss4")
        rs4 = sb([128, 8], F32, "rs4")
        bnd = sb([128, 25], F32, "bnd")
        cbnd = sb([128, 8, 2], F32, "cbnd")
        exts = [sb([128, TT + 2], F32, "ext") for _ in range(2)]
        uext = exts[0]
        Hst = sb([128, 8, 128], F32, "Hst")
        names = ["tmpm", "r", "k", "v", "sz", "tl", "sig", "csum", "t1", "EG", "EGi", "EGp", "a", "kk",
                 "kkn", "kp", "bb", "bonus", "sqrk", "Y"]
        W = {n: sb([128, TT], F32, n) for n in names}
        for al, tgt in (("sq", "sqrk"), ("rk", "sqrk"), ("lnv", "sig"), ("rn", "csum"), ("cex", "t1"), ("fac", "t1"),
                        ("yc", "kk"), ("t2", "kkn"), ("t3", "bb")):
            W[al] = W[tgt]
        KR = sb([128, NCH, 192], F32, "KR")
        Ktb = sb([128, NCH, 128], F32, "Ktb")
        Btb = sb([128, NCH, 128], F32, "Btb")
        Vb = sb([128, NCH, 128], F32, "Vb")
        NSLOT = 3
        TM = [sb([128, 384], F32, "TM") for _ in range(NSLOT)]
        AKB = [sb([128, 512], F32, "AKB") for _ in range(NSLOT)]
        MT = [sb([128, 128], F32, "MT") for _ in range(NSLOT)]
        SXW = [[sb([128, 384], F32, "SXW") for _ in range(2)] for _ in range(2)]
        RH = sb([128, 128], F32, "RH")
        nU = sb([128, 128], F32, "nU")
        print("sbuf bytes remaining:", nc.sbuf_bytes_remaining)

        for b in (KR, Ktb, Btb, Vb):
            n_ = b.t.shape[1] * b.t.shape[2]
            V(lambda b=b, n_=n_: nc.vector.tensor_copy(R(b.t[:, :, :].rearrange("p a b -> p (a b)")), zcol.to_broadcast([128, n_])), [cst], [b])

        def v3(ap):
            return ap.rearrange("p (c t) -> p c t", t=CH)

        G = lambda fn, r, w: kb.op("pool", fn, r, w)
        _si = [0]

        def pss():
            b = PS[4 + _si[0] % 4]
            _si[0] += 1
            return b

        for l in range(L):
          try:
              _stage(0)
              P = lambda a, b2, l=l: prm.t[:, l, a:b2]
              x_src = x_d if l == 0 else out_d
              xsrc_buf = Buf("xsrc")
              V(lambda: nc.vector.memset(bnd.t[:, :], 0.0), [], [bnd])
              V(lambda: nc.vector.memset(cbnd.t[:, :, :], 0.0), [], [cbnd])
              V(lambda: nc.vector.tensor_copy(R(Hst.t[:, :, :].rearrange("p a b -> p (a b)")), zcol.to_broadcast([128, 1024])), [cst], Hbufs)
              if l + 1 < L:
                  issue_conv(l + 1)
              _stage(0.3)
              Rt = xo[0]
              for dc in range(DC):
                  V(lambda dc=dc, l=l: nc.vector.tensor_scalar(Rt.t[:, dc * 128:(dc + 1) * 128], ident, gpfm.t[:, l, dc:dc + 1], None, ALU.mult), [cst, gpfm], [Rt])
              for g in range(4):
                  pg = ps()
                  MM(pg.t[:, :], ones, Rt.t[:, g * 512:(g + 1) * 512], [cst, Rt], [pg])
                  A(lambda g=g, pg=pg: nc.scalar.copy(GP.t[:, g * 512:(g + 1) * 512], pg.t[:, :]), [pg], [GP])

              _stage(0.5)
              order = []
              for tt in range(NT):
                  seq_ = [24]
                  for j in range(8):
                      seq_ += [j, 8 + j, 16 + j, 25 + j]
                  for j in range(8):
                      seq_ += [33 + j, 41 + j, 49 + j, 57 + j]
                  for m in range(16):
                      seq_ += [65 + m, 81 + m]
                  order += seq_
              wstate = {"issued": 0, "used": 0}

              def wget(cc, l=l, order=order, wstate=wstate):
                  while wstate["issued"] < len(order) and wstate["issued"] < wstate["used"] + NW:
                      i = wstate["issued"]
                      wb = wbufs[i % NW]
                      kb.dma("sp", ("w", i % NW), wb.t[:, :, :], win_b[l, order[i]].rearrange("p (a b) -> p a b", b=128),
                             reads=[conv_ev[l]["win"]], writes=[wb])
                      wstate["issued"] += 1
                  i = wstate["used"]
                  assert order[i] == cc, (order[i], cc)
                  wstate["used"] += 1
                  return wbufs[i % NW]

              def g_inproj(ccs, banks):
                  for cc, p in zip(ccs, banks):
                      wb = wget(cc)
                      for dc in range(DC):
                          MM(p.t[:, :], wb.t[:, dc, :], hT.t[:, dc, :], [wb, hTs[dc]], [p], start=(dc == 0), stop=(dc == DC - 1), inc=(dc == DC - 1))
                          if dc % 4 == 3:
                              yield

              def inproj_now(ccs, banks):
                  for _ in g_inproj(ccs, banks):
                      pass

              _e = [0]

              def mix_shift(p, cc, outb, l=l, rr=False):
                  ext = exts[_e[0] % 2]
                  _e[0] += 1
                  A(lambda: nc.scalar.copy(ext.t[:, 1:TT + 1], p.t[:, :]), [p], [ext])
                  V(lambda: nc.vector.tensor_copy(ext.t[:, 0:1], bnd.t[:, cc:cc + 1]), [bnd], [ext])
                  tm = W["tmpm"]
                  V(lambda: nc.vector.tensor_scalar(tm.t[:, :], ext.t[:, 0:TT], prm.t[:, l, cc:cc + 1], None, ALU.mult), [ext, prm], [tm])
                  oap = R(outb.t[:, :]) if rr else outb.t[:, :]
                  V(lambda: nc.vector.scalar_tensor_tensor(oap, p.t[:, :], der.t[:, l, cc:cc + 1], tm.t[:, :], ALU.mult, ALU.add), [p, der, tm], [outb])
                  V(lambda: nc.vector.tensor_copy(bnd.t[:, cc:cc + 1], ext.t[:, TT:TT + 1]), [ext], [bnd])

              for tt in range(NT):
                  t0 = tt * TT
                  junk4 = ya_in.t[:, 0:4, :]
                  for s in range(NSUB):
                      xb = xo[s % 2]
                      kb.dma("sp", ("x", s % 2), xb.t[:, :], x_src[t0 + s * 128:t0 + (s + 1) * 128, :], reads=[xsrc_buf], writes=[xb])
                      A(lambda: nc.scalar.activation(out=junk4, in_=xb.t[:, :].rearrange("p (a b) -> p a b", b=TT), func=AF.Square, accum_out=ss4.t[:, s:s + 1]), [xb], yas[0:4] + [ss4])
                      V(lambda: nc.vector.tensor_scalar(rs4.t[:, s:s + 1], ss4.t[:, s:s + 1], 1.0 / D, RMS_EPS, ALU.mult, ALU.add), [ss4], [rs4])
                      A(lambda: nc.scalar.activation(out=rs4.t[:, s:s + 1], in_=rs4.t[:, s:s + 1], func=AF.Ln), [rs4], [rs4])
                      A(lambda: nc.scalar.activation(out=rs4.t[:, s:s + 1], in_=rs4.t[:, s:s + 1], func=AF.Exp, scale=-0.5), [rs4], [rs4])
                      V(lambda: nc.vector.tensor_scalar(xb.t[:, :], xb.t[:, :], rs4.t[:, s:s + 1], None, ALU.mult), [xb, rs4], [xb])
                      for q in range(4):
                          p = ps()
                          for i in range(4):
                              dc = q * 4 + i
                              kb.op("pe", lambda: nc.tensor.transpose(p.t[:, i * 128:(i + 1) * 128], xb.t[:, dc * 128:(dc + 1) * 128], ident),
                                    [xb, cst], [p], inc=(i == 3))
                          for i in range(4):
                              dc = q * 4 + i
                              if q % 2 == 0:
                                  A(lambda: nc.scalar.activation(out=hT.t[:, dc, s * 128:(s + 1) * 128], in_=p.t[:, i * 128:(i + 1) * 128], func=AF.Identity, scale=g1.t[:, l, dc:dc + 1], bias=modfm.t[:, l, dc:dc + 1]),
                                    [p, g1, modfm], [hTs[dc]])
                              else:
                                  V(lambda: nc.vector.tensor_scalar(hT.t[:, dc, s * 128:(s + 1) * 128], p.t[:, i * 128:(i + 1) * 128], g1.t[:, l, dc:dc + 1], modfm.t[:, l, dc:dc + 1], ALU.mult, ALU.add),
                                    [p, g1, modfm], [hTs[dc]])

                  _stage(1)
                  BIG = PS[0:4]
                  inproj_now([24], [BIG[0]])
                  tl = W["tl"]
                  mix_shift(BIG[0], 24, tl)
                  A(lambda: nc.scalar.activation(out=tl.t[0:64, :], in_=tl.t[0:64, :], func=AF.Tanh), [tl], [tl])
                  inproj_now([0, 8, 16, 25], BIG)
                  _stage(2)
                  for j in range(8):
                      r, k, v = W["r"], W["k"], W["v"]
                      lora = lorab[j % 2]
                      kb.dma("sp", ("lora", j % 2), lora.t[:, :], lora_d[l][:, j * 256:(j + 1) * 256], writes=[lora])
                      mix_shift(BIG[0], j, r)
                      mix_shift(BIG[1], 8 + j, k)
                      mix_shift(BIG[2], 16 + j, v)
                      p = BIG[3]
                      t1, sz = W["t1"], W["sz"]
                      A(lambda: nc.scalar.activation(out=t1.t[:, :], in_=p.t[:, :], func=AF.Sigmoid), [p], [t1])
                      V(lambda: nc.vector.tensor_tensor(sz.t[:, :], p.t[:, :], t1.t[:, :], ALU.mult), [p, t1], [sz])
                      pw = pss()
                      MM(pw.t[:, :], lora.t[:, 0:128], tl.t[:, :], [lora, tl], [pw])
                      sig = W["sig"]
                      A(lambda: nc.scalar.activation(out=sig.t[:, :], in_=pw.t[:, :], func=AF.Sigmoid, bias=P(25 + j, 26 + j)), [pw, prm], [sig])
                      pa_ = pss()
                      MM(pa_.t[:, :], lora.t[:, 128:256], tl.t[:, :], [lora, tl], [pa_])
                      a = W["a"]
                      A(lambda: nc.scalar.activation(out=a.t[:, :], in_=pa_.t[:, :], func=AF.Sigmoid, bias=P(33 + j, 34 + j)), [pa_, prm], [a])
                      csum, cex = W["csum"], W["cex"]
                      V(lambda: nc.vector.tensor_tensor_scan(csum.t[:, :], rmask, sig.t[:, :], 0.0, ALU.mult, ALU.add), [cst, sig], [csum])
                      G(lambda: nc.gpsimd.tensor_tensor(cex.t[:, :], csum.t[:, :], sig.t[:, :], ALU.subtract), [csum, sig], [cex])
                      EG, EGi, EGp = W["EG"], W["EGi"], W["EGp"]
                      A(lambda: nc.scalar.activation(out=EG.t[:, :], in_=csum.t[:, :], func=AF.Exp, scale=-C0), [csum], [EG])
                      A(lambda: nc.scalar.activation(out=EGi.t[:, :], in_=csum.t[:, :], func=AF.Exp, scale=C0), [csum], [EGi])
                      A(lambda: nc.scalar.activation(out=EGp.t[:, :], in_=cex.t[:, :], func=AF.Exp, scale=-C0), [cex], [EGp])
                      kk, sq = W["kk"], W["sq"]
                      V(lambda: nc.vector.tensor_scalar(kk.t[:, :], k.t[:, :], P(41 + j, 42 + j), None, ALU.mult), [k, prm], [kk])
                      A(lambda: nc.scalar.activation(out=R(sq.t[:, :]), in_=kk.t[:, :], func=AF.Square), [kk], [sq])
                      pss_ = pss()
                      MM(pss_.t[:, :], onesblkr, R(sq.t[:, :]), [cstr, sq], [pss_])
                      lnv, rn, kkn = W["lnv"], W["rn"], W["kkn"]
                      V(lambda: nc.vector.tensor_scalar(lnv.t[:, :], pss_.t[:, :], 1e-24, None, ALU.max), [pss_], [lnv])
                      A(lambda: nc.scalar.activation(out=lnv.t[:, :], in_=lnv.t[:, :], func=AF.Ln), [lnv], [lnv])
                      A(lambda: nc.scalar.activation(out=rn.t[:, :], in_=lnv.t[:, :], func=AF.Exp, scale=-0.5), [lnv], [rn])
                      G(lambda: nc.gpsimd.tensor_tensor(kkn.t[:, :], kk.t[:, :], rn.t[:, :], ALU.mult), [kk, rn], [kkn])
                      fac, kp, bb = W["fac"], W["kp"], W["bb"]
                      V(lambda: nc.vector.tensor_scalar(fac.t[:, :], a.t[:, :], P(49 + j, 50 + j), der.t[:, l, 25 + j:26 + j], ALU.mult, ALU.add), [a, prm, der], [fac])
                      G(lambda: nc.gpsimd.tensor_tensor(kp.t[:, :], k.t[:, :], fac.t[:, :], ALU.mult), [k, fac], [kp])
                      G(lambda: nc.gpsimd.tensor_tensor(bb.t[:, :], kkn.t[:, :], a.t[:, :], ALU.mult), [kkn, a], [bb])
                      rk, bonus = W["rk"], W["bonus"]
                      V(lambda: nc.vector.scalar_tensor_tensor(R(rk.t[:, :]), r.t[:, :], P(57 + j, 58 + j), kp.t[:, :], ALU.mult, ALU.mult), [r, prm, kp], [rk])
                      pbn = pss()
                      MM(pbn.t[:, :], onesblkr, R(rk.t[:, :]), [cstr, rk], [pbn])
                      V(lambda: nc.vector.tensor_tensor(bonus.t[:, :], pbn.t[:, :], v.t[:, :], ALU.mult), [pbn, v], [bonus])
                      for hd in range(2):
                          lo, hi = hd * 64, hd * 64 + 64
                          V(lambda: nc.vector.tensor_tensor(R(KR.t[lo:hi, :, lo:hi]), v3(kkn.t[lo:hi, :]), v3(EGp.t[lo:hi, :]), ALU.mult), [kkn, EGp], [KR])
                          V(lambda: nc.vector.tensor_tensor(R(Ktb.t[lo:hi, :, lo:hi]), v3(kp.t[lo:hi, :]), v3(EGi.t[lo:hi, :]), ALU.mult), [kp, EGi], [Ktb])
                          V(lambda: nc.vector.tensor_tensor(R(Btb.t[lo:hi, :, lo:hi]), v3(bb.t[lo:hi, :]), v3(EGi.t[lo:hi, :]), ALU.mult), [bb, EGi], [Btb])
                          A(lambda: nc.scalar.copy(R(Vb.t[lo:hi, :, lo:hi]), v3(v.t[lo:hi, :])), [v], [Vb])
                      V(lambda: nc.vector.tensor_tensor(R(KR.t[:, :, 128:192]), v3(r.t[:, :]), v3(EG.t[:, :]), ALU.mult), [r, EG], [KR])

                      _stage(3)
                      Hb = Hbufs[j]
                      Hj = Hst.t[:, j, :]
                      Y = W["Y"]

                      def phaseA(c):
                          sl = c % NSLOT
                          tm, akb, mt = TM[sl], AKB[sl], MT[sl]
                          S = SXW[c % 2]
                          odd = (c % 2 == 1)

                          def cp(dst, src_ps, rds, wrs, prefer_act=True):
                              if prefer_act:
                                  A(lambda: nc.scalar.copy(R(dst), src_ps), rds, wrs)
                              else:
                                  V(lambda: nc.vector.tensor_copy(R(dst), src_ps), rds, wrs)

                          pT = pss()
                          for i, src in enumerate((Ktb, Btb, Vb)):
                              kb.op("pe", lambda: nc.tensor.transpose(pT.t[:, i * 128:(i + 1) * 128], src.t[:, c, :], ident),
                                    [src, cst], [pT], inc=(i == 2))
                          A(lambda: nc.scalar.copy(R(tm.t[:, :]), pT.t[:, 0:384]), [pT], [tm])
                          yield
                          pA = pss()
                          MM(pA.t[:, 0:192], R(Ktb.t[:, c, :]), R(KR.t[:, c, :]), [Ktb, KR], [pA], inc=False)
                          MM(pA.t[:, 192:384], R(Btb.t[:, c, :]), R(KR.t[:, c, :]), [Btb, KR], [pA], inc=False)
                          MM(pA.t[:, 384:512], R(KR.t[:, c, 0:128]), R(Btb.t[:, c, :]), [KR, Btb], [pA])
                          V(lambda: nc.vector.tensor_tensor(R(akb.t[:, :]), pA.t[:, :], maskAll, ALU.mult), [pA, cst], [akb])
                          yield
                          X0, Y0 = akb.t[:, 192:320], akb.t[:, 384:512]
                          s0 = S[0]
                          V(lambda: nc.vector.tensor_tensor(R(s0.t[:, 128:256]), ident, X0, ALU.add), [cst, akb], [s0])
                          pX = pss()
                          MM(pX.t[:, 0:128], R(Y0), R(X0), [akb], [pX])
                          cp(s0.t[:, 0:128], pX.t[:, 0:128], [pX], [s0])
                          yield
                          pY = pss()
                          MM(pY.t[:, 0:128], R(X0), R(Y0), [akb], [pY])
                          cp(s0.t[:, 256:384], pY.t[:, 0:128], [pY], [s0], prefer_act=not odd)
                          yield
                          cur = 0
                          for lev in range(1, 5):
                              sc, sn = S[cur], S[1 - cur]
                              pXW = pss()
                              MM(pXW.t[:, 0:256], R(sc.t[:, 256:384]), R(sc.t[:, 0:256]), [sc], [pXW])
                              if lev < 4:
                                  cp(sn.t[:, 0:128], pXW.t[:, 0:128], [pXW], [sn])
                              V(lambda: nc.vector.tensor_tensor(R(sn.t[:, 128:256]), pXW.t[:, 128:256], sc.t[:, 128:256], ALU.add), [pXW, sc], [sn])
                              yield
                              pY = pss()
                              MM(pY.t[:, 0:128], R(sc.t[:, 0:128]), R(sc.t[:, 256:384]), [sc], [pY])
                              cp(sn.t[:, 256:384], pY.t[:, 0:128], [pY], [sn], prefer_act=not odd)
                              yield
                              cur = 1 - cur
                          sc = S[cur]
                          pW = pss()
                          MM(pW.t[:, 0:128], R(sc.t[:, 256:384]), R(sc.t[:, 128:256]), [sc], [pW])
                          V(lambda: nc.vector.tensor_tensor(R(mt.t[:, :]), pW.t[:, 0:128], sc.t[:, 128:256], ALU.add), [pW, sc], [mt])
                          yield

                      def seqc(c):
                          sl = c % NSLOT
                          tm, akb, mt = TM[sl], AKB[sl], MT[sl]
                          pR = pss()
                          MM(pR.t[:, 0:128], R(KR.t[:, c, 0:128]), R(Hj), [KR, Hb], [pR], start=True, stop=False, inc=False)
                          MM(pR.t[:, 0:128], R(akb.t[:, 0:128]), R(tm.t[:, 256:384]), [akb, tm], [pR], start=False, stop=True)
                          A(lambda: nc.scalar.copy(R(RH.t[:, :]), pR.t[:, 0:128]), [pR], [RH])
                          yield
                          pU = pss()
                          MM(pU.t[:, 0:128], R(mt.t[:, :]), R(RH.t[:, :]), [mt, RH], [pU])
                          V(lambda: nc.vector.tensor_scalar(R(nU.t[:, :]), pU.t[:, 0:128], -1.0, None, ALU.mult), [pU], [nU])
                          yield
                          pH = pss()
                          MM(pH.t[:, 0:128], identr, R(Hj), [cstr, Hb], [pH], start=True, stop=False, inc=False)
                          MM(pH.t[:, 0:128], R(tm.t[:, 0:128]), R(tm.t[:, 256:384]), [tm], [pH], start=False, stop=False, inc=False)
                          MM(pH.t[:, 0:128], R(tm.t[:, 128:256]), R(nU.t[:, :]), [tm, nU], [pH], start=False, stop=True, inc=False)
                          MM(pH.t[:, 128:192], R(Hj), R(KR.t[:, c, 128:192]), [Hb, KR], [pH], start=True, stop=False, inc=False)
                          MM(pH.t[:, 128:192], R(tm.t[:, 256:384]), R(akb.t[:, 128:192]), [tm, akb], [pH], start=False, stop=False, inc=False)
                          MM(pH.t[:, 128:192], R(nU.t[:, :]), R(akb.t[:, 320:384]), [nU, akb], [pH], start=False, stop=True)
                          A(lambda: nc.scalar.activation(out=R(Hj), in_=pH.t[:, 0:128], func=AF.Identity, scale=EG.t[:, c * 64 + 63:c * 64 + 64]), [pH, EG], [Hb])
                          A(lambda: nc.scalar.copy(R(Y.t[:, c * 64:(c + 1) * 64]), pH.t[:, 128:192]), [pH], [Y])
                          yield

                      if j < 7:
                          filler = g_inproj([j + 1, 9 + j, 17 + j, 26 + j], BIG)
                      else:
                          filler = g_inproj([33, 41, 49, 57], BIG)
                      active = []
                      nextA, doneA, doneSeq, seq_run = 0, set(), -1, False
                      it = 0
                      while doneSeq < NCH - 1:
                          while nextA < NCH and sum(1 for a_ in active if a_[0] == "A") < 2 and nextA <= doneSeq + NSLOT:
                              active.append(("A", nextA, phaseA(nextA)))
                              nextA += 1
                          if not seq_run and (doneSeq + 1) in doneA:
                              active.append(("S", doneSeq + 1, seqc(doneSeq + 1)))
                              seq_run = True
                          for item in list(active):
                              try:
                                  next(item[2])
                              except StopIteration:
                                  active.remove(item)
                                  if item[0] == "A":
                                      doneA.add(item[1])
                                  else:
                                      doneSeq = item[1]
                                      seq_run = False
                          it += 1
                          if filler is not None and it % 3 == 0:
                              try:
                                  next(filler)
                              except StopIteration:
                                  filler = None
                      if filler is not None:
                          for _ in filler:
                              pass
                      _stage(4)
                      pm_ = pss()
                      MM(pm_.t[:, :], meanblkr, R(Y.t[:, :]), [cstr, Y], [pm_])
                      yc = W["yc"]
                      V(lambda: nc.vector.tensor_tensor(yc.t[:, :], Y.t[:, :], pm_.t[:, :], ALU.subtract), [Y, pm_], [yc])
                      A(lambda: nc.scalar.activation(out=R(sq.t[:, :]), in_=yc.t[:, :], func=AF.Square), [yc], [sq])
                      pv_ = pss()
                      MM(pv_.t[:, :], meanblkr, R(sq.t[:, :]), [cstr, sq], [pv_])
                      V(lambda: nc.vector.tensor_scalar(lnv.t[:, :], pv_.t[:, :], GN_EPS, None, ALU.add), [pv_], [lnv])
                      A(lambda: nc.scalar.activation(out=lnv.t[:, :], in_=lnv.t[:, :], func=AF.Ln), [lnv], [lnv])
                      A(lambda: nc.scalar.activation(out=rn.t[:, :], in_=lnv.t[:, :], func=AF.Exp, scale=-0.5), [lnv], [rn])
                      t2, t3 = W["t2"], W["t3"]
                      G(lambda: nc.gpsimd.tensor_tensor(t2.t[:, :], yc.t[:, :], rn.t[:, :], ALU.mult), [yc, rn], [t2])
                      G(lambda: nc.gpsimd.tensor_scalar(t3.t[:, :], t2.t[:, :], P(65 + j, 66 + j), P(73 + j, 74 + j), ALU.mult, ALU.add), [t2, prm], [t3])
                      G(lambda: nc.gpsimd.tensor_tensor(t2.t[:, :], t3.t[:, :], bonus.t[:, :], ALU.add), [t3, bonus], [t2])
                      V(lambda: nc.vector.tensor_tensor(ya_in.t[:, j, :], t2.t[:, :], sz.t[:, :], ALU.mult), [t2, sz], [yas[j]])

                  _stage(5)
                  unit = [0]

                  def bankset():
                      s_ = PS[0:4] if unit[0] % 2 == 0 else PS[4:8]
                      unit[0] += 1
                      return s_

                  for j in range(8):
                      bs = bankset()
                      if j > 0:
                          inproj_now([33 + j, 41 + j, 49 + j, 57 + j], bs)
                      pbg, pcg, phb, pzb = bs
                      t1, t2, t3 = W["t1"], W["t2"], W["t3"]
                      A(lambda: nc.scalar.copy(t1.t[:, :], pcg.t[:, :]), [pcg], [t1])
                      V(lambda: nc.vector.tensor_tensor(uext.t[:, 2:TT + 2], phb.t[:, :], t1.t[:, :], ALU.mult), [phb, t1], [uext])
                      V(lambda: nc.vector.tensor_copy(uext.t[:, 0:2], cbnd.t[:, j, :]), [cbnd], [uext])
                      G(lambda: nc.gpsimd.tensor_scalar(t2.t[:, :], uext.t[:, 0:TT], P(81 + j, 82 + j), 0.0, ALU.mult, ALU.add), [uext, prm], [t2])
                      V(lambda: nc.vector.scalar_tensor_tensor(t3.t[:, :], uext.t[:, 1:TT + 1], P(89 + j, 90 + j), t2.t[:, :], ALU.mult, ALU.add), [uext, prm, t2], [t3])
                      V(lambda: nc.vector.scalar_tensor_tensor(t2.t[:, :], uext.t[:, 2:TT + 2], P(97 + j, 98 + j), t3.t[:, :], ALU.mult, ALU.add), [uext, prm, t3], [t2])
                      V(lambda: nc.vector.tensor_copy(cbnd.t[:, j, :], uext.t[:, TT:TT + 2]), [uext], [cbnd])
                      V(lambda: nc.vector.tensor_tensor(t3.t[:, :], pbg.t[:, :], t2.t[:, :], ALU.mult), [pbg, t2], [t3])
                      A(lambda: nc.scalar.activation(out=t1.t[:, :], in_=pzb.t[:, :], func=AF.Sigmoid), [pzb], [t1])
                      V(lambda: nc.vector.tensor_tensor(t2.t[:, :], pzb.t[:, :], t1.t[:, :], ALU.mult), [pzb, t1], [t2])
                      G(lambda: nc.gpsimd.tensor_tensor(yb_in.t[:, j, :], t3.t[:, :], t2.t[:, :], ALU.mult), [t3, t2], [ybs[j]])

                  _stage(6)
                  for m in range(16):
                      pab, pbb = pbufs[(2 * m) % 4], pbufs[(2 * m + 1) % 4]
                      kb.dma("sp", ("pp", (2 * m) % 4), pab.t[:, :, :], pa_b[l, m].rearrange("p (a b) -> p a b", b=128), reads=[conv_ev[l]["pa"]], writes=[pab])
                      kb.dma("sp", ("pp", (2 * m + 1) % 4), pbb.t[:, :, :], pb_b[l, m].rearrange("p (a b) -> p a b", b=128), reads=[conv_ev[l]["pb"]], writes=[pbb])
                      pya, pyb, pga, pgb = bankset()
                      for kc in range(8):
                          MM(pya.t[:, :], pab.t[:, kc, :], ya_in.t[:, kc, :], [pab, yas[kc]], [pya], start=(kc == 0), stop=(kc == 7), inc=(kc == 7))
                      for kc in range(8):
                          MM(pyb.t[:, :], pbb.t[:, kc, :], yb_in.t[:, kc, :], [pbb, ybs[kc]], [pyb], start=(kc == 0), stop=(kc == 7), inc=(kc == 7))
                      inproj_now([65 + m, 81 + m], [pga, pgb])
                      ta, tb = exts[0], exts[1]
                      A(lambda: nc.scalar.activation(out=ta.t[:, 0:TT], in_=pga.t[:, :], func=AF.Sigmoid), [pga], [ta])
                      V(lambda: nc.vector.tensor_tensor(ta.t[:, 0:TT], pya.t[:, :], ta.t[:, 0:TT], ALU.mult), [pya, ta], [ta])
                      A(lambda: nc.scalar.activation(out=tb.t[:, 0:TT], in_=pgb.t[:, :], func=AF.Sigmoid), [pgb], [tb])
                      V(lambda: nc.vector.tensor_tensor(tb.t[:, 0:TT], pyb.t[:, :], tb.t[:, 0:TT], ALU.mult), [pyb, tb], [tb])
                      G(lambda: nc.gpsimd.tensor_tensor(mix.t[:, m, :], ta.t[:, 0:TT], tb.t[:, 0:TT], ALU.add), [ta, tb], [mixs[m]])

                  _stage(7)
                  for half in range(2):
                      posb = [PS[0:4], PS[4:8]]
                      for g in range(8):
                          wo = wobufs[g % 2]
                          kb.dma("sp", ("wo", g % 2), wo.t[:, :, :], wo_b[l, g].rearrange("p (a b) -> p a b", b=256), reads=[conv_ev[l]["wo"]], writes=[wo])
                          c0_ = (g % 2) * 256
                          for sl_ in range(2):
                              s = half * 2 + sl_
                              po = posb[sl_][g // 2]
                              for dc in range(DC):
                                  MM(po.t[:, c0_:c0_ + 256], mix.t[:, dc, s * 128:(s + 1) * 128], wo.t[:, dc, :], [mixs[dc], wo], [po], start=(dc == 0), stop=(dc == DC - 1), inc=(dc == DC - 1))
                      for sl_ in range(2):
                          s = half * 2 + sl_
                          pos = posb[sl_]
                          xr = xo[1]
                          kb.dma("sp", ("x", 1), xr.t[:, :], x_src[t0 + s * 128:t0 + (s + 1) * 128, :], reads=[xsrc_buf], writes=[xr])
                          for q in range(4):
                              A(lambda: nc.scalar.activation(out=ya_in.t[:, 0, :], in_=pos[q].t[:, :], func=AF.Square, accum_out=ss4.t[:, 4 + q:5 + q]), [pos[q]], [yas[0], ss4])
                          V(lambda: nc.vector.tensor_reduce(rs4.t[:, 4:5], ss4.t[:, 4:8], mybir.AxisListType.X, ALU.add), [ss4], [rs4])
                          V(lambda: nc.vector.tensor_scalar(rs4.t[:, 4:5], rs4.t[:, 4:5], 1.0 / D, RMS_EPS, ALU.mult, ALU.add), [rs4], [rs4])
                          A(lambda: nc.scalar.activation(out=rs4.t[:, 4:5], in_=rs4.t[:, 4:5], func=AF.Ln), [rs4], [rs4])
                          A(lambda: nc.scalar.activation(out=rs4.t[:, 4:5], in_=rs4.t[:, 4:5], func=AF.Exp, scale=-0.5), [rs4], [rs4])
                          xn = xo[0]
                          for q in range(4):
                              V(lambda: nc.vector.scalar_tensor_tensor(xn.t[:, q * 512:(q + 1) * 512], pos[q].t[:, :], rs4.t[:, 4:5], GP.t[:, q * 512:(q + 1) * 512], ALU.mult, ALU.mult),
                                [pos[q], rs4, GP], [xn])
                          G(lambda: nc.gpsimd.tensor_tensor(xn.t[:, :], xn.t[:, :], xr.t[:, :], ALU.add), [xn, xr], [xn])
                          ob = Buf("orow")
                          kb.dma("pool", "st", out_d[t0 + s * 128:t0 + (s + 1) * 128, :], xn.t[:, :], reads=[xn], writes=[ob])
              kb.barrier()
          except _Stop:
            break
        kb.barrier()
        print("kernel built: ops=%d waits=%d" % (kb.nops, kb.nwait))
    return nc


def _consts():
    cst = np.zeros((128, NCST), np.float32)
    cst[:, 0:128] = np.eye(128, dtype=np.float32)
    blk = np.zeros((128, 128), np.float32)
    blk[0:64, 0:64] = 1.0
    blk[64:128, 64:128] = 1.0
    cst[:, 128:256] = blk
    cst[:, 256:384] = blk / 64.0
    s = np.arange(64)[:, None]
    t = np.arange(64)[None, :]
    su = (s < t).astype(np.float32)
    iu = (s <= t).astype(np.float32)
    mA = np.zeros((128, 192), np.float32)
    mA[0:64, 0:64] = su
    mA[64:128, 64:128] = su
    mA[0:64, 128:192] = iu
    mA[64:128, 128:192] = iu
    mQ = np.zeros((128, 128), np.float32)
    mQ[0:64, 0:64] = su.T
    mQ[64:128, 64:128] = su.T
    mB = mA.copy()
    mB[:, 0:128] *= -1.0
    cst[:, 384:576] = mA
    cst[:, 576:768] = mB
    cst[:, 768:896] = -mQ
    rm = np.ones((128, TT), np.float32)
    rm[:, ::CH] = 0.0
    cst[:, 896:1408] = rm
    cst[:, 1408:1536] = 1.0
    return cst


def prep_shared(inp, L):
    f = lambda a: np.ascontiguousarray(a, dtype=np.float32)
    sh = {}
    def relay(w, colblk):
        Lh, K, N = w.shape
        return f(w.reshape(Lh, K // 128, 128, N // colblk, colblk).transpose(0, 3, 2, 1, 4).reshape(Lh, N // colblk, 128, (K // 128) * colblk))
    sh["ada_r"] = relay(inp["ada_w"][:L], 128)
    sh["win_r"] = relay(inp["w_in"][:L], 128)
    sh["pa_r"] = relay(inp["p_a"][:L], 128)
    sh["pb_r"] = relay(inp["p_b"][:L], 128)
    sh["wo_r"] = relay(inp["w_out"][:L], 256)
    fm = lambda v, n: v.reshape(L, n, 128).transpose(0, 2, 1)
    prm = np.zeros((L, 128, NPRM), np.float32)
    prm[:, :, 0:25] = fm(inp["mu_shift"][:L], 25)
    prm[:, :, 25:33] = fm(inp["w0"][:L], 8)
    prm[:, :, 33:41] = fm(inp["a0"][:L], 8)
    prm[:, :, 41:49] = fm(inp["k_k"][:L], 8)
    prm[:, :, 49:57] = fm(inp["k_a"][:L], 8)
    prm[:, :, 57:65] = fm(inp["r_k"][:L].reshape(L, 1024), 8)
    prm[:, :, 65:73] = fm(inp["lnx_gain"][:L], 8)
    prm[:, :, 73:81] = fm(inp["lnx_bias"][:L], 8)
    cw = inp["conv_w"][:L]
    prm[:, :, 81:89] = fm(cw[:, 0], 8)
    prm[:, :, 89:97] = fm(cw[:, 1], 8)
    prm[:, :, 97:105] = fm(cw[:, 2], 8)
    prm[:, :, 105:121] = fm(inp["pre_gain"][:L], 16)
    prm[:, :, 121:137] = fm(inp["post_gain"][:L], 16)
    prm[:, :, 137:185] = fm(inp["ada_b"][:L], 48)
    sh["prm"] = f(prm.transpose(1, 0, 2))
    lora = np.zeros((L, 128, 8, 256), np.float32)
    w2 = inp["w2"][:L].reshape(L, 64, 8, 128)
    a2 = inp["a2"][:L].reshape(L, 64, 8, 128)
    lora[:, 0:64, :, 0:128] = w2
    lora[:, 64:128, :, 128:256] = a2
    sh["lora"] = f(lora.reshape(L, 128, 8 * 256))
    sh["cst"] = _consts()
    return sh


_CACHE = {}


def run(inp, T, L, nb):
    key = (T, L)
    if key not in _CACHE:
        _CACHE[key] = build_program(T, L)
    nc = _CACHE[key]
    sh = prep_shared(inp, L)
    in_maps = []
    zero = None
    slots = [0, 2, 4, 6, 1, 3, 5, 7]
    owner = {}
    for i in range(min(nb, 8)):
        owner[slots[i]] = i
    for core in range(8):
        if core in owner:
            b = owner[core]
            m = dict(sh)
            m["x"] = np.ascontiguousarray(inp["x"][b, :T], dtype=np.float32)
            m["cfm"] = np.ascontiguousarray(inp["c"][b].reshape(DC, 128).T, dtype=np.float32)
        else:
            if zero is None:
                zero = {k: np.zeros_like(v) for k, v in sh.items()}
                zero["x"] = np.zeros((T, D), np.float32)
                zero["cfm"] = np.zeros((128, DC), np.float32)
            m = zero
        in_maps.append(m)
    res = run_bass_kernel_spmd(nc, in_maps, core_ids=list(range(8)))
    inv = {b: c for c, b in owner.items()}
    return np.stack([res.results[inv[b]]["out"] for b in range(nb)], axis=0)


def kernel(**inputs):
    inp = {k: np.asarray(v) for k, v in inputs.items()}
    B, T, _ = inp["x"].shape
    L = inp["w_in"].shape[0]
    out = run(inp, T, L, B)
    return out.astype(np.float32)
```

```python
import contextlib
import os
import numpy as np
import concourse.bass as bass
import concourse.mybir as mybir
from concourse.bass_utils import run_bass_kernel_spmd

F32, BF16 = mybir.dt.float32, mybir.dt.bfloat16
F32R = mybir.dt.float32r
AF = mybir.ActivationFunctionType
ALU = mybir.AluOpType

D = 2048
DC = 16
DA = 1024
NCC = 97
TT = 512
NSUB = 4
CH = 64
NCH = TT // CH
NPRM = 185
NCST = 1536
C0 = float(np.exp(-0.5))
RMS_EPS = 1e-6
GN_EPS = 64e-5
SEM_LIMIT = 30000


class _Stop(Exception):
    pass


STAGE = float(os.environ.get("KSTAGE", "99"))


def _stage(n):
    if STAGE <= n:
        raise _Stop()


class Buf:
    __slots__ = ("name", "w", "r", "t", "ps")

    def __init__(self, name, t=None, ps=False):
        self.name = name
        self.ps = ps
        self.w = None
        self.r = []
        self.t = t


class KB:
    def __init__(self, nc, es):
        self.nc = nc
        self.es = es
        self.eng = {"pe": nc.tensor, "act": nc.scalar, "dve": nc.vector, "pool": nc.gpsimd, "sp": nc.sync}
        self.cnt = {e: 0 for e in self.eng}
        self.gen = {e: 0 for e in self.eng}
        self.sem = {e: es.enter_context(nc.semaphore("c_" + e + "0")) for e in self.eng}
        self.waited = {e: {} for e in self.eng}
        self.dsem = {}
        self.nwait = 0
        self.nops = 0

    def _wait(self, e, ev):
        key, sem, val = ev
        if self.waited[e].get(key, 0) >= val:
            return
        self.eng[e].wait_ge(sem, val)
        self.waited[e][key] = val
        self.nwait += 1

    def _deps(self, e, reads, writes, skip_key=None):
        best = {}
        for b in reads:
            if b.w is not None:
                ev = b.w
                if ev[0] not in best or best[ev[0]][2] < ev[2]:
                    best[ev[0]] = ev
        for b in writes:
            evs = list(b.r)
            if b.w is not None:
                evs.append(b.w)
            for ev in evs:
                if ev[0] not in best or best[ev[0]][2] < ev[2]:
                    best[ev[0]] = ev
        for ev in best.values():
            if e == "pe" and ev[0][0] == "eng" and ev[0][1] == "pe":
                continue
            if skip_key is not None and ev[0] == skip_key:
                continue
            self._wait(e, ev)

    def _record(self, myev, reads, writes):
        for b in reads:
            b.r.append(myev)
            if len(b.r) > 64:
                best = {}
                for ev in b.r:
                    if ev[0] not in best or best[ev[0]][2] < ev[2]:
                        best[ev[0]] = ev
                b.r = list(best.values())
        for b in writes:
            b.w = myev
            b.r = []

    def op(self, e, fn, reads=(), writes=(), inc=True):
        if self.cnt[e] >= SEM_LIMIT:
            self.gen[e] += 1
            self.cnt[e] = 0
            self.sem[e] = self.es.enter_context(self.nc.semaphore("c_%s%d" % (e, self.gen[e])))
        psr = [b for b in reads if b.ps]
        if psr:
            reads = [b for b in reads if not b.ps]
            writes = list(writes) + psr
        self._deps(e, reads, writes)
        inst = fn()
        self.nops += 1
        key = ("eng", e, self.gen[e])
        if inc:
            self.cnt[e] += 1
            inst.then_inc(self.sem[e], 1)
            myev = (key, self.sem[e], self.cnt[e])
        else:
            myev = (key, self.sem[e], self.cnt[e] + 1)
        self._record(myev, reads, writes)
        return inst

    def dma(self, q, slot, out, in_, reads=(), writes=(), multi=None, **kw):
        if slot not in self.dsem:
            self.dsem[slot] = [self.es.enter_context(self.nc.semaphore("d%d" % len(self.dsem))), 0]
        sem, n = self.dsem[slot]
        key = ("dma", slot)
        self._deps(q, reads, writes, skip_key=(key if multi is not None else None))
        if multi is None and n > 0:
            self._wait(q, (key, sem, 16 * n))
        inst = self.eng[q].dma_start(out=out, in_=in_, **kw)
        inst.then_inc(sem, 16)
        self.dsem[slot][1] = n + 1
        val = 16 * (n + 1) if multi is None else 16 * multi
        myev = (key, sem, val)
        self._record(myev, reads, writes)
        self.nops += 1

    def barrier(self):
        for e in self.eng:
            for e2 in self.eng:
                if e2 == e or self.cnt[e2] == 0:
                    continue
                self._wait(e, (("eng", e2, self.gen[e2]), self.sem[e2], self.cnt[e2]))
            for slot, (sem, n) in self.dsem.items():
                if n > 0:
                    self._wait(e, (("dma", slot), sem, 16 * n))


def build_program(T, L, taps=False):
    NT = T // TT
    nc = bass.Bass("TRN2", target_bir_lowering=False)
    dt_in = lambda name, shape: nc.dram_tensor(name, shape, F32, kind="ExternalInput").ap()
    x_d = dt_in("x", [T, D])
    cfm_d = dt_in("cfm", [128, DC])
    ada_d = dt_in("ada_r", [L, 48, 128, DC * 128])
    win_d = dt_in("win_r", [L, NCC, 128, DC * 128])
    pa_d = dt_in("pa_r", [L, 16, 128, 8 * 128])
    pb_d = dt_in("pb_r", [L, 16, 128, 8 * 128])
    wo_d = dt_in("wo_r", [L, 8, 128, DC * 256])
    prm_d = dt_in("prm", [128, L, NPRM])
    lora_d = dt_in("lora", [L, 128, 8 * 256])
    cst_d = dt_in("cst", [128, NCST])
    out_d = nc.dram_tensor("out", [T, D], F32, kind="ExternalOutput").ap()
    win_b = nc.dram_tensor("win_b", [L, NCC, 128, DC * 128], BF16, kind="Internal").ap()
    pa_b = nc.dram_tensor("pa_b", [L, 16, 128, 8 * 128], BF16, kind="Internal").ap()
    pb_b = nc.dram_tensor("pb_b", [L, 16, 128, 8 * 128], BF16, kind="Internal").ap()
    wo_b = nc.dram_tensor("wo_b", [L, 8, 128, DC * 256], BF16, kind="Internal").ap()

    es = contextlib.ExitStack()
    with es:
        kb = KB(nc, es)
        _n = [0]

        def sb(shape, dt=F32, name=None):
            _n[0] += 1
            nm = "%s_%d" % (name or "t", _n[0])
            t = es.enter_context(nc.sbuf_tensor(nm, list(shape), dt))
            return Buf(nm, t)

        PS = []
        for i in range(8):
            t = es.enter_context(nc.psum_tensor("ps%d" % i, [128, 512], F32))
            PS.append(Buf("ps%d" % i, t, ps=True))
        _pi = [0]

        def ps():
            b = PS[_pi[0] % 8]
            _pi[0] += 1
            return b

        V = lambda fn, r, w: kb.op("dve", fn, r, w)
        A = lambda fn, r, w: kb.op("act", fn, r, w)
        _alt = [0]

        def AV(fa, fv, r, w):
            _alt[0] += 1
            kav = os.environ.get("KAV", "")
            if kav == "act" or (kav != "dve" and _alt[0] % 2):
                return kb.op("act", fa, r, w)
            return kb.op("dve", fv, r, w)

        def MM(out_ap, lhsT, rhs, r, w, start=True, stop=True, inc=True):
            return kb.op("pe", lambda: nc.tensor.matmul(out_ap, lhsT, rhs, start=start, stop=stop), r, w, inc=inc)

        conv_ev = {}

        def issue_conv(l):
            bufs = {k: Buf("cv_%s_%d" % (k, l)) for k in ("win", "pa", "pb", "wo")}
            conv_ev[l] = bufs
            tot = NCC + 16 + 16 + 8
            slot = ("conv", l)
            for cc in range(NCC):
                kb.dma("pool", slot, win_b[l, cc], win_d[l, cc], writes=[bufs["win"]], multi=tot, max_dma_last_dim=4096)
            for m in range(16):
                kb.dma("pool", slot, pa_b[l, m], pa_d[l, m], writes=[bufs["pa"]], multi=tot, max_dma_last_dim=4096)
                kb.dma("pool", slot, pb_b[l, m], pb_d[l, m], writes=[bufs["pb"]], multi=tot, max_dma_last_dim=4096)
            for g in range(8):
                kb.dma("pool", slot, wo_b[l, g], wo_d[l, g], writes=[bufs["wo"]], multi=tot, max_dma_last_dim=4096)

        issue_conv(0)

        cst = sb([128, NCST], F32, "cst")
        kb.dma("sp", "cst", cst.t[:, :], cst_d, writes=[cst])
        ident = cst.t[:, 0:128]
        onesblk = cst.t[:, 128:256]
        meanblk = cst.t[:, 256:384]
        maskAll = cst.t[:, 384:896]
        rmask = cst.t[:, 896:1408]
        ones = cst.t[:, 1408:1536]
        zcol = cst.t[:, 384:385]
        prm = sb([128, L, NPRM], F32, "prm")
        kb.dma("sp", "prm", prm.t[:, :, :], prm_d, writes=[prm])
        der = sb([128, L, 33], F32, "der")
        for l in range(L):
            V(lambda l=l: nc.vector.tensor_scalar(der.t[:, l, 0:25], prm.t[:, l, 0:25], -1.0, 1.0, ALU.mult, ALU.add), [prm], [der])
            V(lambda l=l: nc.vector.tensor_scalar(der.t[:, l, 25:33], prm.t[:, l, 49:57], -1.0, 1.0, ALU.mult, ALU.add), [prm], [der])
        cf = sb([128, DC], F32, "cf")
        kb.dma("sp", "cf", cf.t[:, :], cfm_d, writes=[cf])
        sgc = sb([128, DC], F32, "sgc")
        sc = sb([128, DC], F32, "sc")
        A(lambda: nc.scalar.activation(out=sgc.t[:, :], in_=cf.t[:, :], func=AF.Sigmoid), [cf], [sgc])
        V(lambda: nc.vector.tensor_tensor(sc.t[:, :], cf.t[:, :], sgc.t[:, :], ALU.mult), [cf, sgc], [sc])

        modfm = sb([128, L, 48], F32, "modfm")
        g1 = sb([128, L, DC], F32, "g1")
        gpfm = sb([128, L, DC], F32, "gpfm")
        with contextlib.ExitStack() as es2:
            adab = []
            for i in range(3):
                t = es2.enter_context(nc.sbuf_tensor("adab%d" % i, [128, DC, 128], F32))
                adab.append(Buf("adab%d" % i, t))
            k = 0
            for l in range(L):
                pm = ps()
                for cc in range(48):
                    ab = adab[k % 3]
                    kb.dma("sp", ("ada", k % 3), ab.t[:, :, :], ada_d[l, cc].rearrange("p (a b) -> p a b", b=128), writes=[ab])
                    k += 1
                    for dc in range(DC):
                        MM(pm.t[:, cc:cc + 1], ab.t[:, dc, :], sc.t[:, dc:dc + 1], [ab, sc], [pm],
                           start=(dc == 0), stop=(dc == DC - 1), inc=(dc == DC - 1))
                V(lambda l=l, pm=pm: nc.vector.tensor_tensor(modfm.t[:, l, :], pm.t[:, 0:48], prm.t[:, l, 137:185], ALU.add), [pm, prm], [modfm])
                V(lambda l=l: nc.vector.scalar_tensor_tensor(g1.t[:, l, :], modfm.t[:, l, 16:32], 1.0, prm.t[:, l, 105:121], ALU.add, ALU.mult), [modfm, prm], [g1])
                V(lambda l=l: nc.vector.tensor_tensor(gpfm.t[:, l, :], modfm.t[:, l, 32:48], prm.t[:, l, 121:137], ALU.mult), [modfm, prm], [gpfm])
            kb.barrier()

        R = lambda ap: ap.bitcast(F32R)
        cstr = sb([128, 384], F32, "cstr")
        V(lambda: nc.vector.tensor_copy(R(cstr.t[:, :]), cst.t[:, 0:384]), [cst], [cstr])
        identr, onesblkr, meanblkr = R(cstr.t[:, 0:128]), R(cstr.t[:, 128:256]), R(cstr.t[:, 256:384])
        hT = sb([128, DC, TT], BF16, "hT")
        hTs = [Buf("hT%d" % i) for i in range(DC)]
        NW = 3
        wbufs = [sb([128, DC, 128], BF16, "wb") for _ in range(NW)]
        pbufs = [sb([128, 8, 128], BF16, "pbuf") for _ in range(3)]
        wobufs = [sb([128, DC, 256], BF16, "wob") for _ in range(2)]
        ya_in = sb([128, 8, TT], BF16, "ya_in")
        yb_in = sb([128, 8, TT], BF16, "yb_in")
        mix = sb([128, DC, TT], BF16, "mix")
        yas = [Buf("ya%d" % i) for i in range(8)]
        ybs = [Buf("yb%d" % i) for i in range(8)]
        mixs = [Buf("mix%d" % i) for i in range(DC)]
        Hbufs = [Buf("H%d" % i) for i in range(8)]
        GP = sb([128, D], F32, "GP")
        lorab = [sb([128, 256], F32, "lora") for _ in range(1)]
        xo = [sb([128, D], F32, "xo") for _ in range(2)]
        ss4 = sb([128, 8], F32, "ss4")
        rs4 = sb([128, 8], F32, "rs4")
        bnd = sb([128, 25], F32, "bnd")
        cbnd = sb([128, 8, 2], F32, "cbnd")
        exts = [sb([128, TT + 2], F32, "ext") for _ in range(1)]
        uext = exts[0]
        Hst = sb([128, 8, 128], F32, "Hst")
        names = ["r", "k", "v", "sz", "tl", "sig", "csum", "t1", "EG", "EGi", "a", "kk",
                 "kkn", "kp", "bb", "bonus", "sqrk", "Y"]
        W = {n: sb([128, TT], F32, n) for n in names}
        for al, tgt in (("sq", "sqrk"), ("rk", "sqrk"), ("lnv", "sig"), ("rn", "csum"), ("cex", "t1"), ("EGp", "t1"), ("fac", "sig"), ("tmpm", "a"),
                        ("yc", "kk"), ("t2", "kkn"), ("t3", "bb")):
            W[al] = W[tgt]
        KR = sb([128, NCH, 192], F32, "KR")
        Ktb = sb([128, NCH, 128], F32, "Ktb")
        Btb = sb([128, NCH, 128], F32, "Btb")
        Vb = sb([128, NCH, 128], F32, "Vb")
        NSLOT = 4
        NCHAIN = 3
        TM = [sb([128, 384], F32, "TM") for _ in range(NSLOT)]
        AKB = [sb([128, 512], F32, "AKB") for _ in range(NSLOT)]
        MT = [sb([128, 128], F32, "MT") for _ in range(NSLOT)]
        SXW = [[sb([128, 384], F32, "SXW") for _ in range(2)] for _ in range(NCHAIN)]
        RH = sb([128, 128], F32, "RH")
        nU = sb([128, 128], F32, "nU")
        print("sbuf bytes remaining:", nc.sbuf_bytes_remaining)

        for b in (KR, Ktb, Btb, Vb):
            n_ = b.t.shape[1] * b.t.shape[2]
            V(lambda b=b, n_=n_: nc.vector.tensor_copy(R(b.t[:, :, :].rearrange("p a b -> p (a b)")), zcol.to_broadcast([128, n_])), [cst], [b])

        def v3(ap):
            return ap.rearrange("p (c t) -> p c t", t=CH)

        G = lambda fn, r, w: kb.op("pool", fn, r, w)
        _si = [0]

        def pss():
            b = PS[4 + _si[0] % 4]
            _si[0] += 1
            return b

        for l in range(L):
          try:
              _stage(0)
              P = lambda a, b2, l=l: prm.t[:, l, a:b2]
              x_src = x_d if l == 0 else out_d
              xsrc_buf = Buf("xsrc")
              V(lambda: nc.vector.memset(bnd.t[:, :], 0.0), [], [bnd])
              V(lambda: nc.vector.memset(cbnd.t[:, :, :], 0.0), [], [cbnd])
              V(lambda: nc.vector.tensor_copy(R(Hst.t[:, :, :].rearrange("p a b -> p (a b)")), zcol.to_broadcast([128, 1024])), [cst], Hbufs)
              if l + 1 < L:
                  issue_conv(l + 1)
              _stage(0.3)
              Rt = xo[0]
              for dc in range(DC):
                  V(lambda dc=dc, l=l: nc.vector.tensor_scalar(Rt.t[:, dc * 128:(dc + 1) * 128], ident, gpfm.t[:, l, dc:dc + 1], None, ALU.mult), [cst, gpfm], [Rt])
              for g in range(4):
                  pg = ps()
                  MM(pg.t[:, :], ones, Rt.t[:, g * 512:(g + 1) * 512], [cst, Rt], [pg])
                  A(lambda g=g, pg=pg: nc.scalar.copy(GP.t[:, g * 512:(g + 1) * 512], pg.t[:, :]), [pg], [GP])

              _stage(0.5)
              order = []
              for tt in range(NT):
                  seq_ = [24]
                  for j in range(8):
                      seq_ += [j, 8 + j, 16 + j, 25 + j]
                  for j in range(8):
                      seq_ += [33 + j, 41 + j, 49 + j, 57 + j]
                  for m in range(16):
                      seq_ += [65 + m, 81 + m]
                  order += seq_
              wstate = {"issued": 0, "used": 0}

              def wget(cc, l=l, order=order, wstate=wstate):
                  while wstate["issued"] < len(order) and wstate["issued"] < wstate["used"] + NW:
                      i = wstate["issued"]
                      wb = wbufs[i % NW]
                      kb.dma("sp", ("w", i % NW), wb.t[:, :, :], win_b[l, order[i]].rearrange("p (a b) -> p a b", b=128),
                             reads=[conv_ev[l]["win"]], writes=[wb])
                      wstate["issued"] += 1
                  i = wstate["used"]
                  assert order[i] == cc, (order[i], cc)
                  wstate["used"] += 1
                  return wbufs[i % NW]

              def g_inproj(ccs, banks):
                  for cc, p in zip(ccs, banks):
                      wb = wget(cc)
                      for dc in range(DC):
                          MM(p.t[:, :], wb.t[:, dc, :], hT.t[:, dc, :], [wb, hTs[dc]], [p], start=(dc == 0), stop=(dc == DC - 1), inc=(dc == DC - 1))
                          if dc % 4 == 3:
                              yield

              def inproj_now(ccs, banks):
                  for _ in g_inproj(ccs, banks):
                      pass

              _e = [0]

              def mix_shift(p, cc, outb, l=l, rr=False):
                  ext = exts[0]
                  _e[0] += 1
                  A(lambda: nc.scalar.copy(ext.t[:, 1:TT + 1], p.t[:, :]), [p], [ext])
                  V(lambda: nc.vector.tensor_copy(ext.t[:, 0:1], bnd.t[:, cc:cc + 1]), [bnd], [ext])
                  tm = W["tmpm"]
                  V(lambda: nc.vector.tensor_scalar(tm.t[:, :], ext.t[:, 0:TT], prm.t[:, l, cc:cc + 1], None, ALU.mult), [ext, prm], [tm])
                  oap = R(outb.t[:, :]) if rr else outb.t[:, :]
                  V(lambda: nc.vector.scalar_tensor_tensor(oap, p.t[:, :], der.t[:, l, cc:cc + 1], tm.t[:, :], ALU.mult, ALU.add), [p, der, tm], [outb])
                  V(lambda: nc.vector.tensor_copy(bnd.t[:, cc:cc + 1], ext.t[:, TT:TT + 1]), [ext], [bnd])

              for tt in range(NT):
                  t0 = tt * TT
                  junk4 = ya_in.t[:, 0:4, :]
                  for s in range(NSUB):
                      xb = xo[s % 2]
                      kb.dma("sp", ("x", s % 2), xb.t[:, :], x_src[t0 + s * 128:t0 + (s + 1) * 128, :], reads=[xsrc_buf], writes=[xb])
                      A(lambda: nc.scalar.activation(out=junk4, in_=xb.t[:, :].rearrange("p (a b) -> p a b", b=TT), func=AF.Square, accum_out=ss4.t[:, s:s + 1]), [xb], yas[0:4] + [ss4])
                      V(lambda: nc.vector.tensor_scalar(rs4.t[:, s:s + 1], ss4.t[:, s:s + 1], 1.0 / D, RMS_EPS, ALU.mult, ALU.add), [ss4], [rs4])
                      A(lambda: nc.scalar.activation(out=rs4.t[:, s:s + 1], in_=rs4.t[:, s:s + 1], func=AF.Ln), [rs4], [rs4])
                      A(lambda: nc.scalar.activation(out=rs4.t[:, s:s + 1], in_=rs4.t[:, s:s + 1], func=AF.Exp, scale=-0.5), [rs4], [rs4])
                      V(lambda: nc.vector.tensor_scalar(xb.t[:, :], xb.t[:, :], rs4.t[:, s:s + 1], None, ALU.mult), [xb, rs4], [xb])
                      for q in range(4):
                          p = ps()
                          for i in range(4):
                              dc = q * 4 + i
                              kb.op("pe", lambda: nc.tensor.transpose(p.t[:, i * 128:(i + 1) * 128], xb.t[:, dc * 128:(dc + 1) * 128], ident),
                                    [xb, cst], [p], inc=(i == 3))
                          for i in range(4):
                              dc = q * 4 + i
                              if q % 2 == 0:
                                  A(lambda: nc.scalar.activation(out=hT.t[:, dc, s * 128:(s + 1) * 128], in_=p.t[:, i * 128:(i + 1) * 128], func=AF.Identity, scale=g1.t[:, l, dc:dc + 1], bias=modfm.t[:, l, dc:dc + 1]),
                                    [p, g1, modfm], [hTs[dc]])
                              else:
                                  V(lambda: nc.vector.tensor_scalar(hT.t[:, dc, s * 128:(s + 1) * 128], p.t[:, i * 128:(i + 1) * 128], g1.t[:, l, dc:dc + 1], modfm.t[:, l, dc:dc + 1], ALU.mult, ALU.add),
                                    [p, g1, modfm], [hTs[dc]])

                  _stage(1)
                  BIG = PS[0:4]
                  inproj_now([24], [BIG[0]])
                  tl = W["tl"]
                  mix_shift(BIG[0], 24, tl)
                  A(lambda: nc.scalar.activation(out=tl.t[0:64, :], in_=tl.t[0:64, :], func=AF.Tanh), [tl], [tl])
                  inproj_now([0, 8, 16, 25], BIG)
                  _stage(2)
                  for j in range(8):
                      r, k, v = W["r"], W["k"], W["v"]
                      lora = lorab[0]
                      premixed = (j > 0)
                      kb.dma("sp", ("lora", 0), lora.t[:, :], lora_d[l][:, j * 256:(j + 1) * 256], writes=[lora])
                      if not premixed:
                          mix_shift(BIG[0], j, r)
                          mix_shift(BIG[1], 8 + j, k)
                          mix_shift(BIG[2], 16 + j, v)
                      p = BIG[3]
                      t1, sz = W["t1"], W["sz"]
                      A(lambda: nc.scalar.activation(out=t1.t[:, :], in_=p.t[:, :], func=AF.Sigmoid), [p], [t1])
                      V(lambda: nc.vector.tensor_tensor(sz.t[:, :], p.t[:, :], t1.t[:, :], ALU.mult), [p, t1], [sz])
                      pw = pss()
                      MM(pw.t[:, :], lora.t[:, 0:128], tl.t[:, :], [lora, tl], [pw])
                      sig = W["sig"]
                      A(lambda: nc.scalar.activation(out=sig.t[:, :], in_=pw.t[:, :], func=AF.Sigmoid, bias=P(25 + j, 26 + j)), [pw, prm], [sig])
                      pa_ = pss()
                      MM(pa_.t[:, :], lora.t[:, 128:256], tl.t[:, :], [lora, tl], [pa_])
                      a = W["a"]
                      A(lambda: nc.scalar.activation(out=a.t[:, :], in_=pa_.t[:, :], func=AF.Sigmoid, bias=P(33 + j, 34 + j)), [pa_, prm], [a])
                      csum, cex = W["csum"], W["cex"]
                      V(lambda: nc.vector.tensor_tensor_scan(csum.t[:, :], rmask, sig.t[:, :], 0.0, ALU.mult, ALU.add), [cst, sig], [csum])
                      G(lambda: nc.gpsimd.tensor_tensor(cex.t[:, :], csum.t[:, :], sig.t[:, :], ALU.subtract), [csum, sig], [cex])
                      EG, EGi, EGp = W["EG"], W["EGi"], W["EGp"]
                      A(lambda: nc.scalar.activation(out=EG.t[:, :], in_=csum.t[:, :], func=AF.Exp, scale=-C0), [csum], [EG])
                      A(lambda: nc.scalar.activation(out=EGi.t[:, :], in_=csum.t[:, :], func=AF.Exp, scale=C0), [csum], [EGi])
                      A(lambda: nc.scalar.activation(out=EGp.t[:, :], in_=cex.t[:, :], func=AF.Exp, scale=-C0), [cex], [EGp])
                      kk, sq = W["kk"], W["sq"]
                      V(lambda: nc.vector.tensor_scalar(kk.t[:, :], k.t[:, :], P(41 + j, 42 + j), None, ALU.mult), [k, prm], [kk])
                      A(lambda: nc.scalar.activation(out=R(sq.t[:, :]), in_=kk.t[:, :], func=AF.Square), [kk], [sq])
                      pss_ = pss()
                      MM(pss_.t[:, :], onesblkr, R(sq.t[:, :]), [cstr, sq], [pss_])
                      lnv, rn, kkn = W["lnv"], W["rn"], W["kkn"]
                      V(lambda: nc.vector.tensor_scalar(lnv.t[:, :], pss_.t[:, :], 1e-24, None, ALU.max), [pss_], [lnv])
                      A(lambda: nc.scalar.activation(out=lnv.t[:, :], in_=lnv.t[:, :], func=AF.Ln), [lnv], [lnv])
                      A(lambda: nc.scalar.activation(out=rn.t[:, :], in_=lnv.t[:, :], func=AF.Exp, scale=-0.5), [lnv], [rn])
                      G(lambda: nc.gpsimd.tensor_tensor(kkn.t[:, :], kk.t[:, :], rn.t[:, :], ALU.mult), [kk, rn], [kkn])
                      fac, kp, bb = W["fac"], W["kp"], W["bb"]
                      V(lambda: nc.vector.tensor_scalar(fac.t[:, :], a.t[:, :], P(49 + j, 50 + j), der.t[:, l, 25 + j:26 + j], ALU.mult, ALU.add), [a, prm, der], [fac])
                      G(lambda: nc.gpsimd.tensor_tensor(kp.t[:, :], k.t[:, :], fac.t[:, :], ALU.mult), [k, fac], [kp])
                      G(lambda: nc.gpsimd.tensor_tensor(bb.t[:, :], kkn.t[:, :], a.t[:, :], ALU.mult), [kkn, a], [bb])
                      rk, bonus = W["rk"], W["bonus"]
                      V(lambda: nc.vector.scalar_tensor_tensor(R(rk.t[:, :]), r.t[:, :], P(57 + j, 58 + j), kp.t[:, :], ALU.mult, ALU.mult), [r, prm, kp], [rk])
                      pbn = pss()
                      MM(pbn.t[:, :], onesblkr, R(rk.t[:, :]), [cstr, rk], [pbn])
                      V(lambda: nc.vector.tensor_tensor(bonus.t[:, :], pbn.t[:, :], v.t[:, :], ALU.mult), [pbn, v], [bonus])
                      for hd in range(2):
                          lo, hi = hd * 64, hd * 64 + 64
                          V(lambda: nc.vector.tensor_tensor(R(KR.t[lo:hi, :, lo:hi]), v3(kkn.t[lo:hi, :]), v3(EGp.t[lo:hi, :]), ALU.mult), [kkn, EGp], [KR])
                          V(lambda: nc.vector.tensor_tensor(R(Ktb.t[lo:hi, :, lo:hi]), v3(kp.t[lo:hi, :]), v3(EGi.t[lo:hi, :]), ALU.mult), [kp, EGi], [Ktb])
                          V(lambda: nc.vector.tensor_tensor(R(Btb.t[lo:hi, :, lo:hi]), v3(bb.t[lo:hi, :]), v3(EGi.t[lo:hi, :]), ALU.mult), [bb, EGi], [Btb])
                          A(lambda: nc.scalar.copy(R(Vb.t[lo:hi, :, lo:hi]), v3(v.t[lo:hi, :])), [v], [Vb])
                      V(lambda: nc.vector.tensor_tensor(R(KR.t[:, :, 128:192]), v3(r.t[:, :]), v3(EG.t[:, :]), ALU.mult), [r, EG], [KR])

                      _stage(3)
                      Hb = Hbufs[j]
                      Hj = Hst.t[:, j, :]
                      Y = W["Y"]

                      def phaseA(c):
                          sl = c % NSLOT
                          tm, akb, mt = TM[sl], AKB[sl], MT[sl]
                          S = SXW[c % NCHAIN]
                          odd = (c % 2 == 1)

                          def cp(dst, src_ps, rds, wrs, prefer_act=True):
                              if prefer_act:
                                  A(lambda: nc.scalar.copy(R(dst), src_ps), rds, wrs)
                              else:
                                  V(lambda: nc.vector.tensor_copy(R(dst), src_ps), rds, wrs)

                          pT = pss()
                          for i, src in enumerate((Ktb, Btb, Vb)):
                              kb.op("pe", lambda: nc.tensor.transpose(pT.t[:, i * 128:(i + 1) * 128], src.t[:, c, :], ident),
                                    [src, cst], [pT], inc=(i == 2))
                          A(lambda: nc.scalar.copy(R(tm.t[:, :]), pT.t[:, 0:384]), [pT], [tm])
                          yield
                          pA = pss()
                          MM(pA.t[:, 0:192], R(Ktb.t[:, c, :]), R(KR.t[:, c, :]), [Ktb, KR], [pA], inc=False)
                          MM(pA.t[:, 192:384], R(Btb.t[:, c, :]), R(KR.t[:, c, :]), [Btb, KR], [pA], inc=False)
                          MM(pA.t[:, 384:512], R(KR.t[:, c, 0:128]), R(Btb.t[:, c, :]), [KR, Btb], [pA])
                          V(lambda: nc.vector.tensor_tensor(R(akb.t[:, :]), pA.t[:, :], maskAll, ALU.mult), [pA, cst], [akb])
                          yield
                          X0, Y0 = akb.t[:, 192:320], akb.t[:, 384:512]
                          s0 = S[0]
                          V(lambda: nc.vector.tensor_tensor(R(s0.t[:, 128:256]), ident, X0, ALU.add), [cst, akb], [s0])
                          pX = pss()
                          MM(pX.t[:, 0:128], R(Y0), R(X0), [akb], [pX])
                          cp(s0.t[:, 0:128], pX.t[:, 0:128], [pX], [s0])
                          yield
                          pY = pss()
                          MM(pY.t[:, 0:128], R(X0), R(Y0), [akb], [pY])
                          cp(s0.t[:, 256:384], pY.t[:, 0:128], [pY], [s0], prefer_act=not odd)
                          yield
                          cur = 0
                          for lev in range(1, 5):
                              sc, sn = S[cur], S[1 - cur]
                              pXW = pss()
                              MM(pXW.t[:, 0:256], R(sc.t[:, 256:384]), R(sc.t[:, 0:256]), [sc], [pXW])
                              if lev < 4:
                                  cp(sn.t[:, 0:128], pXW.t[:, 0:128], [pXW], [sn])
                              V(lambda: nc.vector.tensor_tensor(R(sn.t[:, 128:256]), pXW.t[:, 128:256], sc.t[:, 128:256], ALU.add), [pXW, sc], [sn])
                              yield
                              pY = pss()
                              MM(pY.t[:, 0:128], R(sc.t[:, 0:128]), R(sc.t[:, 256:384]), [sc], [pY])
                              cp(sn.t[:, 256:384], pY.t[:, 0:128], [pY], [sn], prefer_act=not odd)
                              yield
                              cur = 1 - cur
                          sc = S[cur]
                          pW = pss()
                          MM(pW.t[:, 0:128], R(sc.t[:, 256:384]), R(sc.t[:, 128:256]), [sc], [pW])
                          V(lambda: nc.vector.tensor_tensor(R(mt.t[:, :]), pW.t[:, 0:128], sc.t[:, 128:256], ALU.add), [pW, sc], [mt])
                          yield

                      def seqc(c):
                          sl = c % NSLOT
                          tm, akb, mt = TM[sl], AKB[sl], MT[sl]
                          pR = pss()
                          MM(pR.t[:, 0:128], R(KR.t[:, c, 0:128]), R(Hj), [KR, Hb], [pR], start=True, stop=False, inc=False)
                          MM(pR.t[:, 0:128], R(akb.t[:, 0:128]), R(tm.t[:, 256:384]), [akb, tm], [pR], start=False, stop=True)
                          A(lambda: nc.scalar.copy(R(RH.t[:, :]), pR.t[:, 0:128]), [pR], [RH])
                          yield
                          pU = pss()
                          MM(pU.t[:, 0:128], R(mt.t[:, :]), R(RH.t[:, :]), [mt, RH], [pU])
                          V(lambda: nc.vector.tensor_scalar(R(nU.t[:, :]), pU.t[:, 0:128], -1.0, None, ALU.mult), [pU], [nU])
                          yield
                          pH = pss()
                          MM(pH.t[:, 0:128], identr, R(Hj), [cstr, Hb], [pH], start=True, stop=False, inc=False)
                          MM(pH.t[:, 0:128], R(tm.t[:, 0:128]), R(tm.t[:, 256:384]), [tm], [pH], start=False, stop=False, inc=False)
                          MM(pH.t[:, 0:128], R(tm.t[:, 128:256]), R(nU.t[:, :]), [tm, nU], [pH], start=False, stop=True, inc=False)
                          MM(pH.t[:, 128:192], R(Hj), R(KR.t[:, c, 128:192]), [Hb, KR], [pH], start=True, stop=False, inc=False)
                          MM(pH.t[:, 128:192], R(tm.t[:, 256:384]), R(akb.t[:, 128:192]), [tm, akb], [pH], start=False, stop=False, inc=False)
                          MM(pH.t[:, 128:192], R(nU.t[:, :]), R(akb.t[:, 320:384]), [nU, akb], [pH], start=False, stop=True)
                          A(lambda: nc.scalar.activation(out=R(Hj), in_=pH.t[:, 0:128], func=AF.Identity, scale=EG.t[:, c * 64 + 63:c * 64 + 64]), [pH, EG], [Hb])
                          A(lambda: nc.scalar.copy(R(Y.t[:, c * 64:(c + 1) * 64]), pH.t[:, 128:192]), [pH], [Y])
                          yield

                      if j < 7:
                          filler = g_inproj([j + 1, 9 + j, 17 + j, 26 + j], BIG)
                      else:
                          filler = g_inproj([33, 41, 49, 57], BIG)
                      def g_mix_next(jn):
                          for bank, cc, dst in ((BIG[0], jn, r), (BIG[1], 8 + jn, k), (BIG[2], 16 + jn, v)):
                              mix_shift(bank, cc, dst)
                              yield

                      filler2 = g_mix_next(j + 1) if j < 7 else None
                      active = []
                      nextA, doneA, doneSeq, seq_run = 0, set(), -1, False
                      it = 0
                      while doneSeq < NCH - 1:
                          while nextA < NCH and sum(1 for a_ in active if a_[0] == "A") < NCHAIN and nextA <= doneSeq + NSLOT:
                              active.append(("A", nextA, phaseA(nextA)))
                              nextA += 1
                          if not seq_run and (doneSeq + 1) in doneA:
                              active.append(("S", doneSeq + 1, seqc(doneSeq + 1)))
                              seq_run = True
                          for item in list(active):
                              try:
                                  next(item[2])
                              except StopIteration:
                                  active.remove(item)
                                  if item[0] == "A":
                                      doneA.add(item[1])
                                  else:
                                      doneSeq = item[1]
                                      seq_run = False
                          it += 1
                          if filler is not None and it % 3 == 0:
                              try:
                                  next(filler)
                              except StopIteration:
                                  filler = None
                          elif filler is None and filler2 is not None and it % 4 == 0:
                              try:
                                  next(filler2)
                              except StopIteration:
                                  filler2 = None
                      if filler is not None:
                          for _ in filler:
                              pass
                      if filler2 is not None:
                          for _ in filler2:
                              pass
                      _stage(4)
                      pm_ = pss()
                      MM(pm_.t[:, :], meanblkr, R(Y.t[:, :]), [cstr, Y], [pm_])
                      yc = W["yc"]
                      V(lambda: nc.vector.tensor_tensor(yc.t[:, :], Y.t[:, :], pm_.t[:, :], ALU.subtract), [Y, pm_], [yc])
                      A(lambda: nc.scalar.activation(out=R(sq.t[:, :]), in_=yc.t[:, :], func=AF.Square), [yc], [sq])
                      pv_ = pss()
                      MM(pv_.t[:, :], meanblkr, R(sq.t[:, :]), [cstr, sq], [pv_])
                      V(lambda: nc.vector.tensor_scalar(lnv.t[:, :], pv_.t[:, :], GN_EPS, None, ALU.add), [pv_], [lnv])
                      A(lambda: nc.scalar.activation(out=lnv.t[:, :], in_=lnv.t[:, :], func=AF.Ln), [lnv], [lnv])
                      A(lambda: nc.scalar.activation(out=rn.t[:, :], in_=lnv.t[:, :], func=AF.Exp, scale=-0.5), [lnv], [rn])
                      t2, t3 = W["t2"], W["t3"]
                      G(lambda: nc.gpsimd.tensor_tensor(t2.t[:, :], yc.t[:, :], rn.t[:, :], ALU.mult), [yc, rn], [t2])
                      G(lambda: nc.gpsimd.tensor_scalar(t3.t[:, :], t2.t[:, :], P(65 + j, 66 + j), P(73 + j, 74 + j), ALU.mult, ALU.add), [t2, prm], [t3])
                      G(lambda: nc.gpsimd.tensor_tensor(t2.t[:, :], t3.t[:, :], bonus.t[:, :], ALU.add), [t3, bonus], [t2])
                      V(lambda: nc.vector.tensor_tensor(ya_in.t[:, j, :], t2.t[:, :], sz.t[:, :], ALU.mult), [t2, sz], [yas[j]])

                  _stage(5)
                  unit = [0]

                  def bankset():
                      s_ = PS[0:4] if unit[0] % 2 == 0 else PS[4:8]
                      unit[0] += 1
                      return s_

                  for j in range(8):
                      bs = bankset()
                      if j > 0:
                          inproj_now([33 + j, 41 + j, 49 + j, 57 + j], bs)
                      pbg, pcg, phb, pzb = bs
                      t1, t2, t3 = W["t1"], W["t2"], W["t3"]
                      A(lambda: nc.scalar.copy(t1.t[:, :], pcg.t[:, :]), [pcg], [t1])
                      V(lambda: nc.vector.tensor_tensor(uext.t[:, 2:TT + 2], phb.t[:, :], t1.t[:, :], ALU.mult), [phb, t1], [uext])
                      V(lambda: nc.vector.tensor_copy(uext.t[:, 0:2], cbnd.t[:, j, :]), [cbnd], [uext])
                      G(lambda: nc.gpsimd.tensor_scalar(t2.t[:, :], uext.t[:, 0:TT], P(81 + j, 82 + j), 0.0, ALU.mult, ALU.add), [uext, prm], [t2])
                      V(lambda: nc.vector.scalar_tensor_tensor(t3.t[:, :], uext.t[:, 1:TT + 1], P(89 + j, 90 + j), t2.t[:, :], ALU.mult, ALU.add), [uext, prm, t2], [t3])
                      V(lambda: nc.vector.scalar_tensor_tensor(t2.t[:, :], uext.t[:, 2:TT + 2], P(97 + j, 98 + j), t3.t[:, :], ALU.mult, ALU.add), [uext, prm, t3], [t2])
                      V(lambda: nc.vector.tensor_copy(cbnd.t[:, j, :], uext.t[:, TT:TT + 2]), [uext], [cbnd])
                      V(lambda: nc.vector.tensor_tensor(t3.t[:, :], pbg.t[:, :], t2.t[:, :], ALU.mult), [pbg, t2], [t3])
                      A(lambda: nc.scalar.activation(out=t1.t[:, :], in_=pzb.t[:, :], func=AF.Sigmoid), [pzb], [t1])
                      V(lambda: nc.vector.tensor_tensor(t2.t[:, :], pzb.t[:, :], t1.t[:, :], ALU.mult), [pzb, t1], [t2])
                      G(lambda: nc.gpsimd.tensor_tensor(yb_in.t[:, j, :], t3.t[:, :], t2.t[:, :], ALU.mult), [t3, t2], [ybs[j]])

                  _stage(6)
                  for m in range(16):
                      pab, pbb = pbufs[(2 * m) % 3], pbufs[(2 * m + 1) % 3]
                      kb.dma("sp", ("pp", (2 * m) % 3), pab.t[:, :, :], pa_b[l, m].rearrange("p (a b) -> p a b", b=128), reads=[conv_ev[l]["pa"]], writes=[pab])
                      kb.dma("sp", ("pp", (2 * m + 1) % 3), pbb.t[:, :, :], pb_b[l, m].rearrange("p (a b) -> p a b", b=128), reads=[conv_ev[l]["pb"]], writes=[pbb])
                      pya, pyb, pga, pgb = bankset()
                      for kc in range(8):
                          MM(pya.t[:, :], pab.t[:, kc, :], ya_in.t[:, kc, :], [pab, yas[kc]], [pya], start=(kc == 0), stop=(kc == 7), inc=(kc == 7))
                      for kc in range(8):
                          MM(pyb.t[:, :], pbb.t[:, kc, :], yb_in.t[:, kc, :], [pbb, ybs[kc]], [pyb], start=(kc == 0), stop=(kc == 7), inc=(kc == 7))
                      inproj_now([65 + m, 81 + m], [pga, pgb])
                      ta, tb = exts[0], W["t1"]
                      A(lambda: nc.scalar.activation(out=ta.t[:, 0:TT], in_=pga.t[:, :], func=AF.Sigmoid), [pga], [ta])
                      V(lambda: nc.vector.tensor_tensor(ta.t[:, 0:TT], pya.t[:, :], ta.t[:, 0:TT], ALU.mult), [pya, ta], [ta])
                      A(lambda: nc.scalar.activation(out=tb.t[:, 0:TT], in_=pgb.t[:, :], func=AF.Sigmoid), [pgb], [tb])
                      V(lambda: nc.vector.tensor_tensor(tb.t[:, 0:TT], pyb.t[:, :], tb.t[:, 0:TT], ALU.mult), [pyb, tb], [tb])
                      G(lambda: nc.gpsimd.tensor_tensor(mix.t[:, m, :], ta.t[:, 0:TT], tb.t[:, 0:TT], ALU.add), [ta, tb], [mixs[m]])

                  _stage(7)
                  for half in range(2):
                      posb = [PS[0:4], PS[4:8]]
                      for g in range(8):
                          wo = wobufs[g % 2]
                          kb.dma("sp", ("wo", g % 2), wo.t[:, :, :], wo_b[l, g].rearrange("p (a b) -> p a b", b=256), reads=[conv_ev[l]["wo"]], writes=[wo])
                          c0_ = (g % 2) * 256
                          for sl_ in range(2):
                              s = half * 2 + sl_
                              po = posb[sl_][g // 2]
                              for dc in range(DC):
                                  MM(po.t[:, c0_:c0_ + 256], mix.t[:, dc, s * 128:(s + 1) * 128], wo.t[:, dc, :], [mixs[dc], wo], [po], start=(dc == 0), stop=(dc == DC - 1), inc=(dc == DC - 1))
                      for sl_ in range(2):
                          s = half * 2 + sl_
                          pos = posb[sl_]
                          xr = xo[1]
                          kb.dma("sp", ("x", 1), xr.t[:, :], x_src[t0 + s * 128:t0 + (s + 1) * 128, :], reads=[xsrc_buf], writes=[xr])
                          for q in range(4):
                              A(lambda: nc.scalar.activation(out=ya_in.t[:, 0, :], in_=pos[q].t[:, :], func=AF.Square, accum_out=ss4.t[:, 4 + q:5 + q]), [pos[q]], [yas[0], ss4])
                          V(lambda: nc.vector.tensor_reduce(rs4.t[:, 4:5], ss4.t[:, 4:8], mybir.AxisListType.X, ALU.add), [ss4], [rs4])
                          V(lambda: nc.vector.tensor_scalar(rs4.t[:, 4:5], rs4.t[:, 4:5], 1.0 / D, RMS_EPS, ALU.mult, ALU.add), [rs4], [rs4])
                          A(lambda: nc.scalar.activation(out=rs4.t[:, 4:5], in_=rs4.t[:, 4:5], func=AF.Ln), [rs4], [rs4])
                          A(lambda: nc.scalar.activation(out=rs4.t[:, 4:5], in_=rs4.t[:, 4:5], func=AF.Exp, scale=-0.5), [rs4], [rs4])
                          xn = xo[0]
                          for q in range(4):
                              V(lambda: nc.vector.scalar_tensor_tensor(xn.t[:, q * 512:(q + 1) * 512], pos[q].t[:, :], rs4.t[:, 4:5], GP.t[:, q * 512:(q + 1) * 512], ALU.mult, ALU.mult),
                                [pos[q], rs4, GP], [xn])
                          G(lambda: nc.gpsimd.tensor_tensor(xn.t[:, :], xn.t[:, :], xr.t[:, :], ALU.add), [xn, xr], [xn])
                          ob = Buf("orow")
                          kb.dma("pool", "st", out_d[t0 + s * 128:t0 + (s + 1) * 128, :], xn.t[:, :], reads=[xn], writes=[ob])
              kb.barrier()
          except _Stop:
            break
        kb.barrier()
        print("kernel built: ops=%d waits=%d" % (kb.nops, kb.nwait))
    return nc


def _consts():
    cst = np.zeros((128, NCST), np.float32)
    cst[:, 0:128] = np.eye(128, dtype=np.float32)
    blk = np.zeros((128, 128), np.float32)
    blk[0:64, 0:64] = 1.0
    blk[64:128, 64:128] = 1.0
    cst[:, 128:256] = blk
    cst[:, 256:384] = blk / 64.0
    s = np.arange(64)[:, None]
    t = np.arange(64)[None, :]
    su = (s < t).astype(np.float32)
    iu = (s <= t).astype(np.float32)
    mA = np.zeros((128, 192), np.float32)
    mA[0:64, 0:64] = su
    mA[64:128, 64:128] = su
    mA[0:64, 128:192] = iu
    mA[64:128, 128:192] = iu
    mQ = np.zeros((128, 128), np.float32)
    mQ[0:64, 0:64] = su.T
    mQ[64:128, 64:128] = su.T
    mB = mA.copy()
    mB[:, 0:128] *= -1.0
    cst[:, 384:576] = mA
    cst[:, 576:768] = mB
    cst[:, 768:896] = -mQ
    rm = np.ones((128, TT), np.float32)
    rm[:, ::CH] = 0.0
    cst[:, 896:1408] = rm
    cst[:, 1408:1536] = 1.0
    return cst


def prep_shared(inp, L):
    f = lambda a: np.ascontiguousarray(a, dtype=np.float32)
    sh = {}
    def relay(w, colblk):
        Lh, K, N = w.shape
        return f(w.reshape(Lh, K // 128, 128, N // colblk, colblk).transpose(0, 3, 2, 1, 4).reshape(Lh, N // colblk, 128, (K // 128) * colblk))
    sh["ada_r"] = relay(inp["ada_w"][:L], 128)
    sh["win_r"] = relay(inp["w_in"][:L], 128)
    sh["pa_r"] = relay(inp["p_a"][:L], 128)
    sh["pb_r"] = relay(inp["p_b"][:L], 128)
    sh["wo_r"] = relay(inp["w_out"][:L], 256)
    fm = lambda v, n: v.reshape(L, n, 128).transpose(0, 2, 1)
    prm = np.zeros((L, 128, NPRM), np.float32)
    prm[:, :, 0:25] = fm(inp["mu_shift"][:L], 25)
    prm[:, :, 25:33] = fm(inp["w0"][:L], 8)
    prm[:, :, 33:41] = fm(inp["a0"][:L], 8)
    prm[:, :, 41:49] = fm(inp["k_k"][:L], 8)
    prm[:, :, 49:57] = fm(inp["k_a"][:L], 8)
    prm[:, :, 57:65] = fm(inp["r_k"][:L].reshape(L, 1024), 8)
    prm[:, :, 65:73] = fm(inp["lnx_gain"][:L], 8)
    prm[:, :, 73:81] = fm(inp["lnx_bias"][:L], 8)
    cw = inp["conv_w"][:L]
    prm[:, :, 81:89] = fm(cw[:, 0], 8)
    prm[:, :, 89:97] = fm(cw[:, 1], 8)
    prm[:, :, 97:105] = fm(cw[:, 2], 8)
    prm[:, :, 105:121] = fm(inp["pre_gain"][:L], 16)
    prm[:, :, 121:137] = fm(inp["post_gain"][:L], 16)
    prm[:, :, 137:185] = fm(inp["ada_b"][:L], 48)
    sh["prm"] = f(prm.transpose(1, 0, 2))
    lora = np.zeros((L, 128, 8, 256), np.float32)
    w2 = inp["w2"][:L].reshape(L, 64, 8, 128)
    a2 = inp["a2"][:L].reshape(L, 64, 8, 128)
    lora[:, 0:64, :, 0:128] = w2
    lora[:, 64:128, :, 128:256] = a2
    sh["lora"] = f(lora.reshape(L, 128, 8 * 256))
    sh["cst"] = _consts()
    return sh


_CACHE = {}


def run(inp, T, L, nb):
    key = (T, L)
    if key not in _CACHE:
        _CACHE[key] = build_program(T, L)
    nc = _CACHE[key]
    sh = prep_shared(inp, L)
    in_maps = []
    zero = None
    slots = [0, 2, 4, 6, 1, 3, 5, 7]
    owner = {}
    for i in range(min(nb, 8)):
        owner[slots[i]] = i
    for core in range(8):
        if core in owner:
            b = owner[core]
            m = dict(sh)
            m["x"] = np.ascontiguousarray(inp["x"][b, :T], dtype=np.float32)
            m["cfm"] = np.ascontiguousarray(inp["c"][b].reshape(DC, 128).T, dtype=np.float32)
        else:
            if zero is None:
                zero = {k: np.zeros_like(v) for k, v in sh.items()}
                zero["x"] = np.zeros((T, D), np.float32)
                zero["cfm"] = np.zeros((128, DC), np.float32)
            m = zero
        in_maps.append(m)
    res = run_bass_kernel_spmd(nc, in_maps, core_ids=list(range(8)))
    inv = {b: c for c, b in owner.items()}
    return np.stack([res.results[inv[b]]["out"] for b in range(nb)], axis=0)


def kernel(**inputs):
    inp = {k: np.asarray(v) for k, v in inputs.items()}
    B, T, _ = inp["x"].shape
    L = inp["w_in"].shape[0]
    out = run(inp, T, L, B)
    return out.astype(np.float32)
```

```python
import contextlib
import os
import numpy as np
import concourse.bass as bass
import concourse.mybir as mybir
from concourse.bass_utils import run_bass_kernel_spmd

F32, BF16 = mybir.dt.float32, mybir.dt.bfloat16
F32R = mybir.dt.float32r
AF = mybir.ActivationFunctionType
ALU = mybir.AluOpType

D = 2048
DC = 16
DA = 1024
NCC = 97
TT = 512
NSUB = 4
CH = 64
NCH = TT // CH
NPRM = 185
NCST = 1536
C0 = float(np.exp(-0.5))
RMS_EPS = 1e-6
GN_EPS = 64e-5
SEM_LIMIT = 30000


class _Stop(Exception):
    pass


STAGE = float(os.environ.get("KSTAGE", "99"))


def _stage(n):
    if STAGE <= n:
        raise _Stop()


class Buf:
    __slots__ = ("name", "w", "r", "t", "ps")

    def __init__(self, name, t=None, ps=False):
        self.name = name
        self.ps = ps
        self.w = None
        self.r = []
        self.t = t


class KB:
    def __init__(self, nc, es):
        self.nc = nc
        self.es = es
        self.eng = {"pe": nc.tensor, "act": nc.scalar, "dve": nc.vector, "pool": nc.gpsimd, "sp": nc.sync}
        self.cnt = {e: 0 for e in self.eng}
        self.gen = {e: 0 for e in self.eng}
        self.sem = {e: es.enter_context(nc.semaphore("c_" + e + "0")) for e in self.eng}
        self.waited = {e: {} for e in self.eng}
        self.dsem = {}
        self.nwait = 0
        self.nops = 0

    def _wait(self, e, ev):
        key, sem, val = ev
        if self.waited[e].get(key, 0) >= val:
            return
        self.eng[e].wait_ge(sem, val)
        self.waited[e][key] = val
        self.nwait += 1

    def _deps(self, e, reads, writes, skip_key=None):
        best = {}
        for b in reads:
            if b.w is not None:
                ev = b.w
                if ev[0] not in best or best[ev[0]][2] < ev[2]:
                    best[ev[0]] = ev
        for b in writes:
            evs = list(b.r)
            if b.w is not None:
                evs.append(b.w)
            for ev in evs:
                if ev[0] not in best or best[ev[0]][2] < ev[2]:
                    best[ev[0]] = ev
        for ev in best.values():
            if e == "pe" and ev[0][0] == "eng" and ev[0][1] == "pe":
                continue
            if skip_key is not None and ev[0] == skip_key:
                continue
            self._wait(e, ev)

    def _record(self, myev, reads, writes):
        for b in reads:
            b.r.append(myev)
            if len(b.r) > 64:
                best = {}
                for ev in b.r:
                    if ev[0] not in best or best[ev[0]][2] < ev[2]:
                        best[ev[0]] = ev
                b.r = list(best.values())
        for b in writes:
            b.w = myev
            b.r = []

    def op(self, e, fn, reads=(), writes=(), inc=True):
        if self.cnt[e] >= SEM_LIMIT:
            self.gen[e] += 1
            self.cnt[e] = 0
            self.sem[e] = self.es.enter_context(self.nc.semaphore("c_%s%d" % (e, self.gen[e])))
        psr = [b for b in reads if b.ps]
        if psr:
            reads = [b for b in reads if not b.ps]
            writes = list(writes) + psr
        self._deps(e, reads, writes)
        inst = fn()
        self.nops += 1
        key = ("eng", e, self.gen[e])
        if inc:
            self.cnt[e] += 1
            inst.then_inc(self.sem[e], 1)
            myev = (key, self.sem[e], self.cnt[e])
        else:
            myev = (key, self.sem[e], self.cnt[e] + 1)
        self._record(myev, reads, writes)
        return inst

    def dma(self, q, slot, out, in_, reads=(), writes=(), multi=None, **kw):
        if slot not in self.dsem:
            self.dsem[slot] = [self.es.enter_context(self.nc.semaphore("d%d" % len(self.dsem))), 0]
        sem, n = self.dsem[slot]
        key = ("dma", slot)
        self._deps(q, reads, writes, skip_key=(key if multi is not None else None))
        if multi is None and n > 0:
            self._wait(q, (key, sem, 16 * n))
        inst = self.eng[q].dma_start(out=out, in_=in_, **kw)
        inst.then_inc(sem, 16)
        self.dsem[slot][1] = n + 1
        val = 16 * (n + 1) if multi is None else 16 * multi
        myev = (key, sem, val)
        self._record(myev, reads, writes)
        self.nops += 1

    def barrier(self):
        for e in self.eng:
            for e2 in self.eng:
                if e2 == e or self.cnt[e2] == 0:
                    continue
                self._wait(e, (("eng", e2, self.gen[e2]), self.sem[e2], self.cnt[e2]))
            for slot, (sem, n) in self.dsem.items():
                if n > 0:
                    self._wait(e, (("dma", slot), sem, 16 * n))


def build_program(T, L, taps=False):
    NT = T // TT
    nc = bass.Bass("TRN2", target_bir_lowering=False)
    dt_in = lambda name, shape: nc.dram_tensor(name, shape, F32, kind="ExternalInput").ap()
    x_d = dt_in("x", [T, D])
    cfm_d = dt_in("cfm", [128, DC])
    ada_d = dt_in("ada_r", [L, 48, 128, DC * 128])
    win_d = dt_in("win_r", [L, NCC, 128, DC * 128])
    pa_d = dt_in("pa_r", [L, 16, 128, 8 * 128])
    pb_d = dt_in("pb_r", [L, 16, 128, 8 * 128])
    wo_d = dt_in("wo_r", [L, 8, 128, DC * 256])
    prm_d = dt_in("prm", [128, L, NPRM])
    lora_d = dt_in("lora", [L, 128, 8 * 256])
    cst_d = dt_in("cst", [128, NCST])
    out_d = nc.dram_tensor("out", [T, D], F32, kind="ExternalOutput").ap()
    win_b = nc.dram_tensor("win_b", [L, NCC, 128, DC * 128], BF16, kind="Internal").ap()
    pa_b = nc.dram_tensor("pa_b", [L, 16, 128, 8 * 128], BF16, kind="Internal").ap()
    pb_b = nc.dram_tensor("pb_b", [L, 16, 128, 8 * 128], BF16, kind="Internal").ap()
    wo_b = nc.dram_tensor("wo_b", [L, 8, 128, DC * 256], BF16, kind="Internal").ap()

    es = contextlib.ExitStack()
    with es:
        kb = KB(nc, es)
        _n = [0]

        def sb(shape, dt=F32, name=None):
            _n[0] += 1
            nm = "%s_%d" % (name or "t", _n[0])
            t = es.enter_context(nc.sbuf_tensor(nm, list(shape), dt))
            return Buf(nm, t)

        PS = []
        for i in range(8):
            t = es.enter_context(nc.psum_tensor("ps%d" % i, [128, 512], F32))
            PS.append(Buf("ps%d" % i, t, ps=True))
        _pi = [0]

        def ps():
            b = PS[_pi[0] % 8]
            _pi[0] += 1
            return b

        V = lambda fn, r, w: kb.op("dve", fn, r, w)
        A = lambda fn, r, w: kb.op("act", fn, r, w)
        _alt = [0]

        def AV(fa, fv, r, w):
            _alt[0] += 1
            kav = os.environ.get("KAV", "")
            if kav == "act" or (kav != "dve" and _alt[0] % 2):
                return kb.op("act", fa, r, w)
            return kb.op("dve", fv, r, w)

        def MM(out_ap, lhsT, rhs, r, w, start=True, stop=True, inc=True):
            return kb.op("pe", lambda: nc.tensor.matmul(out_ap, lhsT, rhs, start=start, stop=stop), r, w, inc=inc)

        conv_ev = {}

        def issue_conv(l):
            bufs = {k: Buf("cv_%s_%d" % (k, l)) for k in ("win", "pa", "pb", "wo")}
            conv_ev[l] = bufs
            tot = NCC + 16 + 16 + 8
            slot = ("conv", l)
            for cc in range(NCC):
                kb.dma("pool", slot, win_b[l, cc], win_d[l, cc], writes=[bufs["win"]], multi=tot, max_dma_last_dim=4096)
            for m in range(16):
                kb.dma("pool", slot, pa_b[l, m], pa_d[l, m], writes=[bufs["pa"]], multi=tot, max_dma_last_dim=4096)
                kb.dma("pool", slot, pb_b[l, m], pb_d[l, m], writes=[bufs["pb"]], multi=tot, max_dma_last_dim=4096)
            for g in range(8):
                kb.dma("pool", slot, wo_b[l, g], wo_d[l, g], writes=[bufs["wo"]], multi=tot, max_dma_last_dim=4096)

        issue_conv(0)

        cst = sb([128, NCST], F32, "cst")
        kb.dma("sp", "cst", cst.t[:, :], cst_d, writes=[cst])
        ident = cst.t[:, 0:128]
        onesblk = cst.t[:, 128:256]
        meanblk = cst.t[:, 256:384]
        maskAll = cst.t[:, 384:896]
        rmask = cst.t[:, 896:1408]
        ones = cst.t[:, 1408:1536]
        zcol = cst.t[:, 384:385]
        prm = sb([128, L, NPRM], F32, "prm")
        kb.dma("sp", "prm", prm.t[:, :, :], prm_d, writes=[prm])
        der = sb([128, L, 33], F32, "der")
        for l in range(L):
            V(lambda l=l: nc.vector.tensor_scalar(der.t[:, l, 0:25], prm.t[:, l, 0:25], -1.0, 1.0, ALU.mult, ALU.add), [prm], [der])
            V(lambda l=l: nc.vector.tensor_scalar(der.t[:, l, 25:33], prm.t[:, l, 49:57], -1.0, 1.0, ALU.mult, ALU.add), [prm], [der])
        cf = sb([128, DC], F32, "cf")
        kb.dma("sp", "cf", cf.t[:, :], cfm_d, writes=[cf])
        sgc = sb([128, DC], F32, "sgc")
        sc = sb([128, DC], F32, "sc")
        A(lambda: nc.scalar.activation(out=sgc.t[:, :], in_=cf.t[:, :], func=AF.Sigmoid), [cf], [sgc])
        V(lambda: nc.vector.tensor_tensor(sc.t[:, :], cf.t[:, :], sgc.t[:, :], ALU.mult), [cf, sgc], [sc])

        modfm = sb([128, L, 48], F32, "modfm")
        g1 = sb([128, L, DC], F32, "g1")
        gpfm = sb([128, L, DC], F32, "gpfm")
        with contextlib.ExitStack() as es2:
            adab = []
            for i in range(3):
                t = es2.enter_context(nc.sbuf_tensor("adab%d" % i, [128, DC, 128], F32))
                adab.append(Buf("adab%d" % i, t))
            k = 0
            for l in range(L):
                pm = ps()
                for cc in range(48):
                    ab = adab[k % 3]
                    kb.dma("sp", ("ada", k % 3), ab.t[:, :, :], ada_d[l, cc].rearrange("p (a b) -> p a b", b=128), writes=[ab])
                    k += 1
                    for dc in range(DC):
                        MM(pm.t[:, cc:cc + 1], ab.t[:, dc, :], sc.t[:, dc:dc + 1], [ab, sc], [pm],
                           start=(dc == 0), stop=(dc == DC - 1), inc=(dc == DC - 1))
                V(lambda l=l, pm=pm: nc.vector.tensor_tensor(modfm.t[:, l, :], pm.t[:, 0:48], prm.t[:, l, 137:185], ALU.add), [pm, prm], [modfm])
                V(lambda l=l: nc.vector.scalar_tensor_tensor(g1.t[:, l, :], modfm.t[:, l, 16:32], 1.0, prm.t[:, l, 105:121], ALU.add, ALU.mult), [modfm, prm], [g1])
                V(lambda l=l: nc.vector.tensor_tensor(gpfm.t[:, l, :], modfm.t[:, l, 32:48], prm.t[:, l, 121:137], ALU.mult), [modfm, prm], [gpfm])
            kb.barrier()

        R = lambda ap: ap.bitcast(F32R)
        cstr = sb([128, 384], F32, "cstr")
        V(lambda: nc.vector.tensor_copy(R(cstr.t[:, :]), cst.t[:, 0:384]), [cst], [cstr])
        identr, onesblkr, meanblkr = R(cstr.t[:, 0:128]), R(cstr.t[:, 128:256]), R(cstr.t[:, 256:384])
        hT = sb([128, DC, TT], BF16, "hT")
        hTs = [Buf("hT%d" % i) for i in range(DC)]
        NW = 3
        wbufs = [sb([128, DC, 128], BF16, "wb") for _ in range(NW)]
        pbufs = [sb([128, 8, 128], BF16, "pbuf") for _ in range(3)]
        wobufs = [sb([128, DC, 256], BF16, "wob") for _ in range(2)]
        ya_in = sb([128, 8, TT], BF16, "ya_in")
        yb_in = sb([128, 8, TT], BF16, "yb_in")
        mix = sb([128, DC, TT], BF16, "mix")
        yas = [Buf("ya%d" % i) for i in range(8)]
        ybs = [Buf("yb%d" % i) for i in range(8)]
        mixs = [Buf("mix%d" % i) for i in range(DC)]
        Hbufs = [Buf("H%d" % i) for i in range(8)]
        GP = sb([128, D], F32, "GP")
        lorab = [sb([128, 256], F32, "lora") for _ in range(1)]
        xo = [sb([128, D], F32, "xo") for _ in range(2)]
        ss4 = sb([128, 8], F32, "ss4")
        rs4 = sb([128, 8], F32, "rs4")
        bnd = sb([128, 25], F32, "bnd")
        cbnd = sb([128, 8, 2], F32, "cbnd")
        exts = [sb([128, TT + 2], F32, "ext") for _ in range(1)]
        uext = exts[0]
        Hst = sb([128, 8, 128], F32, "Hst")
        names = ["r", "k", "v", "sz", "tl", "sig", "csum", "t1", "EG", "EGi", "a", "kk",
                 "kkn", "kp", "bb", "bonus", "sqrk", "Y"]
        W = {n: sb([128, TT], F32, n) for n in names}
        for al, tgt in (("sq", "sqrk"), ("rk", "sqrk"), ("lnv", "sig"), ("rn", "csum"), ("cex", "t1"), ("EGp", "t1"), ("fac", "sig"), ("tmpm", "a"),
                        ("yc", "kk"), ("t2", "kkn"), ("t3", "bb")):
            W[al] = W[tgt]
        KR = sb([128, NCH, 192], F32, "KR")
        Ktb = sb([128, NCH, 128], F32, "Ktb")
        Btb = sb([128, NCH, 128], F32, "Btb")
        Vb = sb([128, NCH, 128], F32, "Vb")
        NSLOT = 4
        NCHAIN = 3
        TM = [sb([128, 384], F32, "TM") for _ in range(NSLOT)]
        AKB = [sb([128, 512], F32, "AKB") for _ in range(NSLOT)]
        MT = [sb([128, 128], F32, "MT") for _ in range(NSLOT)]
        SXW = [[sb([128, 384], F32, "SXW") for _ in range(2)] for _ in range(NCHAIN)]
        RH = sb([128, 128], F32, "RH")
        nU = sb([128, 128], F32, "nU")
        print("sbuf bytes remaining:", nc.sbuf_bytes_remaining)

        for b in (KR, Ktb, Btb, Vb):
            n_ = b.t.shape[1] * b.t.shape[2]
            V(lambda b=b, n_=n_: nc.vector.tensor_copy(R(b.t[:, :, :].rearrange("p a b -> p (a b)")), zcol.to_broadcast([128, n_])), [cst], [b])

        def v3(ap):
            return ap.rearrange("p (c t) -> p c t", t=CH)

        G = lambda fn, r, w: kb.op("pool", fn, r, w)
        _si = [0]

        def pss():
            b = PS[4 + _si[0] % 4]
            _si[0] += 1
            return b

        for l in range(L):
          try:
              _stage(0)
              P = lambda a, b2, l=l: prm.t[:, l, a:b2]
              x_src = x_d if l == 0 else out_d
              xsrc_buf = Buf("xsrc")
              V(lambda: nc.vector.memset(bnd.t[:, :], 0.0), [], [bnd])
              V(lambda: nc.vector.memset(cbnd.t[:, :, :], 0.0), [], [cbnd])
              V(lambda: nc.vector.tensor_copy(R(Hst.t[:, :, :].rearrange("p a b -> p (a b)")), zcol.to_broadcast([128, 1024])), [cst], Hbufs)
              if l + 1 < L:
                  issue_conv(l + 1)
              _stage(0.3)
              Rt = xo[0]
              for dc in range(DC):
                  V(lambda dc=dc, l=l: nc.vector.tensor_scalar(Rt.t[:, dc * 128:(dc + 1) * 128], ident, gpfm.t[:, l, dc:dc + 1], None, ALU.mult), [cst, gpfm], [Rt])
              for g in range(4):
                  pg = ps()
                  MM(pg.t[:, :], ones, Rt.t[:, g * 512:(g + 1) * 512], [cst, Rt], [pg])
                  A(lambda g=g, pg=pg: nc.scalar.copy(GP.t[:, g * 512:(g + 1) * 512], pg.t[:, :]), [pg], [GP])

              _stage(0.5)
              order = []
              for tt in range(NT):
                  seq_ = [24]
                  for j in range(8):
                      seq_ += [j, 8 + j, 16 + j, 25 + j]
                  for j in range(8):
                      seq_ += [33 + j, 41 + j, 49 + j, 57 + j]
                  for m in range(16):
                      seq_ += [65 + m, 81 + m]
                  order += seq_
              wstate = {"issued": 0, "used": 0}

              def wget(cc, l=l, order=order, wstate=wstate):
                  while wstate["issued"] < len(order) and wstate["issued"] < wstate["used"] + NW:
                      i = wstate["issued"]
                      wb = wbufs[i % NW]
                      kb.dma("sp", ("w", i % NW), wb.t[:, :, :], win_b[l, order[i]].rearrange("p (a b) -> p a b", b=128),
                             reads=[conv_ev[l]["win"]], writes=[wb])
                      wstate["issued"] += 1
                  i = wstate["used"]
                  assert order[i] == cc, (order[i], cc)
                  wstate["used"] += 1
                  return wbufs[i % NW]

              def g_inproj(ccs, banks):
                  for cc, p in zip(ccs, banks):
                      wb = wget(cc)
                      for dc in range(DC):
                          MM(p.t[:, :], wb.t[:, dc, :], hT.t[:, dc, :], [wb, hTs[dc]], [p], start=(dc == 0), stop=(dc == DC - 1), inc=(dc == DC - 1))
                          if dc % 4 == 3:
                              yield

              def inproj_now(ccs, banks):
                  for _ in g_inproj(ccs, banks):
                      pass

              _e = [0]

              def mix_shift(p, cc, outb, l=l, rr=False):
                  ext = exts[0]
                  _e[0] += 1
                  A(lambda: nc.scalar.copy(ext.t[:, 1:TT + 1], p.t[:, :]), [p], [ext])
                  V(lambda: nc.vector.tensor_copy(ext.t[:, 0:1], bnd.t[:, cc:cc + 1]), [bnd], [ext])
                  tm = W["tmpm"]
                  V(lambda: nc.vector.tensor_scalar(tm.t[:, :], ext.t[:, 0:TT], prm.t[:, l, cc:cc + 1], None, ALU.mult), [ext, prm], [tm])
                  oap = R(outb.t[:, :]) if rr else outb.t[:, :]
                  V(lambda: nc.vector.scalar_tensor_tensor(oap, p.t[:, :], der.t[:, l, cc:cc + 1], tm.t[:, :], ALU.mult, ALU.add), [p, der, tm], [outb])
                  V(lambda: nc.vector.tensor_copy(bnd.t[:, cc:cc + 1], ext.t[:, TT:TT + 1]), [ext], [bnd])

              for tt in range(NT):
                  t0 = tt * TT
                  junk4 = ya_in.t[:, 0:4, :]
                  for s in range(NSUB):
                      xb = xo[s % 2]
                      kb.dma("sp", ("x", s % 2), xb.t[:, :], x_src[t0 + s * 128:t0 + (s + 1) * 128, :], reads=[xsrc_buf], writes=[xb])
                      A(lambda: nc.scalar.activation(out=junk4, in_=xb.t[:, :].rearrange("p (a b) -> p a b", b=TT), func=AF.Square, accum_out=ss4.t[:, s:s + 1]), [xb], yas[0:4] + [ss4])
                      V(lambda: nc.vector.tensor_scalar(rs4.t[:, s:s + 1], ss4.t[:, s:s + 1], 1.0 / D, RMS_EPS, ALU.mult, ALU.add), [ss4], [rs4])
                      A(lambda: nc.scalar.activation(out=rs4.t[:, s:s + 1], in_=rs4.t[:, s:s + 1], func=AF.Ln), [rs4], [rs4])
                      A(lambda: nc.scalar.activation(out=rs4.t[:, s:s + 1], in_=rs4.t[:, s:s + 1], func=AF.Exp, scale=-0.5), [rs4], [rs4])
                      V(lambda: nc.vector.tensor_scalar(xb.t[:, :], xb.t[:, :], rs4.t[:, s:s + 1], None, ALU.mult), [xb, rs4], [xb])
                      for q in range(4):
                          p = ps()
                          for i in range(4):
                              dc = q * 4 + i
                              kb.op("pe", lambda: nc.tensor.transpose(p.t[:, i * 128:(i + 1) * 128], xb.t[:, dc * 128:(dc + 1) * 128], ident),
                                    [xb, cst], [p], inc=(i == 3))
                          for i in range(4):
                              dc = q * 4 + i
                              if q % 2 == 0:
                                  A(lambda: nc.scalar.activation(out=hT.t[:, dc, s * 128:(s + 1) * 128], in_=p.t[:, i * 128:(i + 1) * 128], func=AF.Identity, scale=g1.t[:, l, dc:dc + 1], bias=modfm.t[:, l, dc:dc + 1]),
                                    [p, g1, modfm], [hTs[dc]])
                              else:
                                  V(lambda: nc.vector.tensor_scalar(hT.t[:, dc, s * 128:(s + 1) * 128], p.t[:, i * 128:(i + 1) * 128], g1.t[:, l, dc:dc + 1], modfm.t[:, l, dc:dc + 1], ALU.mult, ALU.add),
                                    [p, g1, modfm], [hTs[dc]])

                  _stage(1)
                  BIG = PS[0:4]
                  inproj_now([24], [BIG[0]])
                  tl = W["tl"]
                  mix_shift(BIG[0], 24, tl)
                  A(lambda: nc.scalar.activation(out=tl.t[0:64, :], in_=tl.t[0:64, :], func=AF.Tanh), [tl], [tl])
                  inproj_now([0, 8, 16, 25], BIG)
                  _stage(2)
                  for j in range(8):
                      r, k, v = W["r"], W["k"], W["v"]
                      lora = lorab[0]
                      premixed = (j > 0)
                      kb.dma("sp", ("lora", 0), lora.t[:, :], lora_d[l][:, j * 256:(j + 1) * 256], writes=[lora])
                      if not premixed:
                          mix_shift(BIG[0], j, r)
                          mix_shift(BIG[1], 8 + j, k)
                          mix_shift(BIG[2], 16 + j, v)
                      p = BIG[3]
                      t1, sz = W["t1"], W["sz"]
                      A(lambda: nc.scalar.activation(out=t1.t[:, :], in_=p.t[:, :], func=AF.Sigmoid), [p], [t1])
                      V(lambda: nc.vector.tensor_tensor(sz.t[:, :], p.t[:, :], t1.t[:, :], ALU.mult), [p, t1], [sz])
                      pw = pss()
                      MM(pw.t[:, :], lora.t[:, 0:128], tl.t[:, :], [lora, tl], [pw])
                      sig = W["sig"]
                      A(lambda: nc.scalar.activation(out=sig.t[:, :], in_=pw.t[:, :], func=AF.Sigmoid, bias=P(25 + j, 26 + j)), [pw, prm], [sig])
                      pa_ = pss()
                      MM(pa_.t[:, :], lora.t[:, 128:256], tl.t[:, :], [lora, tl], [pa_])
                      a = W["a"]
                      A(lambda: nc.scalar.activation(out=a.t[:, :], in_=pa_.t[:, :], func=AF.Sigmoid, bias=P(33 + j, 34 + j)), [pa_, prm], [a])
                      csum, cex = W["csum"], W["cex"]
                      V(lambda: nc.vector.tensor_tensor_scan(csum.t[:, :], rmask, sig.t[:, :], 0.0, ALU.mult, ALU.add), [cst, sig], [csum])
                      G(lambda: nc.gpsimd.tensor_tensor(cex.t[:, :], csum.t[:, :], sig.t[:, :], ALU.subtract), [csum, sig], [cex])
                      EG, EGi, EGp = W["EG"], W["EGi"], W["EGp"]
                      A(lambda: nc.scalar.activation(out=EG.t[:, :], in_=csum.t[:, :], func=AF.Exp, scale=-C0), [csum], [EG])
                      A(lambda: nc.scalar.activation(out=EGi.t[:, :], in_=csum.t[:, :], func=AF.Exp, scale=C0), [csum], [EGi])
                      A(lambda: nc.scalar.activation(out=EGp.t[:, :], in_=cex.t[:, :], func=AF.Exp, scale=-C0), [cex], [EGp])
                      kk, sq = W["kk"], W["sq"]
                      V(lambda: nc.vector.tensor_scalar(kk.t[:, :], k.t[:, :], P(41 + j, 42 + j), None, ALU.mult), [k, prm], [kk])
                      A(lambda: nc.scalar.activation(out=R(sq.t[:, :]), in_=kk.t[:, :], func=AF.Square), [kk], [sq])
                      pss_ = pss()
                      MM(pss_.t[:, :], onesblkr, R(sq.t[:, :]), [cstr, sq], [pss_])
                      lnv, rn, kkn = W["lnv"], W["rn"], W["kkn"]
                      V(lambda: nc.vector.tensor_scalar(lnv.t[:, :], pss_.t[:, :], 1e-24, None, ALU.max), [pss_], [lnv])
                      A(lambda: nc.scalar.activation(out=lnv.t[:, :], in_=lnv.t[:, :], func=AF.Ln), [lnv], [lnv])
                      A(lambda: nc.scalar.activation(out=rn.t[:, :], in_=lnv.t[:, :], func=AF.Exp, scale=-0.5), [lnv], [rn])
                      G(lambda: nc.gpsimd.tensor_tensor(kkn.t[:, :], kk.t[:, :], rn.t[:, :], ALU.mult), [kk, rn], [kkn])
                      fac, kp, bb = W["fac"], W["kp"], W["bb"]
                      V(lambda: nc.vector.tensor_scalar(fac.t[:, :], a.t[:, :], P(49 + j, 50 + j), der.t[:, l, 25 + j:26 + j], ALU.mult, ALU.add), [a, prm, der], [fac])
                      G(lambda: nc.gpsimd.tensor_tensor(kp.t[:, :], k.t[:, :], fac.t[:, :], ALU.mult), [k, fac], [kp])
                      G(lambda: nc.gpsimd.tensor_tensor(bb.t[:, :], kkn.t[:, :], a.t[:, :], ALU.mult), [kkn, a], [bb])
                      rk, bonus = W["rk"], W["bonus"]
                      V(lambda: nc.vector.scalar_tensor_tensor(R(rk.t[:, :]), r.t[:, :], P(57 + j, 58 + j), kp.t[:, :], ALU.mult, ALU.mult), [r, prm, kp], [rk])
                      pbn = pss()
                      MM(pbn.t[:, :], onesblkr, R(rk.t[:, :]), [cstr, rk], [pbn])
                      V(lambda: nc.vector.tensor_tensor(bonus.t[:, :], pbn.t[:, :], v.t[:, :], ALU.mult), [pbn, v], [bonus])
                      for hd in range(2):
                          lo, hi = hd * 64, hd * 64 + 64
                          V(lambda: nc.vector.tensor_tensor(R(KR.t[lo:hi, :, lo:hi]), v3(kkn.t[lo:hi, :]), v3(EGp.t[lo:hi, :]), ALU.mult), [kkn, EGp], [KR])
                          V(lambda: nc.vector.tensor_tensor(R(Ktb.t[lo:hi, :, lo:hi]), v3(kp.t[lo:hi, :]), v3(EGi.t[lo:hi, :]), ALU.mult), [kp, EGi], [Ktb])
                          V(lambda: nc.vector.tensor_tensor(R(Btb.t[lo:hi, :, lo:hi]), v3(bb.t[lo:hi, :]), v3(EGi.t[lo:hi, :]), ALU.mult), [bb, EGi], [Btb])
                          A(lambda: nc.scalar.copy(R(Vb.t[lo:hi, :, lo:hi]), v3(v.t[lo:hi, :])), [v], [Vb])
                      V(lambda: nc.vector.tensor_tensor(R(KR.t[:, :, 128:192]), v3(r.t[:, :]), v3(EG.t[:, :]), ALU.mult), [r, EG], [KR])

                      _stage(3)
                      Hb = Hbufs[j]
                      Hj = Hst.t[:, j, :]
                      Y = W["Y"]

                      def phaseA(c):
                          sl = c % NSLOT
                          tm, akb, mt = TM[sl], AKB[sl], MT[sl]
                          S = SXW[c % NCHAIN]
                          odd = (c % 2 == 1)

                          def cp(dst, src_ps, rds, wrs, prefer_act=True):
                              if prefer_act:
                                  A(lambda: nc.scalar.copy(R(dst), src_ps), rds, wrs)
                              else:
                                  V(lambda: nc.vector.tensor_copy(R(dst), src_ps), rds, wrs)

                          pT = pss()
                          for i, src in enumerate((Ktb, Btb, Vb)):
                              kb.op("pe", lambda: nc.tensor.transpose(pT.t[:, i * 128:(i + 1) * 128], src.t[:, c, :], ident),
                                    [src, cst], [pT], inc=(i == 2))
                          A(lambda: nc.scalar.copy(R(tm.t[:, :]), pT.t[:, 0:384]), [pT], [tm])
                          yield
                          pA = pss()
                          MM(pA.t[:, 0:192], R(Ktb.t[:, c, :]), R(KR.t[:, c, :]), [Ktb, KR], [pA], inc=False)
                          MM(pA.t[:, 192:384], R(Btb.t[:, c, :]), R(KR.t[:, c, :]), [Btb, KR], [pA], inc=False)
                          MM(pA.t[:, 384:512], R(KR.t[:, c, 0:128]), R(Btb.t[:, c, :]), [KR, Btb], [pA])
                          V(lambda: nc.vector.tensor_tensor(R(akb.t[:, :]), pA.t[:, :], maskAll, ALU.mult), [pA, cst], [akb])
                          yield
                          X0, Y0 = akb.t[:, 192:320], akb.t[:, 384:512]
                          s0 = S[0]
                          V(lambda: nc.vector.tensor_tensor(R(s0.t[:, 128:256]), ident, X0, ALU.add), [cst, akb], [s0])
                          pX = pss()
                          MM(pX.t[:, 0:128], R(Y0), R(X0), [akb], [pX])
                          cp(s0.t[:, 0:128], pX.t[:, 0:128], [pX], [s0])
                          yield
                          pY = pss()
                          MM(pY.t[:, 0:128], R(X0), R(Y0), [akb], [pY])
                          cp(s0.t[:, 256:384], pY.t[:, 0:128], [pY], [s0], prefer_act=not odd)
                          yield
                          cur = 0
                          for lev in range(1, 5):
                              sc, sn = S[cur], S[1 - cur]
                              pXW = pss()
                              MM(pXW.t[:, 0:256], R(sc.t[:, 256:384]), R(sc.t[:, 0:256]), [sc], [pXW])
                              if lev < 4:
                                  cp(sn.t[:, 0:128], pXW.t[:, 0:128], [pXW], [sn])
                              V(lambda: nc.vector.tensor_tensor(R(sn.t[:, 128:256]), pXW.t[:, 128:256], sc.t[:, 128:256], ALU.add), [pXW, sc], [sn])
                              yield
                              pY = pss()
                              MM(pY.t[:, 0:128], R(sc.t[:, 0:128]), R(sc.t[:, 256:384]), [sc], [pY])
                              cp(sn.t[:, 256:384], pY.t[:, 0:128], [pY], [sn], prefer_act=not odd)
                              yield
                              cur = 1 - cur
                          sc = S[cur]
                          pW = pss()
                          MM(pW.t[:, 0:128], R(sc.t[:, 256:384]), R(sc.t[:, 128:256]), [sc], [pW])
                          V(lambda: nc.vector.tensor_tensor(R(mt.t[:, :]), pW.t[:, 0:128], sc.t[:, 128:256], ALU.add), [pW, sc], [mt])
                          yield

                      def seqc(c):
                          sl = c % NSLOT
                          tm, akb, mt = TM[sl], AKB[sl], MT[sl]
                          pR = pss()
                          MM(pR.t[:, 0:128], R(KR.t[:, c, 0:128]), R(Hj), [KR, Hb], [pR], start=True, stop=False, inc=False)
                          MM(pR.t[:, 0:128], R(akb.t[:, 0:128]), R(tm.t[:, 256:384]), [akb, tm], [pR], start=False, stop=True)
                          A(lambda: nc.scalar.copy(R(RH.t[:, :]), pR.t[:, 0:128]), [pR], [RH])
                          yield
                          pU = pss()
                          MM(pU.t[:, 0:128], R(mt.t[:, :]), R(RH.t[:, :]), [mt, RH], [pU])
                          V(lambda: nc.vector.tensor_scalar(R(nU.t[:, :]), pU.t[:, 0:128], -1.0, None, ALU.mult), [pU], [nU])
                          yield
                          pH = pss()
                          MM(pH.t[:, 0:128], identr, R(Hj), [cstr, Hb], [pH], start=True, stop=False, inc=False)
                          MM(pH.t[:, 0:128], R(tm.t[:, 0:128]), R(tm.t[:, 256:384]), [tm], [pH], start=False, stop=False, inc=False)
                          MM(pH.t[:, 0:128], R(tm.t[:, 128:256]), R(nU.t[:, :]), [tm, nU], [pH], start=False, stop=True, inc=False)
                          MM(pH.t[:, 128:192], R(Hj), R(KR.t[:, c, 128:192]), [Hb, KR], [pH], start=True, stop=False, inc=False)
                          MM(pH.t[:, 128:192], R(tm.t[:, 256:384]), R(akb.t[:, 128:192]), [tm, akb], [pH], start=False, stop=False, inc=False)
                          MM(pH.t[:, 128:192], R(nU.t[:, :]), R(akb.t[:, 320:384]), [nU, akb], [pH], start=False, stop=True)
                          A(lambda: nc.scalar.activation(out=R(Hj), in_=pH.t[:, 0:128], func=AF.Identity, scale=EG.t[:, c * 64 + 63:c * 64 + 64]), [pH, EG], [Hb])
                          A(lambda: nc.scalar.copy(R(Y.t[:, c * 64:(c + 1) * 64]), pH.t[:, 128:192]), [pH], [Y])
                          yield

                      if j < 7:
                          filler = g_inproj([j + 1, 9 + j, 17 + j, 26 + j], BIG)
                      else:
                          filler = g_inproj([33, 41, 49, 57], BIG)
                      def g_mix_next(jn):
                          for bank, cc, dst in ((BIG[0], jn, r), (BIG[1], 8 + jn, k), (BIG[2], 16 + jn, v)):
                              mix_shift(bank, cc, dst)
                              yield

                      filler2 = g_mix_next(j + 1) if j < 7 else None
                      active = []
                      nextA, doneA, doneSeq, seq_run = 0, set(), -1, False
                      it = 0
                      while doneSeq < NCH - 1:
                          while nextA < NCH and sum(1 for a_ in active if a_[0] == "A") < NCHAIN and nextA <= doneSeq + NSLOT:
                              active.append(("A", nextA, phaseA(nextA)))
                              nextA += 1
                          if not seq_run and (doneSeq + 1) in doneA:
                              active.append(("S", doneSeq + 1, seqc(doneSeq + 1)))
                              seq_run = True
                          for item in list(active):
                              try:
                                  next(item[2])
                              except StopIteration:
                                  active.remove(item)
                                  if item[0] == "A":
                                      doneA.add(item[1])
                                  else:
                                      doneSeq = item[1]
                                      seq_run = False
                          it += 1
                          if filler is not None:
                              try:
                                  next(filler)
                              except StopIteration:
                                  filler = None
                          elif filler is None and filler2 is not None and it % 2 == 0:
                              try:
                                  next(filler2)
                              except StopIteration:
                                  filler2 = None
                      if filler is not None:
                          for _ in filler:
                              pass
                      if filler2 is not None:
                          for _ in filler2:
                              pass
                      _stage(4)
                      pm_ = pss()
                      MM(pm_.t[:, :], meanblkr, R(Y.t[:, :]), [cstr, Y], [pm_])
                      yc = W["yc"]
                      V(lambda: nc.vector.tensor_tensor(yc.t[:, :], Y.t[:, :], pm_.t[:, :], ALU.subtract), [Y, pm_], [yc])
                      A(lambda: nc.scalar.activation(out=R(sq.t[:, :]), in_=yc.t[:, :], func=AF.Square), [yc], [sq])
                      pv_ = pss()
                      MM(pv_.t[:, :], meanblkr, R(sq.t[:, :]), [cstr, sq], [pv_])
                      V(lambda: nc.vector.tensor_scalar(lnv.t[:, :], pv_.t[:, :], GN_EPS, None, ALU.add), [pv_], [lnv])
                      A(lambda: nc.scalar.activation(out=lnv.t[:, :], in_=lnv.t[:, :], func=AF.Ln), [lnv], [lnv])
                      A(lambda: nc.scalar.activation(out=rn.t[:, :], in_=lnv.t[:, :], func=AF.Exp, scale=-0.5), [lnv], [rn])
                      t2, t3 = W["t2"], W["t3"]
                      G(lambda: nc.gpsimd.tensor_tensor(t2.t[:, :], yc.t[:, :], rn.t[:, :], ALU.mult), [yc, rn], [t2])
                      G(lambda: nc.gpsimd.tensor_scalar(t3.t[:, :], t2.t[:, :], P(65 + j, 66 + j), P(73 + j, 74 + j), ALU.mult, ALU.add), [t2, prm], [t3])
                      G(lambda: nc.gpsimd.tensor_tensor(t2.t[:, :], t3.t[:, :], bonus.t[:, :], ALU.add), [t3, bonus], [t2])
                      V(lambda: nc.vector.tensor_tensor(ya_in.t[:, j, :], t2.t[:, :], sz.t[:, :], ALU.mult), [t2, sz], [yas[j]])

                  _stage(5)
                  unit = [0]

                  def bankset():
                      s_ = PS[0:4] if unit[0] % 2 == 0 else PS[4:8]
                      unit[0] += 1
                      return s_

                  for j in range(8):
                      bs = bankset()
                      if j > 0:
                          inproj_now([33 + j, 41 + j, 49 + j, 57 + j], bs)
                      pbg, pcg, phb, pzb = bs
                      t1, t2, t3 = W["t1"], W["t2"], W["t3"]
                      A(lambda: nc.scalar.copy(t1.t[:, :], pcg.t[:, :]), [pcg], [t1])
                      V(lambda: nc.vector.tensor_tensor(uext.t[:, 2:TT + 2], phb.t[:, :], t1.t[:, :], ALU.mult), [phb, t1], [uext])
                      V(lambda: nc.vector.tensor_copy(uext.t[:, 0:2], cbnd.t[:, j, :]), [cbnd], [uext])
                      G(lambda: nc.gpsimd.tensor_scalar(t2.t[:, :], uext.t[:, 0:TT], P(81 + j, 82 + j), 0.0, ALU.mult, ALU.add), [uext, prm], [t2])
                      V(lambda: nc.vector.scalar_tensor_tensor(t3.t[:, :], uext.t[:, 1:TT + 1], P(89 + j, 90 + j), t2.t[:, :], ALU.mult, ALU.add), [uext, prm, t2], [t3])
                      V(lambda: nc.vector.scalar_tensor_tensor(t2.t[:, :], uext.t[:, 2:TT + 2], P(97 + j, 98 + j), t3.t[:, :], ALU.mult, ALU.add), [uext, prm, t3], [t2])
                      V(lambda: nc.vector.tensor_copy(cbnd.t[:, j, :], uext.t[:, TT:TT + 2]), [uext], [cbnd])
                      V(lambda: nc.vector.tensor_tensor(t3.t[:, :], pbg.t[:, :], t2.t[:, :], ALU.mult), [pbg, t2], [t3])
                      A(lambda: nc.scalar.activation(out=t1.t[:, :], in_=pzb.t[:, :], func=AF.Sigmoid), [pzb], [t1])
                      V(lambda: nc.vector.tensor_tensor(t2.t[:, :], pzb.t[:, :], t1.t[:, :], ALU.mult), [pzb, t1], [t2])
                      G(lambda: nc.gpsimd.tensor_tensor(yb_in.t[:, j, :], t3.t[:, :], t2.t[:, :], ALU.mult), [t3, t2], [ybs[j]])

                  _stage(6)
                  for m in range(16):
                      pab, pbb = pbufs[(2 * m) % 3], pbufs[(2 * m + 1) % 3]
                      kb.dma("sp", ("pp", (2 * m) % 3), pab.t[:, :, :], pa_b[l, m].rearrange("p (a b) -> p a b", b=128), reads=[conv_ev[l]["pa"]], writes=[pab])
                      kb.dma("sp", ("pp", (2 * m + 1) % 3), pbb.t[:, :, :], pb_b[l, m].rearrange("p (a b) -> p a b", b=128), reads=[conv_ev[l]["pb"]], writes=[pbb])
                      pya, pyb, pga, pgb = bankset()
                      for kc in range(8):
                          MM(pya.t[:, :], pab.t[:, kc, :], ya_in.t[:, kc, :], [pab, yas[kc]], [pya], start=(kc == 0), stop=(kc == 7), inc=(kc == 7))
                      for kc in range(8):
                          MM(pyb.t[:, :], pbb.t[:, kc, :], yb_in.t[:, kc, :], [pbb, ybs[kc]], [pyb], start=(kc == 0), stop=(kc == 7), inc=(kc == 7))
                      inproj_now([65 + m, 81 + m], [pga, pgb])
                      ta, tb = exts[0], W["t1"]
                      A(lambda: nc.scalar.activation(out=ta.t[:, 0:TT], in_=pga.t[:, :], func=AF.Sigmoid), [pga], [ta])
                      V(lambda: nc.vector.tensor_tensor(ta.t[:, 0:TT], pya.t[:, :], ta.t[:, 0:TT], ALU.mult), [pya, ta], [ta])
                      A(lambda: nc.scalar.activation(out=tb.t[:, 0:TT], in_=pgb.t[:, :], func=AF.Sigmoid), [pgb], [tb])
                      V(lambda: nc.vector.tensor_tensor(tb.t[:, 0:TT], pyb.t[:, :], tb.t[:, 0:TT], ALU.mult), [pyb, tb], [tb])
                      G(lambda: nc.gpsimd.tensor_tensor(mix.t[:, m, :], ta.t[:, 0:TT], tb.t[:, 0:TT], ALU.add), [ta, tb], [mixs[m]])

                  _stage(7)
                  for half in range(2):
                      posb = [PS[0:4], PS[4:8]]
                      for g in range(8):
                          wo = wobufs[g % 2]
                          kb.dma("sp", ("wo", g % 2), wo.t[:, :, :], wo_b[l, g].rearrange("p (a b) -> p a b", b=256), reads=[conv_ev[l]["wo"]], writes=[wo])
                          c0_ = (g % 2) * 256
                          for sl_ in range(2):
                              s = half * 2 + sl_
                              po = posb[sl_][g // 2]
                              for dc in range(DC):
                                  MM(po.t[:, c0_:c0_ + 256], mix.t[:, dc, s * 128:(s + 1) * 128], wo.t[:, dc, :], [mixs[dc], wo], [po], start=(dc == 0), stop=(dc == DC - 1), inc=(dc == DC - 1))
                      for sl_ in range(2):
                          s = half * 2 + sl_
                          pos = posb[sl_]
                          xr = xo[1]
                          kb.dma("sp", ("x", 1), xr.t[:, :], x_src[t0 + s * 128:t0 + (s + 1) * 128, :], reads=[xsrc_buf], writes=[xr])
                          for q in range(4):
                              A(lambda: nc.scalar.activation(out=ya_in.t[:, 0, :], in_=pos[q].t[:, :], func=AF.Square, accum_out=ss4.t[:, 4 + q:5 + q]), [pos[q]], [yas[0], ss4])
                          V(lambda: nc.vector.tensor_reduce(rs4.t[:, 4:5], ss4.t[:, 4:8], mybir.AxisListType.X, ALU.add), [ss4], [rs4])
                          V(lambda: nc.vector.tensor_scalar(rs4.t[:, 4:5], rs4.t[:, 4:5], 1.0 / D, RMS_EPS, ALU.mult, ALU.add), [rs4], [rs4])
                          A(lambda: nc.scalar.activation(out=rs4.t[:, 4:5], in_=rs4.t[:, 4:5], func=AF.Ln), [rs4], [rs4])
                          A(lambda: nc.scalar.activation(out=rs4.t[:, 4:5], in_=rs4.t[:, 4:5], func=AF.Exp, scale=-0.5), [rs4], [rs4])
                          xn = xo[0]
                          for q in range(4):
                              V(lambda: nc.vector.scalar_tensor_tensor(xn.t[:, q * 512:(q + 1) * 512], pos[q].t[:, :], rs4.t[:, 4:5], GP.t[:, q * 512:(q + 1) * 512], ALU.mult, ALU.mult),
                                [pos[q], rs4, GP], [xn])
                          G(lambda: nc.gpsimd.tensor_tensor(xn.t[:, :], xn.t[:, :], xr.t[:, :], ALU.add), [xn, xr], [xn])
                          ob = Buf("orow")
                          kb.dma("pool", "st", out_d[t0 + s * 128:t0 + (s + 1) * 128, :], xn.t[:, :], reads=[xn], writes=[ob])
              kb.barrier()
          except _Stop:
            break
        kb.barrier()
        print("kernel built: ops=%d waits=%d" % (kb.nops, kb.nwait))
    return nc


def _consts():
    cst = np.zeros((128, NCST), np.float32)
    cst[:, 0:128] = np.eye(128, dtype=np.float32)
    blk = np.zeros((128, 128), np.float32)
    blk[0:64, 0:64] = 1.0
    blk[64:128, 64:128] = 1.0
    cst[:, 128:256] = blk
    cst[:, 256:384] = blk / 64.0
    s = np.arange(64)[:, None]
    t = np.arange(64)[None, :]
    su = (s < t).astype(np.float32)
    iu = (s <= t).astype(np.float32)
    mA = np.zeros((128, 192), np.float32)
    mA[0:64, 0:64] = su
    mA[64:128, 64:128] = su
    mA[0:64, 128:192] = iu
    mA[64:128, 128:192] = iu
    mQ = np.zeros((128, 128), np.float32)
    mQ[0:64, 0:64] = su.T
    mQ[64:128, 64:128] = su.T
    mB = mA.copy()
    mB[:, 0:128] *= -1.0
    cst[:, 384:576] = mA
    cst[:, 576:768] = mB
    cst[:, 768:896] = -mQ
    rm = np.ones((128, TT), np.float32)
    rm[:, ::CH] = 0.0
    cst[:, 896:1408] = rm
    cst[:, 1408:1536] = 1.0
    return cst


def prep_shared(inp, L):
    f = lambda a: np.ascontiguousarray(a, dtype=np.float32)
    sh = {}
    def relay(w, colblk):
        Lh, K, N = w.shape
        return f(w.reshape(Lh, K // 128, 128, N // colblk, colblk).transpose(0, 3, 2, 1, 4).reshape(Lh, N // colblk, 128, (K // 128) * colblk))
    sh["ada_r"] = relay(inp["ada_w"][:L], 128)
    sh["win_r"] = relay(inp["w_in"][:L], 128)
    sh["pa_r"] = relay(inp["p_a"][:L], 128)
    sh["pb_r"] = relay(inp["p_b"][:L], 128)
    sh["wo_r"] = relay(inp["w_out"][:L], 256)
    fm = lambda v, n: v.reshape(L, n, 128).transpose(0, 2, 1)
    prm = np.zeros((L, 128, NPRM), np.float32)
    prm[:, :, 0:25] = fm(inp["mu_shift"][:L], 25)
    prm[:, :, 25:33] = fm(inp["w0"][:L], 8)
    prm[:, :, 33:41] = fm(inp["a0"][:L], 8)
    prm[:, :, 41:49] = fm(inp["k_k"][:L], 8)
    prm[:, :, 49:57] = fm(inp["k_a"][:L], 8)
    prm[:, :, 57:65] = fm(inp["r_k"][:L].reshape(L, 1024), 8)
    prm[:, :, 65:73] = fm(inp["lnx_gain"][:L], 8)
    prm[:, :, 73:81] = fm(inp["lnx_bias"][:L], 8)
    cw = inp["conv_w"][:L]
    prm[:, :, 81:89] = fm(cw[:, 0], 8)
    prm[:, :, 89:97] = fm(cw[:, 1], 8)
    prm[:, :, 97:105] = fm(cw[:, 2], 8)
    prm[:, :, 105:121] = fm(inp["pre_gain"][:L], 16)
    prm[:, :, 121:137] = fm(inp["post_gain"][:L], 16)
    prm[:, :, 137:185] = fm(inp["ada_b"][:L], 48)
    sh["prm"] = f(prm.transpose(1, 0, 2))
    lora = np.zeros((L, 128, 8, 256), np.float32)
    w2 = inp["w2"][:L].reshape(L, 64, 8, 128)
    a2 = inp["a2"][:L].reshape(L, 64, 8, 128)
    lora[:, 0:64, :, 0:128] = w2
    lora[:, 64:128, :, 128:256] = a2
    sh["lora"] = f(lora.reshape(L, 128, 8 * 256))
    sh["cst"] = _consts()
    return sh


_CACHE = {}


def run(inp, T, L, nb):
    key = (T, L)
    if key not in _CACHE:
        _CACHE[key] = build_program(T, L)
    nc = _CACHE[key]
    sh = prep_shared(inp, L)
    in_maps = []
    zero = None
    slots = [0, 2, 4, 6, 1, 3, 5, 7]
    owner = {}
    for i in range(min(nb, 8)):
        owner[slots[i]] = i
    for core in range(8):
        if core in owner:
            b = owner[core]
            m = dict(sh)
            m["x"] = np.ascontiguousarray(inp["x"][b, :T], dtype=np.float32)
            m["cfm"] = np.ascontiguousarray(inp["c"][b].reshape(DC, 128).T, dtype=np.float32)
        else:
            if zero is None:
                zero = {k: np.zeros_like(v) for k, v in sh.items()}
                zero["x"] = np.zeros((T, D), np.float32)
                zero["cfm"] = np.zeros((128, DC), np.float32)
            m = zero
        in_maps.append(m)
    res = run_bass_kernel_spmd(nc, in_maps, core_ids=list(range(8)))
    inv = {b: c for c, b in owner.items()}
    return np.stack([res.results[inv[b]]["out"] for b in range(nb)], axis=0)


def kernel(**inputs):
    inp = {k: np.asarray(v) for k, v in inputs.items()}
    B, T, _ = inp["x"].shape
    L = inp["w_in"].shape[0]
    out = run(inp, T, L, B)
    return out.astype(np.float32)
```

```python
import contextlib
import os
import numpy as np
import concourse.bass as bass
import concourse.mybir as mybir
from concourse.bass_utils import run_bass_kernel_spmd

F32, BF16 = mybir.dt.float32, mybir.dt.bfloat16
F32R = mybir.dt.float32r
AF = mybir.ActivationFunctionType
ALU = mybir.AluOpType

D = 2048
DC = 16
DA = 1024
NCC = 97
TT = 512
NSUB = 4
CH = 64
NCH = TT // CH
NPRM = 185
NCST = 1536
C0 = float(np.exp(-0.5))
RMS_EPS = 1e-6
GN_EPS = 64e-5
SEM_LIMIT = 30000


class _Stop(Exception):
    pass


STAGE = float(os.environ.get("KSTAGE", "99"))


def _stage(n):
    if STAGE <= n:
        raise _Stop()


class Buf:
    __slots__ = ("name", "w", "r", "t", "ps")

    def __init__(self, name, t=None, ps=False):
        self.name = name
        self.ps = ps
        self.w = None
        self.r = []
        self.t = t


class KB:
    def __init__(self, nc, es):
        self.nc = nc
        self.es = es
        self.eng = {"pe": nc.tensor, "act": nc.scalar, "dve": nc.vector, "pool": nc.gpsimd, "sp": nc.sync}
        self.cnt = {e: 0 for e in self.eng}
        self.gen = {e: 0 for e in self.eng}
        self.sem = {e: es.enter_context(nc.semaphore("c_" + e + "0")) for e in self.eng}
        self.waited = {e: {} for e in self.eng}
        self.dsem = {}
        self.nwait = 0
        self.nops = 0

    def _wait(self, e, ev):
        key, sem, val = ev
        if self.waited[e].get(key, 0) >= val:
            return
        self.eng[e].wait_ge(sem, val)
        self.waited[e][key] = val
        self.nwait += 1

    def _deps(self, e, reads, writes, skip_key=None):
        best = {}
        for b in reads:
            if b.w is not None:
                ev = b.w
                if ev[0] not in best or best[ev[0]][2] < ev[2]:
                    best[ev[0]] = ev
        for b in writes:
            evs = list(b.r)
            if b.w is not None:
                evs.append(b.w)
            for ev in evs:
                if ev[0] not in best or best[ev[0]][2] < ev[2]:
                    best[ev[0]] = ev
        for ev in best.values():
            if e == "pe" and ev[0][0] == "eng" and ev[0][1] == "pe":
                continue
            if skip_key is not None and ev[0] == skip_key:
                continue
            self._wait(e, ev)

    def _record(self, myev, reads, writes):
        for b in reads:
            b.r.append(myev)
            if len(b.r) > 64:
                best = {}
                for ev in b.r:
                    if ev[0] not in best or best[ev[0]][2] < ev[2]:
                        best[ev[0]] = ev
                b.r = list(best.values())
        for b in writes:
            b.w = myev
            b.r = []

    def op(self, e, fn, reads=(), writes=(), inc=True):
        if self.cnt[e] >= SEM_LIMIT:
            self.gen[e] += 1
            self.cnt[e] = 0
            self.sem[e] = self.es.enter_context(self.nc.semaphore("c_%s%d" % (e, self.gen[e])))
        psr = [b for b in reads if b.ps]
        if psr:
            reads = [b for b in reads if not b.ps]
            writes = list(writes) + psr
        self._deps(e, reads, writes)
        inst = fn()
        self.nops += 1
        key = ("eng", e, self.gen[e])
        if inc:
            self.cnt[e] += 1
            inst.then_inc(self.sem[e], 1)
            myev = (key, self.sem[e], self.cnt[e])
        else:
            myev = (key, self.sem[e], self.cnt[e] + 1)
        self._record(myev, reads, writes)
        return inst

    def dma(self, q, slot, out, in_, reads=(), writes=(), multi=None, **kw):
        if slot not in self.dsem:
            self.dsem[slot] = [self.es.enter_context(self.nc.semaphore("d%d" % len(self.dsem))), 0]
        sem, n = self.dsem[slot]
        key = ("dma", slot)
        self._deps(q, reads, writes, skip_key=(key if multi is not None else None))
        if multi is None and n > 0:
            self._wait(q, (key, sem, 16 * n))
        inst = self.eng[q].dma_start(out=out, in_=in_, **kw)
        inst.then_inc(sem, 16)
        self.dsem[slot][1] = n + 1
        val = 16 * (n + 1) if multi is None else 16 * multi
        myev = (key, sem, val)
        self._record(myev, reads, writes)
        self.nops += 1

    def barrier(self):
        for e in self.eng:
            for e2 in self.eng:
                if e2 == e or self.cnt[e2] == 0:
                    continue
                self._wait(e, (("eng", e2, self.gen[e2]), self.sem[e2], self.cnt[e2]))
            for slot, (sem, n) in self.dsem.items():
                if n > 0:
                    self._wait(e, (("dma", slot), sem, 16 * n))


def build_program(T, L, taps=False):
    NT = T // TT
    nc = bass.Bass("TRN2", target_bir_lowering=False)
    dt_in = lambda name, shape: nc.dram_tensor(name, shape, F32, kind="ExternalInput").ap()
    x_d = dt_in("x", [T, D])
    cfm_d = dt_in("cfm", [128, DC])
    ada_d = dt_in("ada_r", [L, 48, 128, DC * 128])
    win_d = dt_in("win_r", [L, NCC, 128, DC * 128])
    pa_d = dt_in("pa_r", [L, 16, 128, 8 * 128])
    pb_d = dt_in("pb_r", [L, 16, 128, 8 * 128])
    wo_d = dt_in("wo_r", [L, 8, 128, DC * 256])
    prm_d = dt_in("prm", [128, L, NPRM])
    lora_d = dt_in("lora", [L, 128, 8 * 256])
    cst_d = dt_in("cst", [128, NCST])
    out_d = nc.dram_tensor("out", [T, D], F32, kind="ExternalOutput").ap()
    win_b = nc.dram_tensor("win_b", [L, NCC, 128, DC * 128], BF16, kind="Internal").ap()
    pa_b = nc.dram_tensor("pa_b", [L, 16, 128, 8 * 128], BF16, kind="Internal").ap()
    pb_b = nc.dram_tensor("pb_b", [L, 16, 128, 8 * 128], BF16, kind="Internal").ap()
    wo_b = nc.dram_tensor("wo_b", [L, 8, 128, DC * 256], BF16, kind="Internal").ap()

    es = contextlib.ExitStack()
    with es:
        kb = KB(nc, es)
        _n = [0]

        def sb(shape, dt=F32, name=None):
            _n[0] += 1
            nm = "%s_%d" % (name or "t", _n[0])
            t = es.enter_context(nc.sbuf_tensor(nm, list(shape), dt))
            return Buf(nm, t)

        PS = []
        for i in range(8):
            t = es.enter_context(nc.psum_tensor("ps%d" % i, [128, 512], F32))
            PS.append(Buf("ps%d" % i, t, ps=True))
        _pi = [0]

        def ps():
            b = PS[_pi[0] % 8]
            _pi[0] += 1
            return b

        V = lambda fn, r, w: kb.op("dve", fn, r, w)
        A = lambda fn, r, w: kb.op("act", fn, r, w)
        _alt = [0]

        def AV(fa, fv, r, w):
            _alt[0] += 1
            kav = os.environ.get("KAV", "")
            if kav == "act" or (kav != "dve" and _alt[0] % 2):
                return kb.op("act", fa, r, w)
            return kb.op("dve", fv, r, w)

        def MM(out_ap, lhsT, rhs, r, w, start=True, stop=True, inc=True):
            return kb.op("pe", lambda: nc.tensor.matmul(out_ap, lhsT, rhs, start=start, stop=stop), r, w, inc=inc)

        conv_ev = {}

        def issue_conv(l):
            bufs = {k: Buf("cv_%s_%d" % (k, l)) for k in ("win", "pa", "pb", "wo")}
            conv_ev[l] = bufs
            tot = NCC + 16 + 16 + 8
            slot = ("conv", l)
            for cc in range(NCC):
                kb.dma("pool", slot, win_b[l, cc], win_d[l, cc], writes=[bufs["win"]], multi=tot, max_dma_last_dim=4096)
                yield
            for m in range(16):
                kb.dma("pool", slot, pa_b[l, m], pa_d[l, m], writes=[bufs["pa"]], multi=tot, max_dma_last_dim=4096)
                yield
                kb.dma("pool", slot, pb_b[l, m], pb_d[l, m], writes=[bufs["pb"]], multi=tot, max_dma_last_dim=4096)
                yield
            for g in range(8):
                kb.dma("pool", slot, wo_b[l, g], wo_d[l, g], writes=[bufs["wo"]], multi=tot, max_dma_last_dim=4096)
                yield

        for _ in issue_conv(0):
            pass

        cst = sb([128, NCST], F32, "cst")
        kb.dma("sp", "cst", cst.t[:, :], cst_d, writes=[cst])
        ident = cst.t[:, 0:128]
        onesblk = cst.t[:, 128:256]
        meanblk = cst.t[:, 256:384]
        maskAll = cst.t[:, 384:896]
        rmask = cst.t[:, 896:1408]
        ones = cst.t[:, 1408:1536]
        zcol = cst.t[:, 384:385]
        prm = sb([128, L, NPRM], F32, "prm")
        kb.dma("sp", "prm", prm.t[:, :, :], prm_d, writes=[prm])
        der = sb([128, L, 33], F32, "der")
        for l in range(L):
            V(lambda l=l: nc.vector.tensor_scalar(der.t[:, l, 0:25], prm.t[:, l, 0:25], -1.0, 1.0, ALU.mult, ALU.add), [prm], [der])
            V(lambda l=l: nc.vector.tensor_scalar(der.t[:, l, 25:33], prm.t[:, l, 49:57], -1.0, 1.0, ALU.mult, ALU.add), [prm], [der])
        cf = sb([128, DC], F32, "cf")
        kb.dma("sp", "cf", cf.t[:, :], cfm_d, writes=[cf])
        sgc = sb([128, DC], F32, "sgc")
        sc = sb([128, DC], F32, "sc")
        A(lambda: nc.scalar.activation(out=sgc.t[:, :], in_=cf.t[:, :], func=AF.Sigmoid), [cf], [sgc])
        V(lambda: nc.vector.tensor_tensor(sc.t[:, :], cf.t[:, :], sgc.t[:, :], ALU.mult), [cf, sgc], [sc])

        modfm = sb([128, L, 48], F32, "modfm")
        g1 = sb([128, L, DC], F32, "g1")
        gpfm = sb([128, L, DC], F32, "gpfm")
        with contextlib.ExitStack() as es2:
            adab = []
            for i in range(3):
                t = es2.enter_context(nc.sbuf_tensor("adab%d" % i, [128, DC, 128], F32))
                adab.append(Buf("adab%d" % i, t))
            k = 0
            for l in range(L):
                pm = ps()
                for cc in range(48):
                    ab = adab[k % 3]
                    kb.dma("sp", ("ada", k % 3), ab.t[:, :, :], ada_d[l, cc].rearrange("p (a b) -> p a b", b=128), writes=[ab])
                    k += 1
                    for dc in range(DC):
                        MM(pm.t[:, cc:cc + 1], ab.t[:, dc, :], sc.t[:, dc:dc + 1], [ab, sc], [pm],
                           start=(dc == 0), stop=(dc == DC - 1), inc=(dc == DC - 1))
                V(lambda l=l, pm=pm: nc.vector.tensor_tensor(modfm.t[:, l, :], pm.t[:, 0:48], prm.t[:, l, 137:185], ALU.add), [pm, prm], [modfm])
                V(lambda l=l: nc.vector.scalar_tensor_tensor(g1.t[:, l, :], modfm.t[:, l, 16:32], 1.0, prm.t[:, l, 105:121], ALU.add, ALU.mult), [modfm, prm], [g1])
                V(lambda l=l: nc.vector.tensor_tensor(gpfm.t[:, l, :], modfm.t[:, l, 32:48], prm.t[:, l, 121:137], ALU.mult), [modfm, prm], [gpfm])
            kb.barrier()

        R = lambda ap: ap.bitcast(F32R)
        cstr = sb([128, 384], F32, "cstr")
        V(lambda: nc.vector.tensor_copy(R(cstr.t[:, :]), cst.t[:, 0:384]), [cst], [cstr])
        identr, onesblkr, meanblkr = R(cstr.t[:, 0:128]), R(cstr.t[:, 128:256]), R(cstr.t[:, 256:384])
        hT = sb([128, DC, TT], BF16, "hT")
        hTs = [Buf("hT%d" % i) for i in range(DC)]
        NW = 3
        wbufs = [sb([128, DC, 128], BF16, "wb") for _ in range(NW)]
        pbufs = [sb([128, 8, 128], BF16, "pbuf") for _ in range(3)]
        wobufs = [sb([128, DC, 256], BF16, "wob") for _ in range(2)]
        ya_in = sb([128, 8, TT], BF16, "ya_in")
        yb_in = sb([128, 8, TT], BF16, "yb_in")
        mix = sb([128, DC, TT], BF16, "mix")
        yas = [Buf("ya%d" % i) for i in range(8)]
        ybs = [Buf("yb%d" % i) for i in range(8)]
        mixs = [Buf("mix%d" % i) for i in range(DC)]
        Hbufs = [Buf("H%d" % i) for i in range(8)]
        GP = sb([128, D], F32, "GP")
        lorab = [sb([128, 256], F32, "lora") for _ in range(1)]
        xo = [sb([128, D], F32, "xo") for _ in range(2)]
        ss4 = sb([128, 8], F32, "ss4")
        rs4 = sb([128, 8], F32, "rs4")
        bnd = sb([128, 25], F32, "bnd")
        cbnd = sb([128, 8, 2], F32, "cbnd")
        exts = [sb([128, TT + 2], F32, "ext") for _ in range(1)]
        uext = exts[0]
        Hst = sb([128, 8, 128], F32, "Hst")
        names = ["r", "k", "v", "sz", "tl", "sig", "csum", "t1", "EG", "EGi", "a", "kk",
                 "kkn", "kp", "bb", "bonus", "sqrk", "Y"]
        W = {n: sb([128, TT], F32, n) for n in names}
        for al, tgt in (("sq", "sqrk"), ("rk", "sqrk"), ("lnv", "sig"), ("rn", "csum"), ("cex", "t1"), ("EGp", "t1"), ("fac", "sig"), ("tmpm", "a"),
                        ("yc", "kk"), ("t2", "kkn"), ("t3", "bb")):
            W[al] = W[tgt]
        KR = sb([128, NCH, 192], F32, "KR")
        Ktb = sb([128, NCH, 128], F32, "Ktb")
        Btb = sb([128, NCH, 128], F32, "Btb")
        Vb = sb([128, NCH, 128], F32, "Vb")
        NSLOT = 4
        NCHAIN = 3
        TM = [sb([128, 384], F32, "TM") for _ in range(NSLOT)]
        AKB = [sb([128, 512], F32, "AKB") for _ in range(NSLOT)]
        MT = [sb([128, 128], F32, "MT") for _ in range(NSLOT)]
        SXW = [[sb([128, 384], F32, "SXW") for _ in range(2)] for _ in range(NCHAIN)]
        RH = sb([128, 128], F32, "RH")
        nU = sb([128, 128], F32, "nU")
        print("sbuf bytes remaining:", nc.sbuf_bytes_remaining)

        for b in (KR, Ktb, Btb, Vb):
            n_ = b.t.shape[1] * b.t.shape[2]
            V(lambda b=b, n_=n_: nc.vector.tensor_copy(R(b.t[:, :, :].rearrange("p a b -> p (a b)")), zcol.to_broadcast([128, n_])), [cst], [b])

        def v3(ap):
            return ap.rearrange("p (c t) -> p c t", t=CH)

        G = lambda fn, r, w: kb.op("pool", fn, r, w)
        _si = [0]

        def pss():
            b = PS[4 + _si[0] % 4]
            _si[0] += 1
            return b

        for l in range(L):
          try:
              _stage(0)
              P = lambda a, b2, l=l: prm.t[:, l, a:b2]
              x_src = x_d if l == 0 else out_d
              xsrc_buf = Buf("xsrc")
              V(lambda: nc.vector.memset(bnd.t[:, :], 0.0), [], [bnd])
              V(lambda: nc.vector.memset(cbnd.t[:, :, :], 0.0), [], [cbnd])
              V(lambda: nc.vector.tensor_copy(R(Hst.t[:, :, :].rearrange("p a b -> p (a b)")), zcol.to_broadcast([128, 1024])), [cst], Hbufs)
              cgen = issue_conv(l + 1) if l + 1 < L else None
              _stage(0.3)
              Rt = xo[0]
              for dc in range(DC):
                  V(lambda dc=dc, l=l: nc.vector.tensor_scalar(Rt.t[:, dc * 128:(dc + 1) * 128], ident, gpfm.t[:, l, dc:dc + 1], None, ALU.mult), [cst, gpfm], [Rt])
              for g in range(4):
                  pg = ps()
                  MM(pg.t[:, :], ones, Rt.t[:, g * 512:(g + 1) * 512], [cst, Rt], [pg])
                  A(lambda g=g, pg=pg: nc.scalar.copy(GP.t[:, g * 512:(g + 1) * 512], pg.t[:, :]), [pg], [GP])

              _stage(0.5)
              order = []
              for tt in range(NT):
                  seq_ = [24]
                  for j in range(8):
                      seq_ += [j, 8 + j, 16 + j, 25 + j]
                  for j in range(8):
                      seq_ += [33 + j, 41 + j, 49 + j, 57 + j]
                  for m in range(16):
                      seq_ += [65 + m, 81 + m]
                  order += seq_
              wstate = {"issued": 0, "used": 0}

              def wget(cc, l=l, order=order, wstate=wstate):
                  while wstate["issued"] < len(order) and wstate["issued"] < wstate["used"] + NW:
                      i = wstate["issued"]
                      wb = wbufs[i % NW]
                      kb.dma("sp", ("w", i % NW), wb.t[:, :, :], win_b[l, order[i]].rearrange("p (a b) -> p a b", b=128),
                             reads=[conv_ev[l]["win"]], writes=[wb])
                      wstate["issued"] += 1
                  i = wstate["used"]
                  assert order[i] == cc, (order[i], cc)
                  wstate["used"] += 1
                  return wbufs[i % NW]

              def g_inproj(ccs, banks):
                  for cc, p in zip(ccs, banks):
                      wb = wget(cc)
                      for dc in range(DC):
                          MM(p.t[:, :], wb.t[:, dc, :], hT.t[:, dc, :], [wb, hTs[dc]], [p], start=(dc == 0), stop=(dc == DC - 1), inc=(dc == DC - 1))
                          if dc % 4 == 3:
                              yield

              def inproj_now(ccs, banks):
                  for _ in g_inproj(ccs, banks):
                      pass

              _e = [0]

              def mix_shift(p, cc, outb, l=l, rr=False):
                  ext = exts[0]
                  _e[0] += 1
                  A(lambda: nc.scalar.copy(ext.t[:, 1:TT + 1], p.t[:, :]), [p], [ext])
                  V(lambda: nc.vector.tensor_copy(ext.t[:, 0:1], bnd.t[:, cc:cc + 1]), [bnd], [ext])
                  tm = W["tmpm"]
                  V(lambda: nc.vector.tensor_scalar(tm.t[:, :], ext.t[:, 0:TT], prm.t[:, l, cc:cc + 1], None, ALU.mult), [ext, prm], [tm])
                  oap = R(outb.t[:, :]) if rr else outb.t[:, :]
                  V(lambda: nc.vector.scalar_tensor_tensor(oap, p.t[:, :], der.t[:, l, cc:cc + 1], tm.t[:, :], ALU.mult, ALU.add), [p, der, tm], [outb])
                  V(lambda: nc.vector.tensor_copy(bnd.t[:, cc:cc + 1], ext.t[:, TT:TT + 1]), [ext], [bnd])

              for tt in range(NT):
                  t0 = tt * TT
                  junk4 = ya_in.t[:, 0:4, :]
                  for s in range(NSUB):
                      xb = xo[s % 2]
                      kb.dma("sp", ("x", s % 2), xb.t[:, :], x_src[t0 + s * 128:t0 + (s + 1) * 128, :], reads=[xsrc_buf], writes=[xb])
                      A(lambda: nc.scalar.activation(out=junk4, in_=xb.t[:, :].rearrange("p (a b) -> p a b", b=TT), func=AF.Square, accum_out=ss4.t[:, s:s + 1]), [xb], yas[0:4] + [ss4])
                      V(lambda: nc.vector.tensor_scalar(rs4.t[:, s:s + 1], ss4.t[:, s:s + 1], 1.0 / D, RMS_EPS, ALU.mult, ALU.add), [ss4], [rs4])
                      A(lambda: nc.scalar.activation(out=rs4.t[:, s:s + 1], in_=rs4.t[:, s:s + 1], func=AF.Ln), [rs4], [rs4])
                      A(lambda: nc.scalar.activation(out=rs4.t[:, s:s + 1], in_=rs4.t[:, s:s + 1], func=AF.Exp, scale=-0.5), [rs4], [rs4])
                      V(lambda: nc.vector.tensor_scalar(xb.t[:, :], xb.t[:, :], rs4.t[:, s:s + 1], None, ALU.mult), [xb, rs4], [xb])
                      for q in range(4):
                          p = ps()
                          for i in range(4):
                              dc = q * 4 + i
                              kb.op("pe", lambda: nc.tensor.transpose(p.t[:, i * 128:(i + 1) * 128], xb.t[:, dc * 128:(dc + 1) * 128], ident),
                                    [xb, cst], [p], inc=(i == 3))
                          for i in range(4):
                              dc = q * 4 + i
                              if q % 2 == 0:
                                  A(lambda: nc.scalar.activation(out=hT.t[:, dc, s * 128:(s + 1) * 128], in_=p.t[:, i * 128:(i + 1) * 128], func=AF.Identity, scale=g1.t[:, l, dc:dc + 1], bias=modfm.t[:, l, dc:dc + 1]),
                                    [p, g1, modfm], [hTs[dc]])
                              else:
                                  V(lambda: nc.vector.tensor_scalar(hT.t[:, dc, s * 128:(s + 1) * 128], p.t[:, i * 128:(i + 1) * 128], g1.t[:, l, dc:dc + 1], modfm.t[:, l, dc:dc + 1], ALU.mult, ALU.add),
                                    [p, g1, modfm], [hTs[dc]])

                  _stage(1)
                  BIG = PS[0:4]
                  inproj_now([24], [BIG[0]])
                  tl = W["tl"]
                  mix_shift(BIG[0], 24, tl)
                  A(lambda: nc.scalar.activation(out=tl.t[0:64, :], in_=tl.t[0:64, :], func=AF.Tanh), [tl], [tl])
                  inproj_now([0, 8, 16, 25], BIG)
                  _stage(2)
                  for j in range(8):
                      r, k, v = W["r"], W["k"], W["v"]
                      lora = lorab[0]
                      premixed = (j > 0)
                      kb.dma("sp", ("lora", 0), lora.t[:, :], lora_d[l][:, j * 256:(j + 1) * 256], writes=[lora])
                      if not premixed:
                          mix_shift(BIG[0], j, r)
                          mix_shift(BIG[1], 8 + j, k)
                          mix_shift(BIG[2], 16 + j, v)
                      p = BIG[3]
                      t1, sz = W["t1"], W["sz"]
                      A(lambda: nc.scalar.activation(out=t1.t[:, :], in_=p.t[:, :], func=AF.Sigmoid), [p], [t1])
                      V(lambda: nc.vector.tensor_tensor(sz.t[:, :], p.t[:, :], t1.t[:, :], ALU.mult), [p, t1], [sz])
                      pw = pss()
                      MM(pw.t[:, :], lora.t[:, 0:128], tl.t[:, :], [lora, tl], [pw])
                      sig = W["sig"]
                      A(lambda: nc.scalar.activation(out=sig.t[:, :], in_=pw.t[:, :], func=AF.Sigmoid, bias=P(25 + j, 26 + j)), [pw, prm], [sig])
                      pa_ = pss()
                      MM(pa_.t[:, :], lora.t[:, 128:256], tl.t[:, :], [lora, tl], [pa_])
                      a = W["a"]
                      A(lambda: nc.scalar.activation(out=a.t[:, :], in_=pa_.t[:, :], func=AF.Sigmoid, bias=P(33 + j, 34 + j)), [pa_, prm], [a])
                      csum, cex = W["csum"], W["cex"]
                      V(lambda: nc.vector.tensor_tensor_scan(csum.t[:, :], rmask, sig.t[:, :], 0.0, ALU.mult, ALU.add), [cst, sig], [csum])
                      G(lambda: nc.gpsimd.tensor_tensor(cex.t[:, :], csum.t[:, :], sig.t[:, :], ALU.subtract), [csum, sig], [cex])
                      EG, EGi, EGp = W["EG"], W["EGi"], W["EGp"]
                      A(lambda: nc.scalar.activation(out=EG.t[:, :], in_=csum.t[:, :], func=AF.Exp, scale=-C0), [csum], [EG])
                      A(lambda: nc.scalar.activation(out=EGi.t[:, :], in_=csum.t[:, :], func=AF.Exp, scale=C0), [csum], [EGi])
                      A(lambda: nc.scalar.activation(out=EGp.t[:, :], in_=cex.t[:, :], func=AF.Exp, scale=-C0), [cex], [EGp])
                      kk, sq = W["kk"], W["sq"]
                      V(lambda: nc.vector.tensor_scalar(kk.t[:, :], k.t[:, :], P(41 + j, 42 + j), None, ALU.mult), [k, prm], [kk])
                      A(lambda: nc.scalar.activation(out=R(sq.t[:, :]), in_=kk.t[:, :], func=AF.Square), [kk], [sq])
                      pss_ = pss()
                      MM(pss_.t[:, :], onesblkr, R(sq.t[:, :]), [cstr, sq], [pss_])
                      lnv, rn, kkn = W["lnv"], W["rn"], W["kkn"]
                      V(lambda: nc.vector.tensor_scalar(lnv.t[:, :], pss_.t[:, :], 1e-24, None, ALU.max), [pss_], [lnv])
                      A(lambda: nc.scalar.activation(out=lnv.t[:, :], in_=lnv.t[:, :], func=AF.Ln), [lnv], [lnv])
                      A(lambda: nc.scalar.activation(out=rn.t[:, :], in_=lnv.t[:, :], func=AF.Exp, scale=-0.5), [lnv], [rn])
                      G(lambda: nc.gpsimd.tensor_tensor(kkn.t[:, :], kk.t[:, :], rn.t[:, :], ALU.mult), [kk, rn], [kkn])
                      fac, kp, bb = W["fac"], W["kp"], W["bb"]
                      V(lambda: nc.vector.tensor_scalar(fac.t[:, :], a.t[:, :], P(49 + j, 50 + j), der.t[:, l, 25 + j:26 + j], ALU.mult, ALU.add), [a, prm, der], [fac])
                      G(lambda: nc.gpsimd.tensor_tensor(kp.t[:, :], k.t[:, :], fac.t[:, :], ALU.mult), [k, fac], [kp])
                      G(lambda: nc.gpsimd.tensor_tensor(bb.t[:, :], kkn.t[:, :], a.t[:, :], ALU.mult), [kkn, a], [bb])
                      rk, bonus = W["rk"], W["bonus"]
                      V(lambda: nc.vector.scalar_tensor_tensor(R(rk.t[:, :]), r.t[:, :], P(57 + j, 58 + j), kp.t[:, :], ALU.mult, ALU.mult), [r, prm, kp], [rk])
                      pbn = pss()
                      MM(pbn.t[:, :], onesblkr, R(rk.t[:, :]), [cstr, rk], [pbn])
                      V(lambda: nc.vector.tensor_tensor(bonus.t[:, :], pbn.t[:, :], v.t[:, :], ALU.mult), [pbn, v], [bonus])
                      for hd in range(2):
                          lo, hi = hd * 64, hd * 64 + 64
                          V(lambda: nc.vector.tensor_tensor(R(KR.t[lo:hi, :, lo:hi]), v3(kkn.t[lo:hi, :]), v3(EGp.t[lo:hi, :]), ALU.mult), [kkn, EGp], [KR])
                          V(lambda: nc.vector.tensor_tensor(R(Ktb.t[lo:hi, :, lo:hi]), v3(kp.t[lo:hi, :]), v3(EGi.t[lo:hi, :]), ALU.mult), [kp, EGi], [Ktb])
                          V(lambda: nc.vector.tensor_tensor(R(Btb.t[lo:hi, :, lo:hi]), v3(bb.t[lo:hi, :]), v3(EGi.t[lo:hi, :]), ALU.mult), [bb, EGi], [Btb])
                          A(lambda: nc.scalar.copy(R(Vb.t[lo:hi, :, lo:hi]), v3(v.t[lo:hi, :])), [v], [Vb])
                      V(lambda: nc.vector.tensor_tensor(R(KR.t[:, :, 128:192]), v3(r.t[:, :]), v3(EG.t[:, :]), ALU.mult), [r, EG], [KR])

                      _stage(3)
                      Hb = Hbufs[j]
                      Hj = Hst.t[:, j, :]
                      Y = W["Y"]

                      def phaseA(c):
                          sl = c % NSLOT
                          tm, akb, mt = TM[sl], AKB[sl], MT[sl]
                          S = SXW[c % NCHAIN]
                          odd = (c % 2 == 1)

                          def cp(dst, src_ps, rds, wrs, prefer_act=True):
                              if prefer_act:
                                  A(lambda: nc.scalar.copy(R(dst), src_ps), rds, wrs)
                              else:
                                  V(lambda: nc.vector.tensor_copy(R(dst), src_ps), rds, wrs)

                          pT = pss()
                          for i, src in enumerate((Ktb, Btb, Vb)):
                              kb.op("pe", lambda: nc.tensor.transpose(pT.t[:, i * 128:(i + 1) * 128], src.t[:, c, :], ident),
                                    [src, cst], [pT], inc=(i == 2))
                          A(lambda: nc.scalar.copy(R(tm.t[:, :]), pT.t[:, 0:384]), [pT], [tm])
                          yield
                          pA = pss()
                          MM(pA.t[:, 0:192], R(Ktb.t[:, c, :]), R(KR.t[:, c, :]), [Ktb, KR], [pA], inc=False)
                          MM(pA.t[:, 192:384], R(Btb.t[:, c, :]), R(KR.t[:, c, :]), [Btb, KR], [pA], inc=False)
                          MM(pA.t[:, 384:512], R(KR.t[:, c, 0:128]), R(Btb.t[:, c, :]), [KR, Btb], [pA])
                          V(lambda: nc.vector.tensor_tensor(R(akb.t[:, :]), pA.t[:, :], maskAll, ALU.mult), [pA, cst], [akb])
                          yield
                          X0, Y0 = akb.t[:, 192:320], akb.t[:, 384:512]
                          s0 = S[0]
                          V(lambda: nc.vector.tensor_tensor(R(s0.t[:, 128:256]), ident, X0, ALU.add), [cst, akb], [s0])
                          pX = pss()
                          MM(pX.t[:, 0:128], R(Y0), R(X0), [akb], [pX])
                          cp(s0.t[:, 0:128], pX.t[:, 0:128], [pX], [s0])
                          yield
                          pY = pss()
                          MM(pY.t[:, 0:128], R(X0), R(Y0), [akb], [pY])
                          cp(s0.t[:, 256:384], pY.t[:, 0:128], [pY], [s0], prefer_act=not odd)
                          yield
                          cur = 0
                          for lev in range(1, 5):
                              sc, sn = S[cur], S[1 - cur]
                              pXW = pss()
                              MM(pXW.t[:, 0:256], R(sc.t[:, 256:384]), R(sc.t[:, 0:256]), [sc], [pXW])
                              if lev < 4:
                                  cp(sn.t[:, 0:128], pXW.t[:, 0:128], [pXW], [sn])
                              V(lambda: nc.vector.tensor_tensor(R(sn.t[:, 128:256]), pXW.t[:, 128:256], sc.t[:, 128:256], ALU.add), [pXW, sc], [sn])
                              yield
                              pY = pss()
                              MM(pY.t[:, 0:128], R(sc.t[:, 0:128]), R(sc.t[:, 256:384]), [sc], [pY])
                              cp(sn.t[:, 256:384], pY.t[:, 0:128], [pY], [sn], prefer_act=not odd)
                              yield
                              cur = 1 - cur
                          sc = S[cur]
                          pW = pss()
                          MM(pW.t[:, 0:128], R(sc.t[:, 256:384]), R(sc.t[:, 128:256]), [sc], [pW])
                          V(lambda: nc.vector.tensor_tensor(R(mt.t[:, :]), pW.t[:, 0:128], sc.t[:, 128:256], ALU.add), [pW, sc], [mt])
                          yield

                      def seqc(c):
                          sl = c % NSLOT
                          tm, akb, mt = TM[sl], AKB[sl], MT[sl]
                          pR = pss()
                          MM(pR.t[:, 0:128], R(KR.t[:, c, 0:128]), R(Hj), [KR, Hb], [pR], start=True, stop=False, inc=False)
                          MM(pR.t[:, 0:128], R(akb.t[:, 0:128]), R(tm.t[:, 256:384]), [akb, tm], [pR], start=False, stop=True)
                          A(lambda: nc.scalar.copy(R(RH.t[:, :]), pR.t[:, 0:128]), [pR], [RH])
                          yield
                          pU = pss()
                          MM(pU.t[:, 0:128], R(mt.t[:, :]), R(RH.t[:, :]), [mt, RH], [pU])
                          V(lambda: nc.vector.tensor_scalar(R(nU.t[:, :]), pU.t[:, 0:128], -1.0, None, ALU.mult), [pU], [nU])
                          yield
                          pH = pss()
                          MM(pH.t[:, 0:128], identr, R(Hj), [cstr, Hb], [pH], start=True, stop=False, inc=False)
                          MM(pH.t[:, 0:128], R(tm.t[:, 0:128]), R(tm.t[:, 256:384]), [tm], [pH], start=False, stop=False, inc=False)
                          MM(pH.t[:, 0:128], R(tm.t[:, 128:256]), R(nU.t[:, :]), [tm, nU], [pH], start=False, stop=True, inc=False)
                          MM(pH.t[:, 128:192], R(Hj), R(KR.t[:, c, 128:192]), [Hb, KR], [pH], start=True, stop=False, inc=False)
                          MM(pH.t[:, 128:192], R(tm.t[:, 256:384]), R(akb.t[:, 128:192]), [tm, akb], [pH], start=False, stop=False, inc=False)
                          MM(pH.t[:, 128:192], R(nU.t[:, :]), R(akb.t[:, 320:384]), [nU, akb], [pH], start=False, stop=True)
                          A(lambda: nc.scalar.activation(out=R(Hj), in_=pH.t[:, 0:128], func=AF.Identity, scale=EG.t[:, c * 64 + 63:c * 64 + 64]), [pH, EG], [Hb])
                          A(lambda: nc.scalar.copy(R(Y.t[:, c * 64:(c + 1) * 64]), pH.t[:, 128:192]), [pH], [Y])
                          yield

                      if j < 7:
                          filler = g_inproj([j + 1, 9 + j, 17 + j, 26 + j], BIG)
                      else:
                          filler = g_inproj([33, 41, 49, 57], BIG)
                      def g_mix_next(jn):
                          for bank, cc, dst in ((BIG[0], jn, r), (BIG[1], 8 + jn, k), (BIG[2], 16 + jn, v)):
                              mix_shift(bank, cc, dst)
                              yield

                      filler2 = g_mix_next(j + 1) if j < 7 else None
                      active = []
                      nextA, doneA, doneSeq, seq_run = 0, set(), -1, False
                      it = 0
                      while doneSeq < NCH - 1:
                          while nextA < NCH and sum(1 for a_ in active if a_[0] == "A") < NCHAIN and nextA <= doneSeq + NSLOT:
                              active.append(("A", nextA, phaseA(nextA)))
                              nextA += 1
                          if not seq_run and (doneSeq + 1) in doneA:
                              active.append(("S", doneSeq + 1, seqc(doneSeq + 1)))
                              seq_run = True
                          for item in list(active):
                              try:
                                  next(item[2])
                              except StopIteration:
                                  active.remove(item)
                                  if item[0] == "A":
                                      doneA.add(item[1])
                                  else:
                                      doneSeq = item[1]
                                      seq_run = False
                          it += 1
                          if filler is not None:
                              try:
                                  next(filler)
                              except StopIteration:
                                  filler = None
                          elif filler is None and filler2 is not None and it % 2 == 0:
                              try:
                                  next(filler2)
                              except StopIteration:
                                  filler2 = None
                      if filler is not None:
                          for _ in filler:
                              pass
                      if filler2 is not None:
                          for _ in filler2:
                              pass
                      _stage(4)
                      pm_ = pss()
                      MM(pm_.t[:, :], meanblkr, R(Y.t[:, :]), [cstr, Y], [pm_])
                      yc = W["yc"]
                      V(lambda: nc.vector.tensor_tensor(yc.t[:, :], Y.t[:, :], pm_.t[:, :], ALU.subtract), [Y, pm_], [yc])
                      A(lambda: nc.scalar.activation(out=R(sq.t[:, :]), in_=yc.t[:, :], func=AF.Square), [yc], [sq])
                      pv_ = pss()
                      MM(pv_.t[:, :], meanblkr, R(sq.t[:, :]), [cstr, sq], [pv_])
                      V(lambda: nc.vector.tensor_scalar(lnv.t[:, :], pv_.t[:, :], GN_EPS, None, ALU.add), [pv_], [lnv])
                      A(lambda: nc.scalar.activation(out=lnv.t[:, :], in_=lnv.t[:, :], func=AF.Ln), [lnv], [lnv])
                      A(lambda: nc.scalar.activation(out=rn.t[:, :], in_=lnv.t[:, :], func=AF.Exp, scale=-0.5), [lnv], [rn])
                      t2, t3 = W["t2"], W["t3"]
                      G(lambda: nc.gpsimd.tensor_tensor(t2.t[:, :], yc.t[:, :], rn.t[:, :], ALU.mult), [yc, rn], [t2])
                      G(lambda: nc.gpsimd.tensor_scalar(t3.t[:, :], t2.t[:, :], P(65 + j, 66 + j), P(73 + j, 74 + j), ALU.mult, ALU.add), [t2, prm], [t3])
                      G(lambda: nc.gpsimd.tensor_tensor(t2.t[:, :], t3.t[:, :], bonus.t[:, :], ALU.add), [t3, bonus], [t2])
                      V(lambda: nc.vector.tensor_tensor(ya_in.t[:, j, :], t2.t[:, :], sz.t[:, :], ALU.mult), [t2, sz], [yas[j]])

                  _stage(5)
                  unit = [0]

                  def bankset():
                      s_ = PS[0:4] if unit[0] % 2 == 0 else PS[4:8]
                      unit[0] += 1
                      return s_

                  for j in range(8):
                      bs = bankset()
                      if j > 0:
                          inproj_now([33 + j, 41 + j, 49 + j, 57 + j], bs)
                      pbg, pcg, phb, pzb = bs
                      t1, t2, t3 = W["t1"], W["t2"], W["t3"]
                      A(lambda: nc.scalar.copy(t1.t[:, :], pcg.t[:, :]), [pcg], [t1])
                      V(lambda: nc.vector.tensor_tensor(uext.t[:, 2:TT + 2], phb.t[:, :], t1.t[:, :], ALU.mult), [phb, t1], [uext])
                      V(lambda: nc.vector.tensor_copy(uext.t[:, 0:2], cbnd.t[:, j, :]), [cbnd], [uext])
                      G(lambda: nc.gpsimd.tensor_scalar(t2.t[:, :], uext.t[:, 0:TT], P(81 + j, 82 + j), 0.0, ALU.mult, ALU.add), [uext, prm], [t2])
                      V(lambda: nc.vector.scalar_tensor_tensor(t3.t[:, :], uext.t[:, 1:TT + 1], P(89 + j, 90 + j), t2.t[:, :], ALU.mult, ALU.add), [uext, prm, t2], [t3])
                      V(lambda: nc.vector.scalar_tensor_tensor(t2.t[:, :], uext.t[:, 2:TT + 2], P(97 + j, 98 + j), t3.t[:, :], ALU.mult, ALU.add), [uext, prm, t3], [t2])
                      V(lambda: nc.vector.tensor_copy(cbnd.t[:, j, :], uext.t[:, TT:TT + 2]), [uext], [cbnd])
                      V(lambda: nc.vector.tensor_tensor(t3.t[:, :], pbg.t[:, :], t2.t[:, :], ALU.mult), [pbg, t2], [t3])
                      A(lambda: nc.scalar.activation(out=t1.t[:, :], in_=pzb.t[:, :], func=AF.Sigmoid), [pzb], [t1])
                      V(lambda: nc.vector.tensor_tensor(t2.t[:, :], pzb.t[:, :], t1.t[:, :], ALU.mult), [pzb, t1], [t2])
                      G(lambda: nc.gpsimd.tensor_tensor(yb_in.t[:, j, :], t3.t[:, :], t2.t[:, :], ALU.mult), [t3, t2], [ybs[j]])

                  _stage(6)
                  for m in range(16):
                      pab, pbb = pbufs[(2 * m) % 3], pbufs[(2 * m + 1) % 3]
                      kb.dma("sp", ("pp", (2 * m) % 3), pab.t[:, :, :], pa_b[l, m].rearrange("p (a b) -> p a b", b=128), reads=[conv_ev[l]["pa"]], writes=[pab])
                      kb.dma("sp", ("pp", (2 * m + 1) % 3), pbb.t[:, :, :], pb_b[l, m].rearrange("p (a b) -> p a b", b=128), reads=[conv_ev[l]["pb"]], writes=[pbb])
                      pya, pyb, pga, pgb = bankset()
                      for kc in range(8):
                          MM(pya.t[:, :], pab.t[:, kc, :], ya_in.t[:, kc, :], [pab, yas[kc]], [pya], start=(kc == 0), stop=(kc == 7), inc=(kc == 7))
                      for kc in range(8):
                          MM(pyb.t[:, :], pbb.t[:, kc, :], yb_in.t[:, kc, :], [pbb, ybs[kc]], [pyb], start=(kc == 0), stop=(kc == 7), inc=(kc == 7))
                      inproj_now([65 + m, 81 + m], [pga, pgb])
                      ta, tb = exts[0], W["t1"]
                      A(lambda: nc.scalar.activation(out=ta.t[:, 0:TT], in_=pga.t[:, :], func=AF.Sigmoid), [pga], [ta])
                      V(lambda: nc.vector.tensor_tensor(ta.t[:, 0:TT], pya.t[:, :], ta.t[:, 0:TT], ALU.mult), [pya, ta], [ta])
                      A(lambda: nc.scalar.activation(out=tb.t[:, 0:TT], in_=pgb.t[:, :], func=AF.Sigmoid), [pgb], [tb])
                      V(lambda: nc.vector.tensor_tensor(tb.t[:, 0:TT], pyb.t[:, :], tb.t[:, 0:TT], ALU.mult), [pyb, tb], [tb])
                      G(lambda: nc.gpsimd.tensor_tensor(mix.t[:, m, :], ta.t[:, 0:TT], tb.t[:, 0:TT], ALU.add), [ta, tb], [mixs[m]])

                  _stage(7)
                  for half in range(2):
                      posb = [PS[0:4], PS[4:8]]
                      for g in range(8):
                          wo = wobufs[g % 2]
                          kb.dma("sp", ("wo", g % 2), wo.t[:, :, :], wo_b[l, g].rearrange("p (a b) -> p a b", b=256), reads=[conv_ev[l]["wo"]], writes=[wo])
                          c0_ = (g % 2) * 256
                          for sl_ in range(2):
                              s = half * 2 + sl_
                              po = posb[sl_][g // 2]
                              for dc in range(DC):
                                  MM(po.t[:, c0_:c0_ + 256], mix.t[:, dc, s * 128:(s + 1) * 128], wo.t[:, dc, :], [mixs[dc], wo], [po], start=(dc == 0), stop=(dc == DC - 1), inc=(dc == DC - 1))
                      for sl_ in range(2):
                          s = half * 2 + sl_
                          pos = posb[sl_]
                          xr = xo[1]
                          kb.dma("sp", ("x", 1), xr.t[:, :], x_src[t0 + s * 128:t0 + (s + 1) * 128, :], reads=[xsrc_buf], writes=[xr])
                          for q in range(4):
                              A(lambda: nc.scalar.activation(out=ya_in.t[:, 0, :], in_=pos[q].t[:, :], func=AF.Square, accum_out=ss4.t[:, 4 + q:5 + q]), [pos[q]], [yas[0], ss4])
                          V(lambda: nc.vector.tensor_reduce(rs4.t[:, 4:5], ss4.t[:, 4:8], mybir.AxisListType.X, ALU.add), [ss4], [rs4])
                          V(lambda: nc.vector.tensor_scalar(rs4.t[:, 4:5], rs4.t[:, 4:5], 1.0 / D, RMS_EPS, ALU.mult, ALU.add), [rs4], [rs4])
                          A(lambda: nc.scalar.activation(out=rs4.t[:, 4:5], in_=rs4.t[:, 4:5], func=AF.Ln), [rs4], [rs4])
                          A(lambda: nc.scalar.activation(out=rs4.t[:, 4:5], in_=rs4.t[:, 4:5], func=AF.Exp, scale=-0.5), [rs4], [rs4])
                          xn = xo[0]
                          for q in range(4):
                              V(lambda: nc.vector.scalar_tensor_tensor(xn.t[:, q * 512:(q + 1) * 512], pos[q].t[:, :], rs4.t[:, 4:5], GP.t[:, q * 512:(q + 1) * 512], ALU.mult, ALU.mult),
                                [pos[q], rs4, GP], [xn])
                          G(lambda: nc.gpsimd.tensor_tensor(xn.t[:, :], xn.t[:, :], xr.t[:, :], ALU.add), [xn, xr], [xn])
                          ob = Buf("orow")
                          kb.dma("pool", "st", out_d[t0 + s * 128:t0 + (s + 1) * 128, :], xn.t[:, :], reads=[xn], writes=[ob])
                          if cgen is not None:
                              for _ in range(6):
                                  if next(cgen, "end") == "end":
                                      cgen = None
                                      break
              if cgen is not None:
                  for _ in cgen:
                      pass
              kb.barrier()
          except _Stop:
            break
        kb.barrier()
        print("kernel built: ops=%d waits=%d" % (kb.nops, kb.nwait))
    return nc


def _consts():
    cst = np.zeros((128, NCST), np.float32)
    cst[:, 0:128] = np.eye(128, dtype=np.float32)
    blk = np.zeros((128, 128), np.float32)
    blk[0:64, 0:64] = 1.0
    blk[64:128, 64:128] = 1.0
    cst[:, 128:256] = blk
    cst[:, 256:384] = blk / 64.0
    s = np.arange(64)[:, None]
    t = np.arange(64)[None, :]
    su = (s < t).astype(np.float32)
    iu = (s <= t).astype(np.float32)
    mA = np.zeros((128, 192), np.float32)
    mA[0:64, 0:64] = su
    mA[64:128, 64:128] = su
    mA[0:64, 128:192] = iu
    mA[64:128, 128:192] = iu
    mQ = np.zeros((128, 128), np.float32)
    mQ[0:64, 0:64] = su.T
    mQ[64:128, 64:128] = su.T
    mB = mA.copy()
    mB[:, 0:128] *= -1.0
    cst[:, 384:576] = mA
    cst[:, 576:768] = mB
    cst[:, 768:896] = -mQ
    rm = np.ones((128, TT), np.float32)
    rm[:, ::CH] = 0.0
    cst[:, 896:1408] = rm
    cst[:, 1408:1536] = 1.0
    return cst


def prep_shared(inp, L):
    f = lambda a: np.ascontiguousarray(a, dtype=np.float32)
    sh = {}
    def relay(w, colblk):
        Lh, K, N = w.shape
        return f(w.reshape(Lh, K // 128, 128, N // colblk, colblk).transpose(0, 3, 2, 1, 4).reshape(Lh, N // colblk, 128, (K // 128) * colblk))
    sh["ada_r"] = relay(inp["ada_w"][:L], 128)
    sh["win_r"] = relay(inp["w_in"][:L], 128)
    sh["pa_r"] = relay(inp["p_a"][:L], 128)
    sh["pb_r"] = relay(inp["p_b"][:L], 128)
    sh["wo_r"] = relay(inp["w_out"][:L], 256)
    fm = lambda v, n: v.reshape(L, n, 128).transpose(0, 2, 1)
    prm = np.zeros((L, 128, NPRM), np.float32)
    prm[:, :, 0:25] = fm(inp["mu_shift"][:L], 25)
    prm[:, :, 25:33] = fm(inp["w0"][:L], 8)
    prm[:, :, 33:41] = fm(inp["a0"][:L], 8)
    prm[:, :, 41:49] = fm(inp["k_k"][:L], 8)
    prm[:, :, 49:57] = fm(inp["k_a"][:L], 8)
    prm[:, :, 57:65] = fm(inp["r_k"][:L].reshape(L, 1024), 8)
    prm[:, :, 65:73] = fm(inp["lnx_gain"][:L], 8)
    prm[:, :, 73:81] = fm(inp["lnx_bias"][:L], 8)
    cw = inp["conv_w"][:L]
    prm[:, :, 81:89] = fm(cw[:, 0], 8)
    prm[:, :, 89:97] = fm(cw[:, 1], 8)
    prm[:, :, 97:105] = fm(cw[:, 2], 8)
    prm[:, :, 105:121] = fm(inp["pre_gain"][:L], 16)
    prm[:, :, 121:137] = fm(inp["post_gain"][:L], 16)
    prm[:, :, 137:185] = fm(inp["ada_b"][:L], 48)
    sh["prm"] = f(prm.transpose(1, 0, 2))
    lora = np.zeros((L, 128, 8, 256), np.float32)
    w2 = inp["w2"][:L].reshape(L, 64, 8, 128)
    a2 = inp["a2"][:L].reshape(L, 64, 8, 128)
    lora[:, 0:64, :, 0:128] = w2
    lora[:, 64:128, :, 128:256] = a2
    sh["lora"] = f(lora.reshape(L, 128, 8 * 256))
    sh["cst"] = _consts()
    return sh


_CACHE = {}


def run(inp, T, L, nb):
    key = (T, L)
    if key not in _CACHE:
        _CACHE[key] = build_program(T, L)
    nc = _CACHE[key]
    sh = prep_shared(inp, L)
    in_maps = []
    zero = None
    slots = [0, 2, 4, 6, 1, 3, 5, 7]
    owner = {}
    for i in range(min(nb, 8)):
        owner[slots[i]] = i
    for core in range(8):
        if core in owner:
            b = owner[core]
            m = dict(sh)
            m["x"] = np.ascontiguousarray(inp["x"][b, :T], dtype=np.float32)
            m["cfm"] = np.ascontiguousarray(inp["c"][b].reshape(DC, 128).T, dtype=np.float32)
        else:
            if zero is None:
                zero = {k: np.zeros_like(v) for k, v in sh.items()}
                zero["x"] = np.zeros((T, D), np.float32)
                zero["cfm"] = np.zeros((128, DC), np.float32)
            m = zero
        in_maps.append(m)
    res = run_bass_kernel_spmd(nc, in_maps, core_ids=list(range(8)))
    inv = {b: c for c, b in owner.items()}
    return np.stack([res.results[inv[b]]["out"] for b in range(nb)], axis=0)


def kernel(**inputs):
    inp = {k: np.asarray(v) for k, v in inputs.items()}
    B, T, _ = inp["x"].shape
    L = inp["w_in"].shape[0]
    out = run(inp, T, L, B)
    return out.astype(np.float32)
```
